# Optimizing a Trainium2 kernel written in Bass

```python
import math
import jax, jax.numpy as jnp
from jax import lax
import numpy as np

D_MODEL = 1024
BATCH = 16
SEQ = 4096
DEPTH = 1
DEC_BATCH = 1
DEC_SEQ = 16384
PAST_LEN = 128

HEAD_DIM = 128
ATTN_HEADS = 8
ATTN_KV_HEADS = 2
ATTN_GROUP = ATTN_HEADS // ATTN_KV_HEADS
ATTN_Q_WIDTH = ATTN_HEADS * HEAD_DIM
ATTN_KV_WIDTH = ATTN_KV_HEADS * HEAD_DIM
Q_BLOCK = 128
ROPE_THETA = 10000.0
GRID_W = 64

DN_HEADS = 8
DN_HEAD_K = 128
DN_HEAD_V = 128
DN_QK_WIDTH = DN_HEADS * DN_HEAD_K
DN_V_WIDTH = DN_HEADS * DN_HEAD_V
DN_CONV_DIM = 2 * DN_QK_WIDTH + DN_V_WIDTH
CONV_K = 5
CHUNK = 64

D_FF = -(-8 * D_MODEL // (3 * 256)) * 256
N_MOD = 6

SPLITS = (ATTN_Q_WIDTH, ATTN_KV_WIDTH, ATTN_KV_WIDTH,
          DN_QK_WIDTH, DN_QK_WIDTH, DN_V_WIDTH,
          DN_V_WIDTH, 2 * DN_HEADS, 2 * DN_HEADS, 2 * D_MODEL)
N_IN = ATTN_Q_WIDTH + 2 * ATTN_KV_WIDTH + 2 * DN_QK_WIDTH + 2 * DN_V_WIDTH + 4 * DN_HEADS + 2 * D_MODEL
EPS = 1e-6

kernel_name = "hybrid_gqa_axialrope_bigdeltanet_adaln_encoder"


def _rmsnorm(x, w):
    xf = x.astype(jnp.float32)
    y = xf * lax.rsqrt(jnp.mean(xf * xf, axis=-1, keepdims=True) + EPS)
    return (y * w.astype(jnp.float32)).astype(x.dtype)


def _l2norm(x):
    return x * lax.rsqrt(jnp.sum(x * x, axis=-1, keepdims=True) + EPS)


def _axial_rope_tables(seq_len):
    rows = seq_len // GRID_W
    row = jnp.repeat(jnp.arange(rows, dtype=jnp.float32), GRID_W)
    col = jnp.tile(jnp.arange(GRID_W, dtype=jnp.float32), rows)
    half = HEAD_DIM // 2
    inv = 1.0 / (ROPE_THETA ** (jnp.arange(0, half, 2, dtype=jnp.float32) / half))
    ar = row[:, None] * inv
    ac = col[:, None] * inv
    ang = jnp.concatenate([ar, ar, ac, ac], axis=-1)
    return jnp.cos(ang), jnp.sin(ang)


def _apply_rope(x, cos, sin):
    xr = x.reshape(x.shape[:-1] + (2, 2, HEAD_DIM // 4))
    rot = jnp.stack([-xr[..., 1, :], xr[..., 0, :]], axis=-2).reshape(x.shape)
    return x * cos + rot * sin


def _block_attention(q, k, v):
    B, S = q.shape[0], q.shape[1]
    nb = S // Q_BLOCK
    scale = HEAD_DIM ** -0.5
    qb = q.reshape(B, nb, Q_BLOCK, ATTN_KV_HEADS, ATTN_GROUP, HEAD_DIM).transpose(1, 0, 2, 3, 4, 5)

    def one_block(qblk):
        s = jnp.einsum('bqhgd,bkhd->bhgqk', qblk, k).astype(jnp.float32) * scale
        p = jax.nn.softmax(s, axis=-1).astype(v.dtype)
        return jnp.einsum('bhgqk,bkhd->bqhgd', p, v)

    o = lax.map(one_block, qb)
    return o.transpose(1, 0, 2, 3, 4, 5).reshape(B, S, ATTN_Q_WIDTH)


def _centred_conv(x, w):
    K = w.shape[0]
    pad = K // 2
    S = x.shape[1]
    xp = jnp.pad(x, ((0, 0), (pad, pad), (0, 0)))
    out = xp[:, 0:S] * w[0]
    for j in range(1, K):
        out = out + xp[:, j:j + S] * w[j]
    return out


def _gated_delta_chunked(q, k, v, g, beta):
    B, T, H, dk = q.shape
    dv = v.shape[-1]
    N = T // CHUNK

    def to_chunks(a):
        return a.reshape((B, N, CHUNK) + a.shape[2:]).swapaxes(2, 3)

    qc, kc, vc, gc, bc = (to_chunks(a) for a in (q, k, v, g, beta))
    gc = jnp.cumsum(gc, axis=-1)
    idx = jnp.arange(CHUNK)
    tril = idx[:, None] >= idx[None, :]
    strict = idx[:, None] > idx[None, :]
    decay = jnp.exp(jnp.where(tril, gc[..., :, None] - gc[..., None, :], -jnp.inf))
    kb = kc * bc[..., None]
    a_kk = jnp.einsum('bnhid,bnhjd->bnhij', kb, kc) * decay
    amat = jnp.eye(CHUNK, dtype=jnp.float32) + jnp.where(strict, a_kk, 0.0)
    rhs = jnp.concatenate([vc * bc[..., None], kb * jnp.exp(gc)[..., None]], axis=-1)
    sol = lax.linalg.triangular_solve(amat, rhs, left_side=True, lower=True)
    u = sol[..., :dv]
    w = sol[..., dv:]
    a_qk = jnp.einsum('bnhid,bnhjd->bnhij', qc, kc) * decay

    def step(state, inp):
        q_i, k_i, u_i, w_i, g_i, a_i = inp
        v_new = u_i - jnp.einsum('bhcd,bhde->bhce', w_i, state)
        o = (jnp.einsum('bhcd,bhde->bhce', q_i * jnp.exp(g_i)[..., None], state)
             + jnp.einsum('bhij,bhje->bhie', a_i, v_new))
        g_last = g_i[..., -1]
        k_dec = k_i * jnp.exp(g_last[..., None] - g_i)[..., None]
        state = state * jnp.exp(g_last)[..., None, None] + jnp.einsum('bhcd,bhce->bhde', k_dec, v_new)
        return state, o

    xs = tuple(a.swapaxes(0, 1) for a in (qc, kc, u, w, gc, a_qk))
    s0 = jnp.zeros((B, H, dk, dv), jnp.float32)
    _, o = lax.scan(step, s0, xs)
    return o.transpose(1, 0, 3, 2, 4).reshape(B, T, H, dv)


def _layer(x, c, norm1_w, norm2_w, w_ada, b_ada, w_in, q_norm_w, k_norm_w, conv_w,
           a_log, dt_bias, dn_norm_w, w_proj_a, w_proj_b, w_out, w_ffn_in, w_ffn_out):
    B, S, _ = x.shape
    mod = jnp.einsum('bd,de->be', jax.nn.silu(c), w_ada) + b_ada
    sh1, sc1, gt1, sh2, sc2, gt2 = jnp.split(mod[:, None, :], N_MOD, axis=-1)

    h = _rmsnorm(x, norm1_w) * (1 + sc1) + sh1
    proj = jnp.einsum('bsd,de->bse', h, w_in)
    cuts = np.cumsum(SPLITS)[:-1].tolist()
    aq, ak, av, dq, dk_, dv_, z, beta_logit, alpha_logit, gate_logit = jnp.split(proj, cuts, axis=-1)

    cos, sin = _axial_rope_tables(S)
    cos = cos.astype(x.dtype)
    sin = sin.astype(x.dtype)
    q = _rmsnorm(aq.reshape(B, S, ATTN_KV_HEADS, ATTN_GROUP, HEAD_DIM), q_norm_w)
    k = _rmsnorm(ak.reshape(B, S, ATTN_KV_HEADS, HEAD_DIM), k_norm_w)
    q = _apply_rope(q, cos[None, :, None, None, :], sin[None, :, None, None, :])
    k = _apply_rope(k, cos[None, :, None, :], sin[None, :, None, :])
    va = av.reshape(B, S, ATTN_KV_HEADS, HEAD_DIM)
    y_a = jnp.einsum('bse,ed->bsd', _block_attention(q, k, va), w_proj_a)

    qkv = jax.nn.silu(_centred_conv(jnp.concatenate([dq, dk_, dv_], axis=-1), conv_w))
    dq, dk_, dv_ = jnp.split(qkv, [DN_QK_WIDTH, 2 * DN_QK_WIDTH], axis=-1)
    qd = _l2norm(dq.reshape(B, S, DN_HEADS, DN_HEAD_K).astype(jnp.float32)) * (DN_HEAD_K ** -0.5)
    kd = _l2norm(dk_.reshape(B, S, DN_HEADS, DN_HEAD_K).astype(jnp.float32))
    vd = dv_.reshape(B, S, DN_HEADS, DN_HEAD_V).astype(jnp.float32)
    beta = jax.nn.sigmoid(beta_logit.astype(jnp.float32)).reshape(B, S, 2, DN_HEADS)
    g = -jnp.exp(a_log.astype(jnp.float32)) * jax.nn.softplus(
        alpha_logit.astype(jnp.float32).reshape(B, S, 2, DN_HEADS) + dt_bias.astype(jnp.float32))
    o_f = _gated_delta_chunked(qd, kd, vd, g[:, :, 0], beta[:, :, 0])
    flip = lambda a: a[:, ::-1]
    o_b = flip(_gated_delta_chunked(flip(qd), flip(kd), flip(vd), flip(g[:, :, 1]), flip(beta[:, :, 1])))
    o = _rmsnorm(o_f + o_b, dn_norm_w) * jax.nn.silu(z.reshape(B, S, DN_HEADS, DN_HEAD_V).astype(jnp.float32))
    y_b = jnp.einsum('bse,ed->bsd', o.reshape(B, S, DN_V_WIDTH).astype(x.dtype), w_proj_b)

    ga, gb = jnp.split(jax.nn.sigmoid(gate_logit), 2, axis=-1)
    mixed = jnp.einsum('bsd,de->bse', ga * y_a + gb * y_b, w_out)
    x = x + gt1 * mixed

    h2 = _rmsnorm(x, norm2_w) * (1 + sc2) + sh2
    u, vf = jnp.split(jnp.einsum('bsd,df->bsf', h2, w_ffn_in), 2, axis=-1)
    x = x + gt2 * jnp.einsum('bsf,fd->bsd', jax.nn.silu(u) * vf, w_ffn_out)
    return x


def setup_inputs(seed: int = 0) -> dict:
    key = jax.random.key(seed)
    ks = jax.random.split(key, 24)
    f32 = jnp.float32
    nrm = lambda k, shape, s: jax.random.normal(k, shape, f32) * s
    D = D_MODEL
    dt = jnp.exp(jax.random.uniform(ks[12], (DEPTH, 2, DN_HEADS), f32, math.log(1e-3), math.log(1e-1)))
    return {
        "x_prompt": nrm(ks[0], (BATCH, SEQ, D), 1.0),
        "x_sample": nrm(ks[1], (DEC_BATCH, DEC_SEQ, D), 1.0),
        "c_prompt": nrm(ks[2], (BATCH, D), 1.0),
        "c_sample": nrm(ks[3], (DEC_BATCH, D), 1.0),
        "norm1_w": 1.0 + nrm(ks[4], (DEPTH, D), 0.02),
        "norm2_w": 1.0 + nrm(ks[5], (DEPTH, D), 0.02),
        "w_ada": nrm(ks[6], (DEPTH, D, N_MOD * D), 0.3 * D ** -0.5),
        "b_ada": nrm(ks[7], (DEPTH, N_MOD * D), 0.02),
        "w_in": nrm(ks[8], (DEPTH, D, N_IN), D ** -0.5),
        "q_norm_w": 1.0 + nrm(ks[9], (DEPTH, HEAD_DIM), 0.02),
        "k_norm_w": 1.0 + nrm(ks[10], (DEPTH, HEAD_DIM), 0.02),
        "conv_w": nrm(ks[11], (DEPTH, CONV_K, DN_CONV_DIM), CONV_K ** -0.5),
        "a_log": jnp.log(jax.random.uniform(ks[13], (DEPTH, 2, DN_HEADS), f32, 1.0, 16.0)),
        "dt_bias": dt + jnp.log(-jnp.expm1(-dt)),
        "dn_norm_w": 1.0 + nrm(ks[14], (DEPTH, DN_HEAD_V), 0.02),
        "w_proj_a": nrm(ks[15], (DEPTH, ATTN_Q_WIDTH, D), ATTN_Q_WIDTH ** -0.5),
        "w_proj_b": nrm(ks[16], (DEPTH, DN_V_WIDTH, D), DN_V_WIDTH ** -0.5),
        "w_out": nrm(ks[17], (DEPTH, D, D), D ** -0.5),
        "w_ffn_in": nrm(ks[18], (DEPTH, D, 2 * D_FF), D ** -0.5),
        "w_ffn_out": nrm(ks[19], (DEPTH, D_FF, D), D_FF ** -0.5),
        "final_norm_w": 1.0 + nrm(ks[20], (D,), 0.02),
    }


def reference(x_prompt, x_sample, c_prompt, c_sample, norm1_w, norm2_w, w_ada, b_ada, w_in,
              q_norm_w, k_norm_w, conv_w, a_log, dt_bias, dn_norm_w, w_proj_a, w_proj_b,
              w_out, w_ffn_in, w_ffn_out, final_norm_w):
    def trunk(x, c):
        for l in range(DEPTH):
            x = _layer(x, c, norm1_w[l], norm2_w[l], w_ada[l], b_ada[l], w_in[l], q_norm_w[l],
                       k_norm_w[l], conv_w[l], a_log[l], dt_bias[l], dn_norm_w[l], w_proj_a[l],
                       w_proj_b[l], w_out[l], w_ffn_in[l], w_ffn_out[l])
        return _rmsnorm(x, final_norm_w)

    y_prompt = trunk(x_prompt, c_prompt)
    y_sample = trunk(x_sample, c_sample)
    return (y_prompt, y_sample)
```

```python
import contextlib
import os
import numpy as np
import concourse.bass as bass
import concourse.mybir as mybir
from concourse.bass_utils import run_bass_kernel_spmd

F32 = mybir.dt.float32
BF16 = mybir.dt.bfloat16
AF = mybir.ActivationFunctionType
ALU = mybir.AluOpType

D = 1024
NKC = 8
N_IN = 7712
DFF = 2816
EPS = 1e-6
NCORES = 8
EPOCH = 30000
ATTACH_WAITS = os.environ.get('ATTACH_WAITS', '0') == '1'


class Sched:
    def __init__(self, nc, es, n_dma_sems=32, needed=None):
        self.nc = nc
        self.es = es
        self.eng = {"pe": nc.tensor, "act": nc.scalar, "dve": nc.vector, "pool": nc.gpsimd, "sp": nc.sync}
        self.sems = {}
        self.idx = {e: 0 for e in ["pe", "act", "dve", "pool"]}
        self.cnt = {e: 0 for e in self.idx}
        self.epoch = {e: 0 for e in self.idx}
        self.map = {e: {} for e in self.idx}
        self.needed_in = needed
        self.needed = {e: set() for e in self.idx}
        self.dsem = [es.enter_context(nc.semaphore(f"dq{i}")) for i in range(n_dma_sems)]
        self.dcnt = [0] * n_dma_sems
        self.dnext = 0
        self.waited = {}
        self.lastw = {}
        self.rd = {}
        self.ninstr = 0
        self.pending = None

    def _sem(self, key):
        if key[0] == "d":
            return self.dsem[key[1]]
        if key not in self.sems:
            self.sems[key] = self.es.enter_context(self.nc.semaphore(f"s_{key[0]}_{key[1]}"))
        return self.sems[key]

    def _resolve(self, key, val):
        if key[0] == "d":
            return self.dsem[key[1]], val
        f = key[1]
        self.needed[f].add(val)
        if self.needed_in is not None:
            assert val in self.needed_in[f], "two-pass mismatch"
        ep, c = self.map[f][val]
        return self._sem((f, ep)), c

    def _emit_wait(self, e, key, val):
        if self.pending is not None:
            self.pending.append((key, val))
            return
        sem, v = self._resolve(key, val)
        self.eng[e].wait_ge(sem, v)
        self.ninstr += 1

    def _wait(self, e, ev):
        if ev is None:
            return
        key, val = ev
        if e == "pe" and key == ("c", "pe"):
            return
        k = (e, key)
        if self.waited.get(k, 0) >= val:
            return
        self.waited[k] = val
        self._emit_wait(e, key, val)

    def _deps(self, e, reads, writes):
        for r in reads:
            self._wait(e, self.lastw.get(r))
            if isinstance(r, tuple) and r[0] == "ps":
                for key, val in self.rd.get(r, {}).items():
                    if key != ("c", e):
                        self._wait(e, (key, val))
        for w in writes:
            self._wait(e, self.lastw.get(w))
            for key, val in self.rd.get(w, {}).items():
                self._wait(e, (key, val))

    def _record(self, ev, reads, writes):
        key, val = ev
        for r in reads:
            d = self.rd.setdefault(r, {})
            if d.get(key, 0) < val:
                d[key] = val
        for w in writes:
            self.lastw[w] = ev
            self.rd[w] = {}

    def _with_waits(self, e, reads, writes, fn, pre=None):
        self.pending = [] if ATTACH_WAITS else None
        if pre is not None:
            self._wait(e, pre)
        self._deps(e, reads, writes)
        pend, self.pending = self.pending, None
        if pend:
            for key, val in pend[:-1]:
                self._emit_wait(e, key, val)
        ins = fn(self.eng[e])
        if pend:
            sem, v = self._resolve(*pend[-1])
            ins._wait_ge(sem, v)
        return ins

    def op(self, e, reads, writes, fn):
        ins = self._with_waits(e, reads, writes, fn)
        self.idx[e] += 1
        i = self.idx[e]
        if self.needed_in is None or i in self.needed_in[e]:
            if self.cnt[e] >= EPOCH:
                self.epoch[e] += 1
                self.cnt[e] = 0
            self.cnt[e] += 1
            ins.then_inc(self._sem((e, self.epoch[e])), 1)
            self.map[e][i] = (self.epoch[e], self.cnt[e])
        self.ninstr += 1
        self._record((("c", e), i), reads, writes)

    def dma(self, q, out, in_, reads, writes):
        i = self.dnext
        self.dnext = (self.dnext + 1) % len(self.dsem)
        pre = (("d", i), self.dcnt[i]) if self.dcnt[i] > 0 else None
        ins = self._with_waits(q, reads, writes, lambda eng: eng.dma_start(out=out, in_=in_), pre=pre)
        self.dcnt[i] += 16
        assert self.dcnt[i] < 60000
        ins.then_inc(self.dsem[i], 16)
        self.ninstr += 1
        self._record((("d", i), self.dcnt[i]), reads, writes)

    def barrier(self):
        evs = []
        for f in self.idx:
            if self.idx[f] > 0:
                evs.append((("c", f), self.idx[f]))
        for i, v in enumerate(self.dcnt):
            if v > 0:
                evs.append((("d", i), v))
        for e in ["pe", "act", "dve", "pool", "sp"]:
            for key, val in evs:
                if key == ("c", e):
                    continue
                k = (e, key)
                if self.waited.get(k, 0) >= val:
                    continue
                self.waited[k] = val
                self._emit_wait(e, key, val)
        self.lastw = {}
        self.rd = {}


class Ring:
    def __init__(self, items):
        self.items = items
        self.i = 0

    def next(self):
        it = self.items[self.i]
        self.i = (self.i + 1) % len(self.items)
        return it


def make_cfg(SP=4096, SEG=2048):
    cfg = dict(SP=SP, SEG=SEG)
    runs = []
    off = 0
    for name, n, full, seq in [("pA", SP, True, 0), ("pB", SP, True, 1), ("own", SEG, True, 2)] + [
        (f"o{r}", SEG, False, 2) for r in range(7)
    ]:
        runs.append(dict(name=name, n=n, full=full, seq=seq, off=off, idx=len(runs)))
        off += n + 4
    cfg["runs"] = runs
    cfg["NTOT"] = off
    cfg["NOWN"] = 2 * SP + SEG
    o = 0
    for r in runs[:3]:
        r["ooff"] = o
        o += r["n"]
    cfg["ctx_n"] = [SP, SP, 8 * SEG]
    runs[0]["ctx"], runs[0]["coff"] = 0, 0
    runs[1]["ctx"], runs[1]["coff"] = 1, 0
    runs[2]["ctx"], runs[2]["coff"] = 2, 0
    for r in range(7):
        runs[3 + r]["ctx"], runs[3 + r]["coff"] = 2, (1 + r) * SEG
    return cfg


def tiles_of(n, w=512):
    return [(t0, min(w, n - t0)) for t0 in range(0, n, w)]


def build(cfg, debug=False, stages=("A", "CONV", "DN", "ATT", "C1", "C2"), needed=None, two_pass=True):
    if two_pass and needed is None:
        dry = build(cfg, debug=debug, stages=stages, two_pass=False)
        needed = dry._needed
    nc = bass.Bass("TRN2", target_bir_lowering=False)
    runs = cfg["runs"]
    NTOT = cfg["NTOT"]
    NOWN = cfg["NOWN"]
    NR = len(runs)
    okind = "ExternalOutput" if debug else "Internal"

    declared = set()

    def din(name, shape, dt=F32, need=True):
        if not need:
            return None
        declared.add(name)
        return nc.dram_tensor(name, list(shape), dt, kind="ExternalInput").ap()

    dbg_names = os.environ.get("DBG_OUT", "").split(",") if debug else []

    def dscr(name, shape, dt=F32):
        kind = "ExternalOutput" if (debug and (name in dbg_names or dbg_names == ["all"])) else "Internal"
        return nc.dram_tensor(name, list(shape), dt, kind=kind).ap()

    XT = din("XT", [D, NTOT], need=("A" in stages or "C1" in stages))
    COS = din("COS", [128, NTOT])
    SIN = din("SIN", [128, NTOT])
    FL = din("FL", [128, NR * 8])
    CW = din("CW", [128, NR * 24 * 5])
    CT = din("CT", [128, 8 * 3])
    W_ADA = din("w_ada", [D, 6 * D], need="S0" in stages or "A" in stages or "C1" in stages or "C2" in stages)
    B_ADAT = din("b_adaT", [128, 48])
    N1WT = din("n1wT", [128, 8])
    N2WT = din("n2wT", [128, 8])
    FNWT = din("fnwT", [128, 8])
    W_IN = din("w_in", [D, N_IN], need="A" in stages)
    QKNW = din("qknw", [128, 2])
    DNW = din("dnw", [128, 1])
    ALOG = din("alog_rep", [128, 16])
    DTB = din("dtb_rep", [128, 16])
    W_PA = din("w_proj_a", [D, D], need="C1" in stages)
    W_PB = din("w_proj_b", [D, D], need="C1" in stages)
    W_OUT = din("w_out", [D, D], need="C1" in stages)
    W_F1 = din("w_ffn_in", [D, 2 * DFF], need="C2" in stages)
    W_F2 = din("w_ffn_out", [DFF, D], need="C2" in stages)
    CONSTS = din("consts", [128, 10 * 128])
    YT = nc.dram_tensor("YT", [D, NOWN], F32, kind="ExternalOutput").ap()
    QT = dscr("QT", [8 * 128, NOWN], BF16)
    KT = [dscr(f"KT{c}", [2 * 128, cfg["ctx_n"][c]], BF16) for c in range(3)]
    VV = [dscr(f"VV{c}", [cfg["ctx_n"][c], 256], BF16) for c in range(3)]
    DPRE3 = [dscr(f"DPRE{i}", [1024, NTOT]) for i in range(3)]
    ZS = dscr("ZS", [D, NOWN])
    GATES = dscr("GATES", [2 * D, NOWN])
    BG = dscr("BG", [NTOT, 33])
    DQKV3 = [dscr(f"DQKV{i}", [1024, NTOT]) for i in range(3)]
    OF = dscr("OF", [D, NOWN])
    OB = dscr("OB", [D, NOWN])
    ATT = dscr("ATT", [D, NOWN], BF16)
    X1 = dscr("X1", [D, NOWN])

    with contextlib.ExitStack() as es:
        sch = Sched(nc, es, needed=needed)

        def sb(name, shape, dt=F32, stack=es):
            return stack.enter_context(nc.sbuf_tensor("sb_" + name, list(shape), dt))

        psb = [es.enter_context(nc.psum_tensor(f"ps{i}", [128, 512], F32)) for i in range(8)]
        psring = Ring([(psb[i], ("ps", i)) for i in range(8)])

        consts = sb("consts", [128, 10 * 128])
        sch.dma("sp", consts[:], CONSTS[:, :], [], ["consts"])
        ident = consts[:, 0:128]
        ones = consts[:, 128:256]
        rrot = consts[:, 256:384]
        fl = sb("fl", [128, NR * 8])
        sch.dma("sp", fl[:], FL[:, :], [], ["fl"])
        modt = sb("modt", [128, 48 * 3])
        a1 = sb("a1", [128, 8 * 3])
        a2 = sb("a2", [128, 8 * 3])
        modv = modt[:].rearrange("p (f s) -> p f s", s=3)
        a1v = a1[:].rearrange("p (k s) -> p k s", s=3)
        a2v = a2[:].rearrange("p (k s) -> p k s", s=3)

        def flag(r, j):
            return fl[:, r * 8 + j: r * 8 + j + 1]

        with contextlib.ExitStack() as st:
          if W_ADA is not None:
              ct = sb("ct", [128, 24], stack=st)
              sct = sb("sct", [128, 24], stack=st)
              badat = sb("badat", [128, 48], stack=st)
              n1wt = sb("n1wt", [128, 8], stack=st)
              n2wt = sb("n2wt", [128, 8], stack=st)
              wab = [sb(f"wab{i}", [128, 8 * 768], stack=st) for i in range(2)]
              sch.dma("sp", ct[:], CT[:, :], [], ["ct"])
              sch.dma("sp", badat[:], B_ADAT[:, :], [], ["badat"])
              sch.dma("sp", n1wt[:], N1WT[:, :], [], ["n1wt"])
              sch.dma("sp", n2wt[:], N2WT[:, :], [], ["n2wt"])
              sch.op("act", ["ct"], ["sct"], lambda e: e.activation(out=sct[:], in_=ct[:], func=AF.Silu))
              sctv = sct[:].rearrange("p (k s) -> p k s", s=3)
              ps, psk = psring.next()
              W_ADAv = W_ADA.rearrange("(kc p) c -> p kc c", p=128)
              for blk in range(8):
                  wa = wab[blk % 2]
                  wav = wa[:].rearrange("p (k c) -> p k c", c=768)
                  sch.dma("sp", wav, W_ADAv[:, :, blk * 768:(blk + 1) * 768], [], [("wab", blk % 2)])
                  for f6 in range(6):
                      fc = blk * 6 + f6
                      for kc in range(8):
                          sch.op("pe", [("wab", blk % 2), "sct"], [psk],
                                 lambda e, fc=fc, kc=kc, f6=f6, wav=wav: e.matmul(
                                     ps[:, fc * 3:(fc + 1) * 3], lhsT=wav[:, kc, f6 * 128:(f6 + 1) * 128],
                                     rhs=sctv[:, kc, :], start=(kc == 0), stop=(kc == 7)))
              psv = ps[:, 0:144].rearrange("p (f s) -> p f s", s=3)
              for s in range(3):
                  sch.op("dve", [psk, "badat"], ["modt"],
                         lambda e, s=s: e.tensor_tensor(out=modv[:, :, s], in0=psv[:, :, s], in1=badat[:], op=ALU.add))
              for s in range(3):
                  sch.op("dve", ["modt", "n1wt"], ["a1"],
                         lambda e, s=s: e.scalar_tensor_tensor(out=a1v[:, :, s], in0=modv[:, 8:16, s], scalar=1.0,
                                                               in1=n1wt[:], op0=ALU.add, op1=ALU.mult))
                  sch.op("dve", ["modt", "n2wt"], ["a2"],
                         lambda e, s=s: e.scalar_tensor_tensor(out=a2v[:, :, s], in0=modv[:, 32:40, s], scalar=1.0,
                                                               in1=n2wt[:], op0=ALU.add, op1=ALU.mult))
              sch.barrier()

        def rstd_from_ps(psS, pskS, w, scale, dst, dstk, tmp, tmpk):
            sch.op("act", [pskS], [tmpk],
                   lambda e: e.activation(out=tmp[:, :w], in_=psS[:, :w], func=AF.Ln, scale=scale, bias=epsb[:, 0:1]))
            sch.op("act", [tmpk], [dstk],
                   lambda e: e.activation(out=dst[:, :w], in_=tmp[:, :w], func=AF.Exp, scale=-0.5))

        epsb = sb("epsb", [128, 2])
        sch.op("dve", [], ["epsb"], lambda e: e.memset(epsb[:, 0:1], EPS))
        sch.op("dve", [], ["epsb"], lambda e: e.memset(epsb[:, 1:2], 1.0))

        if "A" in stages:
          with contextlib.ExitStack() as st:
            win = sb("win", [128, 8 * N_IN], BF16, stack=st)
            winv = win[:].rearrange("p (k c) -> p k c", c=N_IN)
            W_INv = W_IN.rearrange("(kc p) c -> p kc c", p=128)
            for kc in range(8):
                for c0 in range(0, N_IN, 1928):
                    sch.dma("pool", winv[:, kc, c0:c0 + 1928], W_INv[:, kc, c0:c0 + 1928], [], ["win"])
            qknw = sb("qknw", [128, 2], stack=st)
            sch.dma("sp", qknw[:], QKNW[:, :], [], ["qknw"])
            alog = sb("alog", [128, 16], stack=st)
            dtb = sb("dtb", [128, 16], stack=st)
            nega = sb("nega", [128, 16], stack=st)
            sch.dma("sp", alog[:], ALOG[:, :], [], ["alog"])
            sch.dma("sp", dtb[:], DTB[:, :], [], ["dtb"])
            sch.op("act", ["alog"], ["nega0"], lambda e: e.activation(out=nega[:], in_=alog[:], func=AF.Exp))
            sch.op("dve", ["nega0"], ["nega"],
                   lambda e: e.tensor_scalar(out=nega[:], in0=nega[:], scalar1=-1.0, scalar2=None, op0=ALU.mult))
            xts = [sb(f"xt{i}", [128, 8 * 512], stack=st) for i in range(1)]
            hts = [sb(f"ht{i}", [128, 8 * 512], BF16, stack=st) for i in range(2)]
            sqs = Ring([(sb(f"sq{i}", [128, 512], stack=st), ("sq", i)) for i in range(3)])
            tmps = Ring([(sb(f"tmpa{i}", [128, 512], stack=st), ("tmpa", i)) for i in range(3)])
            rstds = Ring([(sb(f"rstd{i}", [128, 512], stack=st), ("rstd", i)) for i in range(2)])
            xns = Ring([(sb(f"xn{i}", [128, 512], stack=st), ("xn", i)) for i in range(2)])
            t1s = Ring([(sb(f"t1{i}", [128, 512], stack=st), ("t1", i)) for i in range(2)])
            obf = Ring([(sb(f"obf{i}", [128, 512], BF16, stack=st), ("obf", i)) for i in range(3)])
            of32 = Ring([(sb(f"of32{i}", [128, 512], stack=st), ("of32", i)) for i in range(4)])
            coss = [sb(f"cos{i}", [128, 512], stack=st) for i in range(2)]
            sins = [sb(f"sin{i}", [128, 512], stack=st) for i in range(2)]
            vbf = Ring([(sb(f"vbf{i}", [128, 256], BF16, stack=st), ("vbf", i)) for i in range(2)])
            bgt = Ring([(sb(f"bgt{i}", [128, 48], stack=st), ("bgt", i)) for i in range(2)])
            XTv = XT.rearrange("(kc p) t -> p kc t", p=128)
            tix = 0
            for run in runs:
                n, full, s, off = run["n"], run["full"], run["seq"], run["off"]
                ri = run["idx"]
                for (t0, w) in tiles_of(n + 4):
                    b = tix % 2
                    tix += 1
                    xt = xts[0][:].rearrange("p (k t) -> p k t", t=512)
                    ht = hts[b][:].rearrange("p (k t) -> p k t", t=512)
                    kx, kh = ("xt", 0), ("ht", b)
                    g0 = off + t0
                    sch.dma("sp", xt[:, :, :w], XTv[:, :, g0:g0 + w], [], [kx])
                    sch.dma("sp", coss[b][:, :w], COS[:, g0:g0 + w], [], [("cos", b)])
                    sch.dma("sp", sins[b][:, :w], SIN[:, g0:g0 + w], [], [("sin", b)])
                    psS, pskS = psring.next()
                    for kc in range(8):
                        sq, sqk = sqs.next()
                        sch.op("act", [kx], [sqk],
                               lambda e, sq=sq, kc=kc: e.activation(out=sq[:, :w], in_=xt[:, kc, :w], func=AF.Square))
                        sch.op("pe", [sqk, "consts"], [pskS],
                               lambda e, sq=sq, kc=kc: e.matmul(psS[:, :w], lhsT=ones, rhs=sq[:, :w],
                                                                start=(kc == 0), stop=(kc == 7)))
                    rstd, rstdk = rstds.next()
                    tmp, tmpk = tmps.next()
                    rstd_from_ps(psS, pskS, w, 1.0 / D, rstd, rstdk, tmp, tmpk)
                    for kc in range(8):
                        tmp, tmpk = tmps.next()
                        sch.op("dve", [kx, rstdk], [tmpk],
                               lambda e, tmp=tmp, kc=kc: e.tensor_tensor(out=tmp[:, :w], in0=xt[:, kc, :w],
                                                                         in1=rstd[:, :w], op=ALU.mult))
                        sch.op("act", [tmpk, "a1", "modt"], [kh],
                               lambda e, tmp=tmp, kc=kc: e.activation(out=ht[:, kc, :w], in_=tmp[:, :w],
                                                                      func=AF.Identity, scale=a1v[:, kc, s:s + 1],
                                                                      bias=modv[:, 0 + kc, s:s + 1]))

                    def proj(c0, m=128):
                        ps, psk = psring.next()
                        for kc in range(8):
                            sch.op("pe", [kh, "win"], [psk],
                                   lambda e, kc=kc: e.matmul(ps[:m, :w], lhsT=winv[:, kc, c0:c0 + m], rhs=ht[:, kc, :w],
                                                             start=(kc == 0), stop=(kc == 7)))
                        return ps, psk

                    def head_norm_rope(c0, wcol, dst_dram):
                        ps, psk = proj(c0)
                        sq, sqk = sqs.next()
                        sch.op("act", [psk], [sqk], lambda e: e.activation(out=sq[:, :w], in_=ps[:, :w], func=AF.Square))
                        ps2, psk2 = psring.next()
                        sch.op("pe", [sqk, "consts"], [psk2],
                               lambda e: e.matmul(ps2[:, :w], lhsT=ones, rhs=sq[:, :w], start=True, stop=True))
                        tmp, tmpk = tmps.next()
                        rs, rsk = rstds.next()
                        rstd_from_ps(ps2, psk2, w, 1.0 / 128, rs, rsk, tmp, tmpk)
                        xn, xnk = xns.next()
                        sch.op("dve", [psk, rsk, "qknw"], [xnk],
                               lambda e: e.scalar_tensor_tensor(out=xn[:, :w], in0=ps[:, :w], scalar=qknw[:, wcol:wcol + 1],
                                                                in1=rs[:, :w], op0=ALU.mult, op1=ALU.mult))
                        ps3, psk3 = psring.next()
                        sch.op("pe", [xnk, "consts"], [psk3],
                               lambda e: e.matmul(ps3[:, :w], lhsT=rrot, rhs=xn[:, :w], start=True, stop=True))
                        t1, t1k = t1s.next()
                        sch.op("pool", [xnk, ("cos", b)], [t1k],
                               lambda e: e.tensor_tensor(out=t1[:, :w], in0=xn[:, :w], in1=coss[b][:, :w], op=ALU.mult))
                        tmp2, tmp2k = tmps.next()
                        sch.op("dve", [psk3, ("sin", b)], [tmp2k],
                               lambda e: e.tensor_tensor(out=tmp2[:, :w], in0=ps3[:, :w], in1=sins[b][:, :w], op=ALU.mult))
                        ob, obk = obf.next()
                        sch.op("dve", [t1k, tmp2k], [obk],
                               lambda e: e.tensor_tensor(out=ob[:, :w], in0=t1[:, :w], in1=tmp2[:, :w], op=ALU.add))
                        lo, hi = max(t0, 2), min(t0 + w, n + 2)
                        if hi > lo:
                            sch.dma("pool", dst_dram(lo - 2, hi - 2), ob[:, lo - t0:hi - t0], [obk], [])

                    if full:
                        for h in range(8):
                            head_norm_rope(h * 128, 0,
                                           lambda a, bb, h=h: QT[h * 128:(h + 1) * 128, run["ooff"] + a: run["ooff"] + bb])
                    for g in range(2):
                        head_norm_rope(1024 + g * 128, 1,
                                       lambda a, bb, g=g: KT[run["ctx"]][g * 128:(g + 1) * 128,
                                                                         run["coff"] + a: run["coff"] + bb])
                    for sub in range(0, w, 128):
                        sw = min(128, w - sub)
                        ps, psk = psring.next()
                        for kc in range(8):
                            sch.op("pe", [kh, "win"], [psk],
                                   lambda e, kc=kc: e.matmul(ps[:sw, 0:256], lhsT=ht[:, kc, sub:sub + sw],
                                                             rhs=winv[:, kc, 1280:1536], start=(kc == 0), stop=(kc == 7)))
                        for kc in range(8):
                            sch.op("pe", [kh, "win"], [psk],
                                   lambda e, kc=kc: e.matmul(ps[:sw, 256:288], lhsT=ht[:, kc, sub:sub + sw],
                                                             rhs=winv[:, kc, 5632:5664], start=(kc == 0), stop=(kc == 7)))
                        vb, vbk = vbf.next()
                        sch.op("act", [psk], [vbk], lambda e: e.activation(out=vb[:sw, :], in_=ps[:sw, 0:256], func=AF.Copy))
                        lo, hi = max(t0 + sub, 2), min(t0 + sub + sw, n + 2)
                        if hi > lo:
                            c0 = run["coff"]
                            sch.dma("pool", VV[run["ctx"]][c0 + lo - 2:c0 + hi - 2, :],
                                    vb[lo - t0 - sub:hi - t0 - sub, :], [vbk], [])
                        bg, bgk = bgt.next()
                        sch.op("act", [psk], [bgk],
                               lambda e: e.activation(out=bg[:sw, 0:16], in_=ps[:sw, 256:272], func=AF.Sigmoid))
                        sch.op("dve", [psk, "dtb"], [bgk],
                               lambda e: e.tensor_tensor(out=bg[:sw, 32:48], in0=ps[:sw, 272:288], in1=dtb[:sw, 0:16],
                                                         op=ALU.add))
                        sch.op("act", [bgk], [bgk], lambda e: e.activation(out=bg[:sw, 32:48], in_=bg[:sw, 32:48], func=AF.Exp))
                        sch.op("act", [bgk], [bgk],
                               lambda e: e.activation(out=bg[:sw, 32:48], in_=bg[:sw, 32:48], func=AF.Ln, bias=epsb[:sw, 1:2]))
                        sch.op("dve", [bgk, "nega"], [bgk],
                               lambda e: e.tensor_tensor(out=bg[:sw, 16:32], in0=bg[:sw, 32:48], in1=nega[:sw, 0:16], op=ALU.mult))
                        if not full:
                            for cf, cb in ((0, 8), (16, 24)):
                                sch.op("dve", [bgk, "fl"], [bgk],
                                       lambda e, cb=cb: e.tensor_scalar(out=bg[:sw, cb:cb + 8], in0=bg[:sw, cb:cb + 8],
                                                                        scalar1=flag(ri, 6)[:sw, :], scalar2=None, op0=ALU.mult))
                                sch.op("dve", [bgk, "fl"], [bgk],
                                       lambda e, cf=cf, cb=cb: e.scalar_tensor_tensor(
                                           out=bg[:sw, cf:cf + 8], in0=bg[:sw, cf:cf + 8], scalar=flag(ri, 5)[:sw, :],
                                           in1=bg[:sw, cb:cb + 8], op0=ALU.mult, op1=ALU.add))
                        sch.dma("pool", BG[g0 + sub:g0 + sub + sw, 0:32], bg[:sw, 0:32], [bgk], [])
                    for j in range(24):
                        if (not full) and j < 8:
                            continue
                        ps, psk = proj(1536 + j * 128)
                        o, ok = of32.next()
                        if j % 2 == 0:
                            sch.op("act", [psk], [ok], lambda e, o=o, ps=ps: e.activation(out=o[:, :w], in_=ps[:, :w], func=AF.Copy))
                        else:
                            sch.op("dve", [psk], [ok], lambda e, o=o, ps=ps: e.tensor_copy(out=o[:, :w], in_=ps[:, :w]))
                        sch.dma("pool", DPRE3[j // 8][(j % 8) * 128:(j % 8 + 1) * 128, g0:g0 + w], o[:, :w], [ok], [])
                    if full:
                        lo, hi = max(t0, 2), min(t0 + w, n + 2)
                        oo = run["ooff"]
                        for j in range(8):
                            ps, psk = proj(4608 + j * 128)
                            o, ok = of32.next()
                            sch.op("act", [psk], [ok], lambda e, o=o, ps=ps: e.activation(out=o[:, :w], in_=ps[:, :w], func=AF.Silu))
                            if hi > lo:
                                sch.dma("pool", ZS[j * 128:(j + 1) * 128, oo + lo - 2:oo + hi - 2], o[:, lo - t0:hi - t0], [ok], [])
                        for j in range(16):
                            ps, psk = proj(5664 + j * 128)
                            o, ok = of32.next()
                            sch.op("act", [psk], [ok], lambda e, o=o, ps=ps: e.activation(out=o[:, :w], in_=ps[:, :w], func=AF.Sigmoid))
                            if hi > lo:
                                sch.dma("pool", GATES[j * 128:(j + 1) * 128, oo + lo - 2:oo + hi - 2], o[:, lo - t0:hi - t0], [ok], [])
            sch.barrier()

        if "CONV" in stages:
          with contextlib.ExitStack() as st:
            cw = sb("cw", [128, NR * 24 * 5], stack=st)
            sch.dma("sp", cw[:], CW[:, :], [], ["cw"])
            cwv = cw[:].rearrange("p (r j k) -> p r j k", j=24, k=5)
            wins = Ring([(sb(f"cwin{i}", [128, 516], stack=st), ("cwin", i)) for i in range(3)])
            accs = Ring([(sb(f"cacc{i}", [128, 512], stack=st), ("cacc", i)) for i in range(2)])
            sils = Ring([(sb(f"csil{i}", [128, 512], stack=st), ("csil", i)) for i in range(3)])
            sq2 = Ring([(sb(f"csq{i}", [128, 512], stack=st), ("csq", i)) for i in range(2)])
            tm2 = Ring([(sb(f"ctm{i}", [128, 512], stack=st), ("ctm", i)) for i in range(2)])
            rn2 = Ring([(sb(f"crn{i}", [128, 512], stack=st), ("crn", i)) for i in range(2)])
            ou2 = Ring([(sb(f"cou{i}", [128, 512], stack=st), ("cou", i)) for i in range(3)])
            for run in runs:
                n, full, off, ri = run["n"], run["full"], run["off"], run["idx"]
                for j in range(24):
                    if (not full) and j < 8:
                        continue
                    for (t0, w) in tiles_of(n):
                        g0 = off + t0
                        win, wk = wins.next()
                        sch.dma("sp", win[:, :w + 4], DPRE3[j // 8][(j % 8) * 128:(j % 8 + 1) * 128, g0:g0 + w + 4], [], [wk])
                        if t0 == 0:
                            sch.op("dve", [wk, "fl"], [wk],
                                   lambda e: e.tensor_scalar(out=win[:, 0:2], in0=win[:, 0:2], scalar1=flag(ri, 0),
                                                             scalar2=None, op0=ALU.mult))
                        if t0 + w == n:
                            sch.op("dve", [wk, "fl"], [wk],
                                   lambda e: e.tensor_scalar(out=win[:, w + 2:w + 4], in0=win[:, w + 2:w + 4],
                                                             scalar1=flag(ri, 1), scalar2=None, op0=ALU.mult))
                        acc, ak = accs.next()
                        sch.op("dve", [wk, "cw"], [ak],
                               lambda e: e.tensor_scalar(out=acc[:, :w], in0=win[:, 0:w], scalar1=cwv[:, ri, j, 0:1],
                                                         scalar2=None, op0=ALU.mult))
                        for k in range(1, 5):
                            sch.op("dve", [wk, "cw", ak], [ak],
                                   lambda e, k=k: e.scalar_tensor_tensor(out=acc[:, :w], in0=win[:, k:k + w],
                                                                         scalar=cwv[:, ri, j, k:k + 1], in1=acc[:, :w],
                                                                         op0=ALU.mult, op1=ALU.add))
                        sil, sk = sils.next()
                        sch.op("act", [ak], [sk], lambda e: e.activation(out=sil[:, :w], in_=acc[:, :w], func=AF.Silu))
                        dst = DQKV3[j // 8][(j % 8) * 128:(j % 8 + 1) * 128, off + 2 + t0: off + 2 + t0 + w]
                        if j >= 16:
                            sch.dma("pool", dst, sil[:, :w], [sk], [])
                            continue
                        sq, sqk = sq2.next()
                        sch.op("act", [sk], [sqk], lambda e: e.activation(out=sq[:, :w], in_=sil[:, :w], func=AF.Square))
                        ps, psk = psring.next()
                        sch.op("pe", [sqk, "consts"], [psk],
                               lambda e: e.matmul(ps[:, :w], lhsT=ones, rhs=sq[:, :w], start=True, stop=True))
                        tm, tmk = tm2.next()
                        rn, rnk = rn2.next()
                        rstd_from_ps(ps, psk, w, 1.0, rn, rnk, tm, tmk)
                        ou, ouk = ou2.next()
                        cmul = (128.0 ** -0.5) if j < 8 else 1.0
                        sch.op("dve", [sk, rnk], [ouk],
                               lambda e: e.scalar_tensor_tensor(out=ou[:, :w], in0=sil[:, :w], scalar=cmul, in1=rn[:, :w],
                                                                op0=ALU.mult, op1=ALU.mult))
                        sch.dma("pool", dst, ou[:, :w], [ouk], [])
            sch.barrier()

        if "DN" in stages:
          with contextlib.ExitStack() as st:
            MASKA = [consts[:, 384:512], consts[:, 512:640]]
            MASKQ = [consts[:, 640:768], consts[:, 768:896]]
            CUM = [consts[:, 896:1024], consts[:, 1024:1152]]
            ONESBD = consts[:, 1152:1280]
            WIN = int(os.environ.get("DN_WIN", "6"))
            rings = {}

            def RB(name, depth=3, shape=(128, 128), dt=F32):
                if name not in rings:
                    rings[name] = Ring([(sb(f"dn_{name}{i}", list(shape), dt, stack=st), (name, i)) for i in range(depth)])
                return rings[name].next()

            slot_rings = [Ring([(sb(f"dn_l{sl}_{i}", [128, 128], stack=st), ("dnl", sl, i)) for i in range(8)]) for sl in range(WIN)]
            slot_rh = [[(sb(f"dn_rh{sl}_{i}", [128, 128], stack=st), ("dnrh", sl, i)) for i in range(2)] for sl in range(WIN)]
            free_slots = list(range(WIN))
            Sst, S16 = {}, {}
            for h in range(8):
                for d in range(2):
                    Sst[(h, d)] = [sb(f"dn_S{h}_{d}_{i}", [128, 128], stack=st) for i in range(2)]
                    S16[(h, d)] = [sb(f"dn_Sh{h}_{d}_{i}", [128, 128], BF16, stack=st) for i in range(2)]
            Scur = {k: 0 for k in Sst}
            SF = [sb(f"dn_SF{h}", [128, 128], stack=st) for h in range(8)]
            SB = [sb(f"dn_SB{h}", [128, 128], stack=st) for h in range(8)]
            for h in range(8):
                sch.op("pool", [], [("SF", h)], lambda e, h=h: e.memset(SF[h][:], 0.0))
                sch.op("pool", [], [("SB", h)], lambda e, h=h: e.memset(SB[h][:], 0.0))
                sch.op("pool", [], [("Sb", h, 0, 0)], lambda e, h=h: e.memset(Sst[(h, 0)][0][:], 0.0))
            DQv3 = [DQKV3[i].rearrange("(h d) c -> d h c", h=8) for i in range(3)]
            evac_flip = [0]
            chain_pos = {}

            def evac(dst, dstk, src, srck):
                evac_flip[0] ^= 1
                if evac_flip[0]:
                    sch.op("act", [srck], [dstk], lambda e: e.activation(out=dst, in_=src, func=AF.Copy))
                else:
                    sch.op("dve", [srck], [dstk], lambda e: e.tensor_copy(out=dst, in_=src))

            def mm(ps, psk, lhsT, rhs, reads, start=True, stop=True):
                sch.op("pe", list(reads), [psk], lambda e: e.matmul(ps, lhsT=lhsT, rhs=rhs, start=start, stop=stop))

            def tr(ps, psk, in_, reads):
                sch.op("pe", list(reads) + ["consts"], [psk], lambda e: e.transpose(out=ps, in_=in_, identity=ident))

            def load_block(run, p, d, blk, nb):
                full, off = run["full"], run["off"]
                tagd = f"d{d}"
                qkvb, qk = RB("qkvb" + tagd, 2, (128, 2 * 1024))
                kq16, kq16k = RB("kq16" + tagd, 2, (128, 2 * 1024), BF16)
                c0 = off + 2 + blk * 512
                for tq in range(3):
                    if tq == 0 and not full:
                        continue
                    for hh in range(2):
                        src = DQv3[tq][:, 2 * p + hh, c0:c0 + nb * 64].rearrange("d (c t) -> d c t", t=64)
                        if tq == 0:
                            sch.dma("pool", kq16[:, 0:nb * 128].rearrange("p (c h t) -> p h c t", h=2, t=64)[:, hh], src, [], [kq16k])
                        else:
                            sch.dma("sp", qkvb[:, (tq - 1) * 1024:(tq - 1) * 1024 + nb * 128].rearrange("p (c h t) -> p h c t", h=2, t=64)[:, hh],
                                    src, [], [qk])
                sch.op("act", [qk], [kq16k], lambda e: e.activation(out=kq16[:, 1024:1024 + nb * 128], in_=qkvb[:, 0:nb * 128], func=AF.Copy))
                bgb, bk = RB("bgb" + tagd, 2, (128, 8 * 64))
                bgv = bgb[:].rearrange("p (c f) -> p c f", f=64)
                BGr = BG[c0:c0 + nb * 64, :].rearrange("(c t) f -> t c f", t=64)
                for half in range(2):
                    pr = slice(half * 64, half * 64 + 64)
                    sh = 1 if half == 1 else 0
                    sch.dma("sp", bgv[pr, :nb, 0:16], BGr[:, :, 0:16], [], [bk])
                    sch.dma("sp", bgv[pr, :nb, 16:32], BGr[:, :, sh:sh + 16], [], [bk])
                    sch.dma("sp", bgv[pr, :nb, 32:48], BGr[:, :, 16:32], [], [bk])
                    sch.dma("sp", bgv[pr, :nb, 48:64], BGr[:, :, 16 + sh:32 + sh], [], [bk])
                psc, psck = psring.next()
                pst, pstk = psring.next()
                pscv = psc[:, 0:256].rearrange("p (c f) -> p c f", f=32)
                pstv = pst[:, 0:256].rearrange("p (c f) -> p c f", f=32)
                mm(pscv[:, :nb, :], psck, CUM[d], bgv[:, :nb, 32:64], [bk, "consts"])
                mm(pstv[:, :nb, :], pstk, ONESBD, bgv[:, :nb, 32:64], [bk, "consts"])
                sm, smk = RB("small" + tagd, 2, (128, 8 * 256))
                smv = sm[:].rearrange("p (a c f) -> p a c f", a=8, f=32)
                GC, EGC, EDK, EGT, BE, NB, EDKA, EDKB = (smv[:, a, :, :] for a in range(8))
                sch.op("act", [psck], [smk], lambda e: e.activation(out=GC[:, :nb, :], in_=pscv[:, :nb, :], func=AF.Copy))
                sch.op("act", [psck], [smk], lambda e: e.activation(out=EGC[:, :nb, :], in_=pscv[:, :nb, :], func=AF.Exp))
                sch.op("dve", [pstk, smk], [smk],
                       lambda e: e.tensor_tensor(out=EDK[:, :nb, :], in0=pstv[:, :nb, :], in1=GC[:, :nb, :], op=ALU.subtract))
                sch.op("act", [smk], [smk], lambda e: e.activation(out=EDK[:, :nb, :], in_=EDK[:, :nb, :], func=AF.Exp))
                sch.op("act", [pstk], [smk], lambda e: e.activation(out=EGT[:, :nb, :], in_=pstv[:, :nb, :], func=AF.Exp))
                sch.op("dve", [smk, "consts"], [smk],
                       lambda e: e.tensor_scalar(out=EDKA[:, :nb, :], in0=EDK[:, :nb, :], scalar1=ONESBD[:, 0:1], scalar2=None, op0=ALU.mult))
                sch.op("dve", [smk, "consts"], [smk],
                       lambda e: e.tensor_scalar(out=EDKB[:, :nb, :], in0=EDK[:, :nb, :], scalar1=ONESBD[:, 64:65], scalar2=None, op0=ALU.mult))
                sch.op("dve", [bk, smk], [smk],
                       lambda e: e.tensor_tensor(out=BE[:, :nb, :], in0=bgv[:, :nb, 0:32], in1=EGC[:, :nb, :], op=ALU.mult))
                sch.op("dve", [bk], [smk],
                       lambda e: e.tensor_scalar(out=NB[:, :nb, :], in0=bgv[:, :nb, 0:32], scalar1=-1.0, scalar2=None, op0=ALU.mult))
                return dict(qkvb=qkvb, qk=qk, kq16=kq16, kq16k=kq16k, bgv=bgv, bk=bk, GC=GC, EGC=EGC, EDK=EDK, EGT=EGT, BE=BE, NB=NB,
                            EDKA=EDKA, EDKB=EDKB, smk=smk)

            def inst_gen(run, p, d, c, bc, ostctx, tseq):
                full = run["full"]
                nch = run["n"] // 64
                blk, ch = c // 8, c % 8
                nb = min(8, nch - blk * 8)
                slot = free_slots.pop(0)
                slot_rings[slot].i = 0
                TB = slot_rings[slot].next
                qkvb, qk, kq16, kq16k, bgv, bk, smk = bc["qkvb"], bc["qk"], bc["kq16"], bc["kq16k"], bc["bgv"], bc["bk"], bc["smk"]
                colp = 16 + d * 8 + 2 * p
                pcol = lambda A: A[:, ch, colp:colp + 1]
                cs = slice(ch * 64, ch * 64 + 64)
                KTp, VTp = (qkvb[:, tq * 1024 + ch * 128: tq * 1024 + ch * 128 + 128] for tq in (0, 1))
                Q16, K16 = (kq16[:, tq * 1024 + ch * 128: tq * 1024 + ch * 128 + 128] for tq in (0, 1))
                ps_k, ps_kk = psring.next()
                tr(ps_k[:, 0:128], ps_kk, KTp, [qk])
                ps_v, ps_vk = psring.next()
                tr(ps_v[:, 0:128], ps_vk, VTp, [qk])
                dg, dgk = TB()
                sch.op("pool", ["consts", smk], [dgk],
                       lambda e: e.tensor_scalar(out=dg[:], in0=ident, scalar1=pcol(bc["GC"]), scalar2=None, op0=ALU.mult))
                rhsk, rhskk = slot_rh[slot][0]
                sch.op("act", [ps_kk, smk], [rhskk],
                       lambda e: e.activation(out=rhsk[:], in_=ps_k[:, 0:128], func=AF.Identity, scale=pcol(bc["BE"])))
                kda, kdak = RB("kda", WIN + 2, dt=BF16)
                kdb, kdbk = RB("kdb", WIN + 2, dt=BF16)
                sch.op("dve", [ps_kk, smk], [kdak],
                       lambda e: e.tensor_scalar(out=kda[:], in0=ps_k[:, 0:128], scalar1=pcol(bc["EDKA"]), scalar2=None, op0=ALU.mult))
                sch.op("dve", [ps_kk, smk], [kdbk],
                       lambda e: e.tensor_scalar(out=kdb[:], in0=ps_k[:, 0:128], scalar1=pcol(bc["EDKB"]), scalar2=None, op0=ALU.mult))
                rhsv, rhsvk = slot_rh[slot][1]
                sch.op("act", [ps_vk, bk], [rhsvk],
                       lambda e: e.activation(out=rhsv[:], in_=ps_v[:, 0:128], func=AF.Identity, scale=pcol(bgv)))
                yield
                ps_g, ps_gk = psring.next()
                mm(ps_g[:, 0:128], ps_gk, K16, K16, [kq16k])
                ps_r, ps_rk = psring.next()
                mm(ps_r[:, 0:128], ps_rk, ones, dg[:], [dgk, "consts"])
                gm, gmk = TB()
                sch.op("dve", [ps_gk, "consts"], [gmk],
                       lambda e: e.tensor_tensor(out=gm[:], in0=ps_g[:, 0:128], in1=MASKA[d], op=ALU.mult))
                t1, t1k = TB()
                sch.op("dve", [ps_rk, smk], [t1k],
                       lambda e: e.tensor_scalar(out=t1[:], in0=ps_r[:, 0:128], scalar1=pcol(bc["GC"]), scalar2=0.0,
                                                 op0=ALU.subtract, op1=ALU.max))
                if full:
                    ps_q, ps_qk = psring.next()
                    mm(ps_q[:, 0:128], ps_qk, K16, Q16, [kq16k])
                    t2, t2k = TB()
                    sch.op("dve", [ps_rk, smk], [t2k],
                           lambda e: e.tensor_scalar(out=t2[:], in0=ps_r[:, 0:128], scalar1=pcol(bc["GC"]), scalar2=0.0,
                                                     op0=ALU.subtract, op1=ALU.min))
                    erow, erowk = TB()
                    sch.op("act", [ps_rk], [erowk], lambda e: e.activation(out=erow[:], in_=ps_r[:, 0:128], func=AF.Exp))
                    kqm, kqmk = TB()
                    sch.op("dve", [ps_qk, "consts"], [kqmk],
                           lambda e: e.tensor_tensor(out=kqm[:], in0=ps_q[:, 0:128], in1=MASKQ[d], op=ALU.mult))
                yield
                sch.op("act", [t1k], [t1k], lambda e: e.activation(out=t1[:], in_=t1[:], func=AF.Exp, scale=-1.0))
                b0, b0k = TB()
                sch.op("dve", [gmk, t1k, smk], [b0k],
                       lambda e: e.scalar_tensor_tensor(out=b0[:], in0=gm[:], scalar=pcol(bc["NB"]), in1=t1[:], op0=ALU.mult, op1=ALU.mult))
                if full:
                    sch.op("act", [t2k], [t2k], lambda e: e.activation(out=t2[:], in_=t2[:], func=AF.Exp))
                    aqt, aqtk = RB("aqt", WIN + 2, dt=BF16)
                    sch.op("pool", [kqmk, t2k], [aqtk], lambda e: e.tensor_tensor(out=aqt[:], in0=kqm[:], in1=t2[:], op=ALU.mult))
                    qet, qetk = RB("qet", WIN + 2, dt=BF16)
                    sch.op("pool", [kq16k, erowk], [qetk],
                           lambda e: e.tensor_tensor(out=qet[:], in0=Q16, in1=erow[:], op=ALU.mult))
                yield
                ps_t, ps_tk = psring.next()
                tr(ps_t[:, 0:128], ps_tk, b0[:], [b0k])
                bt, btk = TB()
                sch.op("act", [ps_tk], [btk], lambda e: e.activation(out=bt[:], in_=ps_t[:, 0:128], func=AF.Copy))
                pp, ppk = TB()
                sch.op("dve", [ps_tk, "consts"], [ppk],
                       lambda e: e.tensor_tensor(out=pp[:], in0=ps_t[:, 0:128], in1=ident, op=ALU.add))
                bprev, bprevk, btprev, btprevk = b0, b0k, bt, btk
                for lev in range(1, 6):
                    yield
                    ps_b, ps_bk = psring.next()
                    mm(ps_b[:, 0:128], ps_bk, btprev[:], bprev[:], [btprevk, bprevk])
                    ib, ibk = TB()
                    sch.op("dve", [ps_bk, "consts"], [ibk],
                           lambda e: e.tensor_tensor(out=ib[:], in0=ps_b[:, 0:128], in1=ident, op=ALU.add))
                    if lev < 5:
                        bn, bnk = TB()
                        sch.op("act", [ps_bk], [bnk], lambda e: e.activation(out=bn[:], in_=ps_b[:, 0:128], func=AF.Copy))
                    yield
                    ps_p, ps_pk = psring.next()
                    mm(ps_p[:, 0:128], ps_pk, ib[:], pp[:], [ibk, ppk])
                    pn, pnk = TB()
                    evac(pn[:], pnk, ps_p[:, 0:128], ps_pk)
                    pp, ppk = pn, pnk
                    if lev < 5:
                        ps_b2, ps_b2k = psring.next()
                        tr(ps_b2[:, 0:128], ps_b2k, bn[:], [bnk])
                        btn, btnk = TB()
                        evac(btn[:], btnk, ps_b2[:, 0:128], ps_b2k)
                        bprev, bprevk, btprev, btprevk = bn, bnk, btn, btnk
                TT, TTk = pp, ppk
                yield
                ps_u, ps_uk = psring.next()
                mm(ps_u[:, 0:128], ps_uk, TT[:], rhsv[:], [TTk, rhsvk])
                ps_w, ps_wk = psring.next()
                mm(ps_w[:, 0:128], ps_wk, rhsk[:], TT[:], [TTk, rhskk])
                uu, uuk = RB("uu", WIN + 2)
                evac(uu[:], uuk, ps_u[:, 0:128], ps_uk)
                wta, wtak = RB("wta", WIN + 2, dt=BF16)
                wtb, wtbk = RB("wtb", WIN + 2, dt=BF16)
                if ("wtz", wtak) not in rings:
                    rings[("wtz", wtak)] = True
                    sch.op("pool", [], [wtak], lambda e: e.memset(wta[:], 0.0))
                    sch.op("pool", [], [wtbk], lambda e: e.memset(wtb[:], 0.0))
                sch.op("act", [ps_wk], [wtak], lambda e: e.activation(out=wta[:, 0:64], in_=ps_w[:, 0:64], func=AF.Copy))
                sch.op("dve", [ps_wk], [wtbk], lambda e: e.tensor_copy(out=wtb[:, 64:128], in_=ps_w[:, 64:128]))
                free_slots.append(slot)
                yield
                h0, h1 = 2 * p, 2 * p + 1
                assert chain_pos.get((run["idx"], p, d), 0) == tseq, "chain order violated"
                c0_, c1_ = Scur[(h0, d)], Scur[(h1, d)]
                S0, S1 = Sst[(h0, d)][c0_], Sst[(h1, d)][c1_]
                S0h, S1h = S16[(h0, d)][c0_], S16[(h1, d)][c1_]
                S0n, S1n = Sst[(h0, d)][1 - c0_], Sst[(h1, d)][1 - c1_]
                S0hn, S1hn = S16[(h0, d)][1 - c0_], S16[(h1, d)][1 - c1_]
                s0ck, s1ck = ("Sb", h0, d, c0_), ("Sb", h1, d, c1_)
                s0nk, s1nk = ("Sb", h0, d, 1 - c0_), ("Sb", h1, d, 1 - c1_)
                s0hk, s1hk = ("Sh", h0, d, c0_), ("Sh", h1, d, c1_)
                s0hnk, s1hnk = ("Sh", h0, d, 1 - c0_), ("Sh", h1, d, 1 - c1_)
                Scur[(h0, d)], Scur[(h1, d)] = 1 - c0_, 1 - c1_
                ps_ws, ps_wsk = psring.next()
                mm(ps_ws[:, 0:128], ps_wsk, wta[:], S0h[:], [wtak, s0hk], start=True, stop=False)
                mm(ps_ws[:, 0:128], ps_wsk, wtb[:], S1h[:], [wtbk, s1hk], start=False, stop=True)
                vn, vnk = RB("vn", WIN + 2, dt=BF16)
                sch.op("dve", [uuk, ps_wsk], [vnk],
                       lambda e: e.tensor_tensor(out=vn[:], in0=uu[:], in1=ps_ws[:, 0:128], op=ALU.subtract))
                yield
                ps_s0, ps_s0k = psring.next()
                mm(ps_s0[:, 0:128], ps_s0k, kda[:], vn[:], [kdak, vnk])
                ps_s1, ps_s1k = psring.next()
                mm(ps_s1[:, 0:128], ps_s1k, kdb[:], vn[:], [kdbk, vnk])
                if full:
                    ps_o, ps_ok = psring.next()
                    mm(ps_o[:, 0:128], ps_ok, vn[:], aqt[:], [vnk, aqtk], start=True, stop=False)
                    mm(ps_o[:, 0:64], ps_ok, S0h[:], qet[:, 0:64], [s0hk, qetk], start=False, stop=False)
                    mm(ps_o[:, 64:128], ps_ok, S1h[:], qet[:, 64:128], [s1hk, qetk], start=False, stop=True)
                for (ps_s, ps_sk, Sx, Sn, Shn, sck, snk, shnk, hh) in ((ps_s0, ps_s0k, S0, S0n, S0hn, s0ck, s0nk, s0hnk, h0),
                                                                       (ps_s1, ps_s1k, S1, S1n, S1hn, s1ck, s1nk, s1hnk, h1)):
                    ecol = bc["EGT"][:, ch, d * 8 + hh: d * 8 + hh + 1]
                    sch.op("dve", [ps_sk, sck, smk], [snk],
                           lambda e, Sx=Sx, Sn=Sn, ps_s=ps_s, ecol=ecol: e.scalar_tensor_tensor(
                               out=Sn[:], in0=Sx[:], scalar=ecol, in1=ps_s[:, 0:128], op0=ALU.mult, op1=ALU.add))
                    sch.op("pool", [snk], [shnk], lambda e, Sn=Sn, Shn=Shn: e.tensor_copy(out=Shn[:], in_=Sn[:]))
                chain_pos[(run["idx"], p, d)] = tseq + 1
                if full:
                    first_of_block = (ch == 0) if d == 0 else (ch == nb - 1)
                    last_of_block = (ch == nb - 1) if d == 0 else (ch == 0)
                    if first_of_block:
                        ostctx[d] = RB(f"ostd{d}", 2, (128, 2 * 512))
                    ost, ostk = ostctx[d]
                    ostv = ost[:].rearrange("p (h c) -> p h c", h=2)
                    sch.op("act", [ps_ok], [ostk],
                           lambda e: e.activation(out=ostv[:, :, cs], in_=ps_o[:, 0:128].rearrange("p (h c) -> p h c", h=2), func=AF.Copy))
                    if last_of_block:
                        OD = OF if d == 0 else OB
                        oo = run["ooff"] + blk * 512
                        ODv = OD.rearrange("(h d) c -> d h c", h=8)
                        sch.dma("pool", ODv[:, 2 * p:2 * p + 2, oo:oo + nb * 64], ostv[:, :, :nb * 64], [ostk], [])

            order = [r for r in runs if not r["full"]] + [runs[2], runs[0], runs[1]]
            for run in order:
                n, full, off, ri = run["n"], run["full"], run["off"], run["idx"]
                nch = n // 64
                dirs = [0, 1] if full else [0]
                for h in range(8):
                    for d in dirs:
                        cur = Scur[(h, d)]
                        Sb, Sh = Sst[(h, d)][cur], S16[(h, d)][cur]
                        sbk, shk = ("Sb", h, d, cur), ("Sh", h, d, cur)
                        if not full:
                            sch.op("dve", [sbk, "fl"], [sbk],
                                   lambda e, Sb=Sb: e.tensor_scalar(out=Sb[:], in0=Sb[:], scalar1=flag(ri, 2), scalar2=None, op0=ALU.mult))
                        elif run["name"] == "own":
                            src = SF[h] if d == 0 else SB[h]
                            sk = ("SF", h) if d == 0 else ("SB", h)
                            sch.op("pool", [sk], [sbk], lambda e, Sb=Sb, src=src: e.tensor_copy(out=Sb[:], in_=src[:]))
                        else:
                            sch.op("pool", [], [sbk], lambda e, Sb=Sb: e.memset(Sb[:], 0.0))
                        sch.op("pool", [sbk], [shk], lambda e, Sb=Sb, Sh=Sh: e.tensor_copy(out=Sh[:], in_=Sb[:]))
                for p in range(4):
                    ostctx = {}
                    active = []
                    bctx = {}
                    NST = 18
                    STAG = max(3, -(-NST * len(dirs) // WIN))
                    tnext = 0
                    since = STAG
                    while tnext < nch or active:
                        if tnext < nch and since >= STAG and len(active) + len(dirs) <= WIN:
                            since = 0
                            for d in dirs:
                                c = tnext if d == 0 else nch - 1 - tnext
                                blk = c // 8
                                nb = min(8, nch - blk * 8)
                                if bctx.get(d, (None,))[0] != blk:
                                    bctx[d] = (blk, load_block(run, p, d, blk, nb))
                                active.append(inst_gen(run, p, d, c, bctx[d][1], ostctx, tnext))
                            tnext += 1
                        since += 1
                        nxt = []
                        for g in active:
                            try:
                                next(g)
                                nxt.append(g)
                            except StopIteration:
                                pass
                        active = nxt
                if not full:
                    for h in range(8):
                        Sb = Sst[(h, 0)][Scur[(h, 0)]]
                        sck = ("Sb", h, 0, Scur[(h, 0)])
                        sch.op("dve", [sck, "fl", ("SF", h)], [("SF", h)],
                               lambda e, Sb=Sb, h=h: e.scalar_tensor_tensor(out=SF[h][:], in0=Sb[:], scalar=flag(ri, 3), in1=SF[h][:],
                                                                            op0=ALU.mult, op1=ALU.add))
                        sch.op("dve", [sck, "fl", ("SB", h)], [("SB", h)],
                               lambda e, Sb=Sb, h=h: e.scalar_tensor_tensor(out=SB[h][:], in0=Sb[:], scalar=flag(ri, 4), in1=SB[h][:],
                                                                            op0=ALU.mult, op1=ALU.add))
            sch.barrier()

        if "ATT" in stages:
          with contextlib.ExitStack() as st:
            NKV = max(cfg["ctx_n"])
            kt = sb("att_kt", [128, NKV], BF16, stack=st)
            vt = sb("att_vt", [128, NKV], BF16, stack=st)
            onesb = sb("att_ones", [128, 128], BF16, stack=st)
            sch.op("dve", [], ["onesb"], lambda e: e.memset(onesb[:], 1.0))
            qts = Ring([(sb(f"att_q{i}", [128, 512], BF16, stack=st), ("attq", i)) for i in range(2)])
            pts = Ring([(sb(f"att_p{i}", [128, 512], BF16, stack=st), ("attp", i)) for i in range(4)])
            recs = Ring([(sb(f"att_r{i}", [128, 512], stack=st), ("attr", i)) for i in range(2)])
            aos = Ring([(sb(f"att_o{i}", [128, 512], BF16, stack=st), ("atto", i)) for i in range(2)])
            po, pok = psb[0], ("ps", 0)
            pd, pdk = psb[1], ("ps", 1)
            ring2 = Ring([(psb[i], ("ps", i)) for i in range(2, 8)])
            scale = 128.0 ** -0.5
            for run in runs[:3]:
                n, c, oo = run["n"], run["ctx"], run["ooff"]
                nkv = cfg["ctx_n"][c]
                nkb = nkv // 128
                vtv = vt[:, :nkv].rearrange("p (kb d) -> p kb d", d=128)
                for g in range(2):
                    for k0 in range(0, nkv, 4096):
                        k1 = min(nkv, k0 + 4096)
                        sch.dma("sp", kt[:, k0:k1], KT[c][g * 128:(g + 1) * 128, k0:k1], [], ["kt"])
                    VVr = VV[c].rearrange("(kb t) f -> t kb f", t=128)
                    for b0 in range(0, nkb, 8):
                        b1 = min(nkb, b0 + 8)
                        sch.dma("sp", vtv[:, b0:b1, :], VVr[:, b0:b1, g * 128:(g + 1) * 128], [], ["vt"])
                    for hq in range(4 * g, 4 * g + 4):
                        for (t0, w) in tiles_of(n):
                            qt, qk_ = qts.next()
                            sch.dma("sp", qt[:, :w], QT[hq * 128:(hq + 1) * 128, oo + t0:oo + t0 + w], [], [qk_])
                            for kb in range(nkb):
                                ps_s, ps_sk = ring2.next()
                                sch.op("pe", ["kt", qk_], [ps_sk],
                                       lambda e: e.matmul(ps_s[:, :w], lhsT=kt[:, kb * 128:(kb + 1) * 128], rhs=qt[:, :w],
                                                          start=True, stop=True))
                                pt, ptk = pts.next()
                                sch.op("act", [ps_sk], [ptk],
                                       lambda e: e.activation(out=pt[:, :w], in_=ps_s[:, :w], func=AF.Exp, scale=scale))
                                sch.op("pe", ["vt", ptk], [pok],
                                       lambda e: e.matmul(po[:, :w], lhsT=vtv[:, kb, :], rhs=pt[:, :w],
                                                          start=(kb == 0), stop=(kb == nkb - 1)))
                                sch.op("pe", ["onesb", ptk], [pdk],
                                       lambda e: e.matmul(pd[:, :w], lhsT=onesb[:], rhs=pt[:, :w],
                                                          start=(kb == 0), stop=(kb == nkb - 1)))
                            rec, reck = recs.next()
                            sch.op("dve", [pdk], [reck], lambda e: e.reciprocal(out=rec[:, :w], in_=pd[:, :w]))
                            ao, aok = aos.next()
                            sch.op("dve", [pok, reck], [aok],
                                   lambda e: e.tensor_tensor(out=ao[:, :w], in0=po[:, :w], in1=rec[:, :w], op=ALU.mult))
                            sch.dma("pool", ATT[hq * 128:(hq + 1) * 128, oo + t0:oo + t0 + w], ao[:, :w], [aok], [])
            sch.barrier()

        def load_w_bf16(dst, W, nk, ncols, key):
            Wv = W.rearrange("(kc p) c -> p kc c", p=128)
            dv = dst[:].rearrange("p (k c) -> p k c", c=ncols)
            for kc in range(nk):
                for c0 in range(0, ncols, 2048):
                    c1 = min(ncols, c0 + 2048)
                    sch.dma("pool", dv[:, kc, c0:c1], Wv[:, kc, c0:c1], [], [key])
            return dv

        own_tiles = []
        for run in runs[:3]:
            for (t0, w) in tiles_of(run["n"]):
                own_tiles.append((run, t0, w))

        if "C1" in stages:
          with contextlib.ExitStack() as st:
            wpa = load_w_bf16(sb("wpa", [128, 8 * D], BF16, stack=st), W_PA, 8, D, "wpa")
            wpb = load_w_bf16(sb("wpb", [128, 8 * D], BF16, stack=st), W_PB, 8, D, "wpb")
            wo = load_w_bf16(sb("wo", [128, 8 * D], BF16, stack=st), W_OUT, 8, D, "wo")
            dnw = sb("dnw", [128, 1], stack=st)
            sch.dma("sp", dnw[:], DNW[:, :], [], ["dnw"])
            f32r = Ring([(sb(f"c1f{i}", [128, 512], stack=st), ("c1f", i)) for i in range(10)])
            attb = sb("c1att", [128, 8 * 512], BF16, stack=st)
            attv = attb[:].rearrange("p (k t) -> p k t", t=512)
            ogb = sb("c1og", [128, 8 * 512], BF16, stack=st)
            ogv = ogb[:].rearrange("p (k t) -> p k t", t=512)
            mb = sb("c1m", [128, 8 * 512], BF16, stack=st)
            mv = mb[:].rearrange("p (k t) -> p k t", t=512)
            XTv2 = XT.rearrange("(kc p) t -> p kc t", p=128)
            for (run, t0, w) in own_tiles:
                s_, oo, off = run["seq"], run["ooff"] + t0, run["off"] + 2 + t0
                ATTv = ATT.rearrange("(kc p) t -> p kc t", p=128)
                sch.dma("sp", attv[:, :, :w], ATTv[:, :, oo:oo + w], [], ["c1att"])
                for h in range(8):
                    a_, ak_ = f32r.next()
                    b_, bk_ = f32r.next()
                    z_, zk_ = f32r.next()
                    sch.dma("sp", a_[:, :w], OF[h * 128:(h + 1) * 128, oo:oo + w], [], [ak_])
                    sch.dma("sp", b_[:, :w], OB[h * 128:(h + 1) * 128, oo:oo + w], [], [bk_])
                    sch.dma("sp", z_[:, :w], ZS[h * 128:(h + 1) * 128, oo:oo + w], [], [zk_])
                    sch.op("pool", [ak_, bk_], [ak_], lambda e: e.tensor_tensor(out=a_[:, :w], in0=a_[:, :w], in1=b_[:, :w], op=ALU.add))
                    sch.op("act", [ak_], [bk_], lambda e: e.activation(out=b_[:, :w], in_=a_[:, :w], func=AF.Square))
                    ps, psk = psring.next()
                    sch.op("pe", [bk_, "consts"], [psk], lambda e: e.matmul(ps[:, :w], lhsT=ones, rhs=b_[:, :w], start=True, stop=True))
                    t_, tk_ = f32r.next()
                    rstd_from_ps(ps, psk, w, 1.0 / 128, b_, bk_, t_, tk_)
                    sch.op("dve", [ak_, bk_, "dnw"], [ak_],
                           lambda e: e.scalar_tensor_tensor(out=a_[:, :w], in0=a_[:, :w], scalar=dnw[:, 0:1], in1=b_[:, :w],
                                                            op0=ALU.mult, op1=ALU.mult))
                    sch.op("dve", [ak_, zk_], [("c1og", h)],
                           lambda e: e.tensor_tensor(out=ogv[:, h, :w], in0=a_[:, :w], in1=z_[:, :w], op=ALU.mult))
                for fc in range(8):
                    ga_, gak = f32r.next()
                    gb_, gbk = f32r.next()
                    sch.dma("sp", ga_[:, :w], GATES[fc * 128:(fc + 1) * 128, oo:oo + w], [], [gak])
                    sch.dma("sp", gb_[:, :w], GATES[D + fc * 128:D + (fc + 1) * 128, oo:oo + w], [], [gbk])
                    psa, psak = psring.next()
                    for kc in range(8):
                        sch.op("pe", ["wpa", "c1att"], [psak],
                               lambda e, kc=kc: e.matmul(psa[:, :w], lhsT=wpa[:, kc, fc * 128:(fc + 1) * 128], rhs=attv[:, kc, :w],
                                                         start=(kc == 0), stop=(kc == 7)))
                    psb_, psbk = psring.next()
                    for kc in range(8):
                        sch.op("pe", ["wpb", ("c1og", kc)], [psbk],
                               lambda e, kc=kc: e.matmul(psb_[:, :w], lhsT=wpb[:, kc, fc * 128:(fc + 1) * 128], rhs=ogv[:, kc, :w],
                                                         start=(kc == 0), stop=(kc == 7)))
                    sch.op("dve", [psak, gak], [gak],
                           lambda e: e.tensor_tensor(out=ga_[:, :w], in0=ga_[:, :w], in1=psa[:, :w], op=ALU.mult))
                    sch.op("dve", [psbk, gbk], [gbk],
                           lambda e: e.tensor_tensor(out=gb_[:, :w], in0=gb_[:, :w], in1=psb_[:, :w], op=ALU.mult))
                    sch.op("pool", [gak, gbk], [("c1m", fc)],
                           lambda e: e.tensor_tensor(out=mv[:, fc, :w], in0=ga_[:, :w], in1=gb_[:, :w], op=ALU.add))
                for fc in range(8):
                    x_, xk_ = f32r.next()
                    sch.dma("sp", x_[:, :w], XTv2[:, fc, off:off + w], [], [xk_])
                    ps, psk = psring.next()
                    for kc in range(8):
                        sch.op("pe", ["wo", ("c1m", kc)], [psk],
                               lambda e, kc=kc: e.matmul(ps[:, :w], lhsT=wo[:, kc, fc * 128:(fc + 1) * 128], rhs=mv[:, kc, :w],
                                                         start=(kc == 0), stop=(kc == 7)))
                    sch.op("dve", [psk, xk_, "modt"], [xk_],
                           lambda e: e.scalar_tensor_tensor(out=x_[:, :w], in0=ps[:, :w], scalar=modv[:, 16 + fc, s_:s_ + 1],
                                                            in1=x_[:, :w], op0=ALU.mult, op1=ALU.add))
                    sch.dma("pool", X1[fc * 128:(fc + 1) * 128, oo:oo + w], x_[:, :w], [xk_], [])
            sch.barrier()

        if "C2" in stages:
          with contextlib.ExitStack() as st:
            wf1 = load_w_bf16(sb("wf1", [128, 8 * 2 * DFF], BF16, stack=st), W_F1, 8, 2 * DFF, "wf1")
            wf2 = load_w_bf16(sb("wf2", [128, 22 * D], BF16, stack=st), W_F2, 22, D, "wf2")
            fnw = sb("fnw", [128, 8], stack=st)
            sch.dma("sp", fnw[:], FNWT[:, :], [], ["fnw"])
            x1b = sb("c2x1", [128, 8 * 512], stack=st)
            x1v = x1b[:].rearrange("p (k t) -> p k t", t=512)
            h2b = sb("c2h2", [128, 8 * 512], BF16, stack=st)
            h2v = h2b[:].rearrange("p (k t) -> p k t", t=512)
            acb = sb("c2ac", [128, 22 * 512], BF16, stack=st)
            acv = acb[:].rearrange("p (k t) -> p k t", t=512)
            f32r = Ring([(sb(f"c2f{i}", [128, 512], stack=st), ("c2f", i)) for i in range(6)])
            rsA = sb("c2rsA", [128, 512], stack=st)
            rsB = sb("c2rsB", [128, 512], stack=st)
            X1v = X1.rearrange("(kc p) t -> p kc t", p=128)
            for (run, t0, w) in own_tiles:
                s_, oo = run["seq"], run["ooff"] + t0
                sch.dma("sp", x1v[:, :, :w], X1v[:, :, oo:oo + w], [], ["c2x1"])
                psS, pskS = psring.next()
                for kc in range(8):
                    q_, qk2 = f32r.next()
                    sch.op("act", ["c2x1"], [qk2], lambda e, kc=kc, q_=q_: e.activation(out=q_[:, :w], in_=x1v[:, kc, :w], func=AF.Square))
                    sch.op("pe", [qk2, "consts"], [pskS],
                           lambda e, kc=kc, q_=q_: e.matmul(psS[:, :w], lhsT=ones, rhs=q_[:, :w], start=(kc == 0), stop=(kc == 7)))
                rs_, rsk = rsA, "c2rsA"
                t_, tk_ = f32r.next()
                rstd_from_ps(psS, pskS, w, 1.0 / D, rs_, rsk, t_, tk_)
                for kc in range(8):
                    t_, tk_ = f32r.next()
                    sch.op("dve", ["c2x1", rsk], [tk_],
                           lambda e, kc=kc, t_=t_: e.tensor_tensor(out=t_[:, :w], in0=x1v[:, kc, :w], in1=rs_[:, :w], op=ALU.mult))
                    sch.op("act", [tk_, "a2", "modt"], [("c2h2", kc)],
                           lambda e, kc=kc, t_=t_: e.activation(out=h2v[:, kc, :w], in_=t_[:, :w], func=AF.Identity,
                                                                scale=a2v[:, kc, s_:s_ + 1], bias=modv[:, 24 + kc, s_:s_ + 1]))
                for j in range(22):
                    psu, psuk = psring.next()
                    for kc in range(8):
                        sch.op("pe", ["wf1", ("c2h2", kc)], [psuk],
                               lambda e, kc=kc: e.matmul(psu[:, :w], lhsT=wf1[:, kc, j * 128:(j + 1) * 128], rhs=h2v[:, kc, :w],
                                                         start=(kc == 0), stop=(kc == 7)))
                    psv_, psvk = psring.next()
                    for kc in range(8):
                        sch.op("pe", ["wf1", ("c2h2", kc)], [psvk],
                               lambda e, kc=kc: e.matmul(psv_[:, :w], lhsT=wf1[:, kc, DFF + j * 128:DFF + (j + 1) * 128],
                                                         rhs=h2v[:, kc, :w], start=(kc == 0), stop=(kc == 7)))
                    su, suk = f32r.next()
                    sch.op("act", [psuk], [suk], lambda e: e.activation(out=su[:, :w], in_=psu[:, :w], func=AF.Silu))
                    sch.op("dve", [suk, psvk], [("c2ac", j)],
                           lambda e: e.tensor_tensor(out=acv[:, j, :w], in0=su[:, :w], in1=psv_[:, :w], op=ALU.mult))
                for fc in range(8):
                    ps, psk = psring.next()
                    for j in range(22):
                        sch.op("pe", ["wf2", ("c2ac", j)], [psk],
                               lambda e, j=j: e.matmul(ps[:, :w], lhsT=wf2[:, j, fc * 128:(fc + 1) * 128], rhs=acv[:, j, :w],
                                                       start=(j == 0), stop=(j == 21)))
                    sch.op("dve", [psk, "c2x1", "modt"], ["c2x1"],
                           lambda e: e.scalar_tensor_tensor(out=x1v[:, fc, :w], in0=ps[:, :w], scalar=modv[:, 40 + fc, s_:s_ + 1],
                                                            in1=x1v[:, fc, :w], op0=ALU.mult, op1=ALU.add))
                psS2, pskS2 = psring.next()
                for fc in range(8):
                    q_, qk2 = f32r.next()
                    sch.op("act", ["c2x1"], [qk2], lambda e, q_=q_: e.activation(out=q_[:, :w], in_=x1v[:, fc, :w], func=AF.Square))
                    sch.op("pe", [qk2, "consts"], [pskS2],
                           lambda e, q_=q_: e.matmul(psS2[:, :w], lhsT=ones, rhs=q_[:, :w], start=(fc == 0), stop=(fc == 7)))
                rs_, rsk = rsB, "c2rsB"
                t_, tk_ = f32r.next()
                rstd_from_ps(psS2, pskS2, w, 1.0 / D, rs_, rsk, t_, tk_)
                for fc in range(8):
                    y_, yk_ = f32r.next()
                    sch.op("dve", ["c2x1", rsk, "fnw"], [yk_],
                           lambda e: e.scalar_tensor_tensor(out=y_[:, :w], in0=x1v[:, fc, :w], scalar=fnw[:, fc:fc + 1], in1=rs_[:, :w],
                                                            op0=ALU.mult, op1=ALU.mult))
                    sch.dma("pool", YT[fc * 128:(fc + 1) * 128, oo:oo + w], y_[:, :w], [yk_], [])
            sch.barrier()

        sch.barrier()
    nc._ninstr = sch.ninstr
    nc._needed = sch.needed
    nc._nincs = sum(len(v) for v in sch.map.values())
    nc._nops = dict(sch.idx)
    nc._declared = declared
    return nc


def rope_tables_T(pos):
    half = 64
    inv = 1.0 / (10000.0 ** (np.arange(0, half, 2, dtype=np.float32) / half))
    row = (pos // 64).astype(np.float32)
    col = (pos % 64).astype(np.float32)
    ar = row[:, None] * inv[None, :]
    ac = col[:, None] * inv[None, :]
    ang = np.concatenate([ar, ar, ac, ac], axis=-1).astype(np.float32)
    return np.cos(ang).T.astype(np.float32), np.sin(ang).T.astype(np.float32)


def make_consts():
    c = np.zeros((128, 10 * 128), np.float32)
    c[:, 0:128] = np.eye(128, dtype=np.float32)
    c[:, 128:256] = 1.0
    R = np.zeros((128, 128), np.float32)
    for a in range(2):
        for cc in range(32):
            R[a * 64 + 32 + cc, a * 64 + cc] = -1.0
            R[a * 64 + cc, a * 64 + 32 + cc] = 1.0
    c[:, 256:384] = R
    i = np.arange(64)
    low_strict = (i[:, None] > i[None, :]).astype(np.float32)
    low_inc = (i[:, None] >= i[None, :]).astype(np.float32)

    def bd(m):
        z = np.zeros((128, 128), np.float32)
        z[:64, :64] = m
        z[64:, 64:] = m
        return z
    c[:, 384:512] = bd(low_strict)
    c[:, 512:640] = bd(low_strict.T)
    c[:, 640:768] = bd(low_inc.T)
    c[:, 768:896] = bd(low_inc)
    c[:, 896:1024] = bd(low_inc.T)
    c[:, 1024:1152] = bd(low_inc)
    c[:, 1152:1280] = bd(np.ones((64, 64), np.float32))
    return c


def host_inputs(cfg, core, x_prompt, x_sample, c_prompt, c_sample, norm1_w, norm2_w, w_ada, b_ada, w_in,
                q_norm_w, k_norm_w, conv_w, a_log, dt_bias, dn_norm_w, w_proj_a, w_proj_b, w_out,
                w_ffn_in, w_ffn_out, final_norm_w, shared):
    SP, SEG = cfg["SP"], cfg["SEG"]
    runs = cfg["runs"]
    NTOT = cfg["NTOT"]
    NR = len(runs)
    XTm = np.zeros((D, NTOT), np.float32)
    pos = np.zeros((NTOT,), np.int64)
    FLm = np.zeros((NR, 8), np.float32)
    cwT = np.ascontiguousarray(conv_w[0].T.reshape(24, 128, 5).transpose(1, 0, 2))
    CWm = np.zeros((128, NR, 24, 5), np.float32)
    xs = x_sample[0]
    S_S = xs.shape[0]
    i = core
    for run in runs:
        n, off, r = run["n"], run["off"], run["idx"]
        if run["name"] in ("pA", "pB"):
            xx = x_prompt[2 * core + (0 if run["name"] == "pA" else 1)]
            XTm[:, off + 2: off + 2 + n] = xx.T
            pos[off + 2: off + 2 + n] = np.arange(n)
            CWm[:, r] = cwT
        elif run["name"] == "own":
            a, b = i * SEG, (i + 1) * SEG
            lo, hi = max(a - 2, 0), min(b + 2, S_S)
            XTm[:, off + 2 - (a - lo): off + 2 + n + (hi - b)] = xs[lo:hi].T
            pos[off + 2: off + 2 + n] = np.arange(a, b)
            FLm[r, 0] = 1.0 if a > 0 else 0.0
            FLm[r, 1] = 1.0 if b < S_S else 0.0
            CWm[:, r] = cwT
        else:
            k = r - 3
            if k < i:
                seg = k
                a, b = seg * SEG, (seg + 1) * SEG
                idx = np.arange(a - 2, b + 2)
                fwd = True
            else:
                seg = 7 - (k - i)
                a, b = seg * SEG, (seg + 1) * SEG
                idx = np.arange(b + 1, a - 3, -1)
                fwd = False
            valid = (idx >= 0) & (idx < S_S)
            cols = off + np.arange(n + 4)
            XTm[:, cols[valid]] = xs[idx[valid]].T
            pos[cols[valid]] = idx[valid]
            FLm[r, 0] = 1.0 if valid[0] else 0.0
            FLm[r, 1] = 1.0 if valid[-1] else 0.0
            FLm[r, 2] = 0.0 if (k == 0 or k == i) else 1.0
            FLm[r, 3] = 1.0 if (k == i - 1) else 0.0
            FLm[r, 4] = 1.0 if (k == 6 and i <= 6) else 0.0
            FLm[r, 5] = 1.0 if fwd else 0.0
            FLm[r, 6] = 0.0 if fwd else 1.0
            CWm[:, r] = cwT if fwd else cwT[:, :, ::-1]
    cosT, sinT = rope_tables_T(pos)
    cseq = np.stack([c_prompt[2 * core], c_prompt[2 * core + 1], c_sample[0]], axis=0)
    CTm = np.ascontiguousarray(cseq.reshape(3, 8, 128).transpose(2, 1, 0)).reshape(128, 24)
    m = dict(shared)
    m.update({
        "XT": XTm, "COS": np.ascontiguousarray(cosT), "SIN": np.ascontiguousarray(sinT),
        "FL": np.ascontiguousarray(np.broadcast_to(FLm.reshape(1, NR * 8), (128, NR * 8))),
        "CW": np.ascontiguousarray(CWm.reshape(128, NR * 24 * 5)),
        "CT": CTm,
    })
    return m


def host_shared(norm1_w, norm2_w, w_ada, b_ada, w_in, q_norm_w, k_norm_w, a_log, dt_bias, dn_norm_w,
                w_proj_a, w_proj_b, w_out, w_ffn_in, w_ffn_out, final_norm_w):
    def vT(v, nch):
        return np.ascontiguousarray(np.asarray(v, np.float32).reshape(nch, 128).T)
    rep = lambda v: np.ascontiguousarray(np.broadcast_to(np.asarray(v, np.float32).reshape(1, -1), (128, v.size)))
    return {
        "w_ada": np.ascontiguousarray(w_ada[0]), "b_adaT": vT(b_ada[0], 48),
        "n1wT": vT(norm1_w[0], 8), "n2wT": vT(norm2_w[0], 8), "fnwT": vT(final_norm_w, 8),
        "w_in": np.ascontiguousarray(w_in[0]),
        "qknw": np.ascontiguousarray(np.stack([q_norm_w[0], k_norm_w[0]], axis=1).astype(np.float32)),
        "dnw": np.ascontiguousarray(dn_norm_w[0].reshape(128, 1).astype(np.float32)),
        "alog_rep": rep(a_log[0]), "dtb_rep": rep(dt_bias[0]),
        "w_proj_a": np.ascontiguousarray(w_proj_a[0]), "w_proj_b": np.ascontiguousarray(w_proj_b[0]),
        "w_out": np.ascontiguousarray(w_out[0]), "w_ffn_in": np.ascontiguousarray(w_ffn_in[0]),
        "w_ffn_out": np.ascontiguousarray(w_ffn_out[0]),
        "consts": make_consts(),
    }


def run_kernel(cfg, inputs, debug=False, stages=("A", "CONV", "DN", "ATT", "C1", "C2"), trace=False):
    inputs = {k: np.asarray(v) for k, v in inputs.items()}
    nc = build(cfg, debug=debug, stages=stages)
    wkeys = ["norm1_w", "norm2_w", "w_ada", "b_ada", "w_in", "q_norm_w", "k_norm_w", "a_log", "dt_bias",
             "dn_norm_w", "w_proj_a", "w_proj_b", "w_out", "w_ffn_in", "w_ffn_out", "final_norm_w"]
    shared = host_shared(**{k: inputs[k] for k in wkeys})
    in_maps = [host_inputs(cfg, c, shared=shared, **inputs) for c in range(NCORES)]
    in_maps = [{k: v for k, v in m.items() if k in nc._declared} for m in in_maps]
    res = run_bass_kernel_spmd(nc, in_maps, core_ids=list(range(NCORES)), trace=trace)
    return res, nc


def kernel(**inputs):
    cfg = make_cfg()
    res, _ = run_kernel(cfg, inputs)
    SP, SEG = cfg["SP"], cfg["SEG"]
    B = inputs["x_prompt"].shape[0]
    y_prompt = np.zeros((B, SP, D), np.float32)
    y_sample = np.zeros((1, 8 * SEG, D), np.float32)
    for c in range(NCORES):
        yt = np.asarray(res.results[c]["YT"])
        y_prompt[2 * c] = yt[:, 0:SP].T
        y_prompt[2 * c + 1] = yt[:, SP:2 * SP].T
        y_sample[0, c * SEG:(c + 1) * SEG] = yt[:, 2 * SP:2 * SP + SEG].T
    return (y_prompt, y_sample)
```

```python
import contextlib
import os
import numpy as np
import concourse.bass as bass
import concourse.mybir as mybir
from concourse.bass_utils import run_bass_kernel_spmd

F32 = mybir.dt.float32
BF16 = mybir.dt.bfloat16
AF = mybir.ActivationFunctionType
ALU = mybir.AluOpType

D = 1024
NKC = 8
N_IN = 7712
DFF = 2816
EPS = 1e-6
NCORES = 8
EPOCH = 30000
ATTACH_WAITS = os.environ.get('ATTACH_WAITS', '0') == '1'


class Sched:
    def __init__(self, nc, es, n_dma_sems=32, needed=None):
        self.nc = nc
        self.es = es
        self.eng = {"pe": nc.tensor, "act": nc.scalar, "dve": nc.vector, "pool": nc.gpsimd, "sp": nc.sync}
        self.sems = {}
        self.idx = {e: 0 for e in ["pe", "act", "dve", "pool"]}
        self.cnt = {e: 0 for e in self.idx}
        self.epoch = {e: 0 for e in self.idx}
        self.map = {e: {} for e in self.idx}
        self.needed_in = needed
        self.needed = {e: set() for e in self.idx}
        self.dsem = [es.enter_context(nc.semaphore(f"dq{i}")) for i in range(n_dma_sems)]
        self.dcnt = [0] * n_dma_sems
        self.dnext = 0
        self.waited = {}
        self.lastw = {}
        self.rd = {}
        self.ninstr = 0
        self.pending = None

    def _sem(self, key):
        if key[0] == "d":
            return self.dsem[key[1]]
        if key not in self.sems:
            self.sems[key] = self.es.enter_context(self.nc.semaphore(f"s_{key[0]}_{key[1]}"))
        return self.sems[key]

    def _resolve(self, key, val):
        if key[0] == "d":
            return self.dsem[key[1]], val
        f = key[1]
        self.needed[f].add(val)
        if self.needed_in is not None:
            assert val in self.needed_in[f], "two-pass mismatch"
        ep, c = self.map[f][val]
        return self._sem((f, ep)), c

    def _emit_wait(self, e, key, val):
        if self.pending is not None:
            self.pending.append((key, val))
            return
        sem, v = self._resolve(key, val)
        self.eng[e].wait_ge(sem, v)
        self.ninstr += 1

    def _wait(self, e, ev):
        if ev is None:
            return
        key, val = ev
        if e == "pe" and key == ("c", "pe"):
            return
        k = (e, key)
        if self.waited.get(k, 0) >= val:
            return
        self.waited[k] = val
        self._emit_wait(e, key, val)

    def _deps(self, e, reads, writes):
        for r in reads:
            self._wait(e, self.lastw.get(r))
            if isinstance(r, tuple) and r[0] == "ps":
                for key, val in self.rd.get(r, {}).items():
                    if key != ("c", e):
                        self._wait(e, (key, val))
        for w in writes:
            self._wait(e, self.lastw.get(w))
            for key, val in self.rd.get(w, {}).items():
                self._wait(e, (key, val))

    def _record(self, ev, reads, writes):
        key, val = ev
        for r in reads:
            d = self.rd.setdefault(r, {})
            if d.get(key, 0) < val:
                d[key] = val
        for w in writes:
            self.lastw[w] = ev
            self.rd[w] = {}

    def _with_waits(self, e, reads, writes, fn, pre=None):
        self.pending = [] if ATTACH_WAITS else None
        if pre is not None:
            self._wait(e, pre)
        self._deps(e, reads, writes)
        pend, self.pending = self.pending, None
        if pend:
            for key, val in pend[:-1]:
                self._emit_wait(e, key, val)
        ins = fn(self.eng[e])
        if pend:
            sem, v = self._resolve(*pend[-1])
            ins._wait_ge(sem, v)
        return ins

    def op(self, e, reads, writes, fn):
        ins = self._with_waits(e, reads, writes, fn)
        self.idx[e] += 1
        i = self.idx[e]
        if self.needed_in is None or i in self.needed_in[e]:
            if self.cnt[e] >= EPOCH:
                self.epoch[e] += 1
                self.cnt[e] = 0
            self.cnt[e] += 1
            ins.then_inc(self._sem((e, self.epoch[e])), 1)
            self.map[e][i] = (self.epoch[e], self.cnt[e])
        self.ninstr += 1
        self._record((("c", e), i), reads, writes)

    def dma(self, q, out, in_, reads, writes):
        i = self.dnext
        self.dnext = (self.dnext + 1) % len(self.dsem)
        pre = (("d", i), self.dcnt[i]) if self.dcnt[i] > 0 else None
        ins = self._with_waits(q, reads, writes, lambda eng: eng.dma_start(out=out, in_=in_), pre=pre)
        self.dcnt[i] += 16
        assert self.dcnt[i] < 60000
        ins.then_inc(self.dsem[i], 16)
        self.ninstr += 1
        self._record((("d", i), self.dcnt[i]), reads, writes)

    def barrier(self):
        evs = []
        for f in self.idx:
            if self.idx[f] > 0:
                evs.append((("c", f), self.idx[f]))
        for i, v in enumerate(self.dcnt):
            if v > 0:
                evs.append((("d", i), v))
        for e in ["pe", "act", "dve", "pool", "sp"]:
            for key, val in evs:
                if key == ("c", e):
                    continue
                k = (e, key)
                if self.waited.get(k, 0) >= val:
                    continue
                self.waited[k] = val
                self._emit_wait(e, key, val)
        self.lastw = {}
        self.rd = {}


class Ring:
    def __init__(self, items):
        self.items = items
        self.i = 0

    def next(self):
        it = self.items[self.i]
        self.i = (self.i + 1) % len(self.items)
        return it


def make_cfg(SP=4096, SEG=2048):
    cfg = dict(SP=SP, SEG=SEG)
    runs = []
    off = 0
    for name, n, full, seq in [("pA", SP, True, 0), ("pB", SP, True, 1), ("own", SEG, True, 2)] + [
        (f"o{r}", SEG, False, 2) for r in range(7)
    ]:
        runs.append(dict(name=name, n=n, full=full, seq=seq, off=off, idx=len(runs)))
        off += n + 4
    cfg["runs"] = runs
    cfg["NTOT"] = off
    cfg["NOWN"] = 2 * SP + SEG
    o = 0
    for r in runs[:3]:
        r["ooff"] = o
        o += r["n"]
    cfg["ctx_n"] = [SP, SP, 8 * SEG]
    runs[0]["ctx"], runs[0]["coff"] = 0, 0
    runs[1]["ctx"], runs[1]["coff"] = 1, 0
    runs[2]["ctx"], runs[2]["coff"] = 2, 0
    for r in range(7):
        runs[3 + r]["ctx"], runs[3 + r]["coff"] = 2, (1 + r) * SEG
    return cfg


def tiles_of(n, w=512):
    return [(t0, min(w, n - t0)) for t0 in range(0, n, w)]


def build(cfg, debug=False, stages=("A", "CONV", "DN", "ATT", "C1", "C2"), needed=None, two_pass=True):
    if two_pass and needed is None:
        dry = build(cfg, debug=debug, stages=stages, two_pass=False)
        needed = dry._needed
    nc = bass.Bass("TRN2", target_bir_lowering=False)
    runs = cfg["runs"]
    NTOT = cfg["NTOT"]
    NOWN = cfg["NOWN"]
    NR = len(runs)
    okind = "ExternalOutput" if debug else "Internal"

    declared = set()

    def din(name, shape, dt=F32, need=True):
        if not need:
            return None
        declared.add(name)
        return nc.dram_tensor(name, list(shape), dt, kind="ExternalInput").ap()

    dbg_names = os.environ.get("DBG_OUT", "").split(",") if debug else []

    def dscr(name, shape, dt=F32):
        kind = "ExternalOutput" if (debug and (name in dbg_names or dbg_names == ["all"])) else "Internal"
        return nc.dram_tensor(name, list(shape), dt, kind=kind).ap()

    XT = din("XT", [D, NTOT], need=("A" in stages or "C1" in stages))
    COS = din("COS", [128, NTOT])
    SIN = din("SIN", [128, NTOT])
    FL = din("FL", [128, NR * 8])
    CW = din("CW", [128, NR * 24 * 5])
    CT = din("CT", [128, 8 * 3])
    W_ADA = din("w_ada", [D, 6 * D], need="S0" in stages or "A" in stages or "C1" in stages or "C2" in stages)
    B_ADAT = din("b_adaT", [128, 48])
    N1WT = din("n1wT", [128, 8])
    N2WT = din("n2wT", [128, 8])
    FNWT = din("fnwT", [128, 8])
    W_IN = din("w_in", [D, N_IN], need="A" in stages)
    QKNW = din("qknw", [128, 2])
    DNW = din("dnw", [128, 1])
    ALOG = din("alog_rep", [128, 16])
    DTB = din("dtb_rep", [128, 16])
    W_PA = din("w_proj_a", [D, D], need="C1" in stages)
    W_PB = din("w_proj_b", [D, D], need="C1" in stages)
    W_OUT = din("w_out", [D, D], need="C1" in stages)
    W_F1 = din("w_ffn_in", [D, 2 * DFF], need="C2" in stages)
    W_F2 = din("w_ffn_out", [DFF, D], need="C2" in stages)
    CONSTS = din("consts", [128, 10 * 128])
    YT = nc.dram_tensor("YT", [D, NOWN], F32, kind="ExternalOutput").ap()
    QT = dscr("QT", [8 * 128, NOWN], BF16)
    KT = [dscr(f"KT{c}", [2 * 128, cfg["ctx_n"][c]], BF16) for c in range(3)]
    VV = [dscr(f"VV{c}", [cfg["ctx_n"][c], 256], BF16) for c in range(3)]
    DPRE3 = [dscr(f"DPRE{i}", [1024, NTOT]) for i in range(3)]
    ZS = dscr("ZS", [D, NOWN])
    GATES = dscr("GATES", [2 * D, NOWN])
    BG = dscr("BG", [NTOT, 33])
    DQKV3 = [dscr(f"DQKV{i}", [1024, NTOT]) for i in range(3)]
    OF = dscr("OF", [D, NOWN])
    OB = dscr("OB", [D, NOWN])
    ATT = dscr("ATT", [D, NOWN], BF16)
    X1 = dscr("X1", [D, NOWN])

    with contextlib.ExitStack() as es:
        sch = Sched(nc, es, needed=needed)

        def sb(name, shape, dt=F32, stack=es):
            return stack.enter_context(nc.sbuf_tensor("sb_" + name, list(shape), dt))

        psb = [es.enter_context(nc.psum_tensor(f"ps{i}", [128, 512], F32)) for i in range(8)]
        psring = Ring([(psb[i], ("ps", i)) for i in range(8)])

        consts = sb("consts", [128, 10 * 128])
        sch.dma("sp", consts[:], CONSTS[:, :], [], ["consts"])
        ident = consts[:, 0:128]
        ones = consts[:, 128:256]
        rrot = consts[:, 256:384]
        fl = sb("fl", [128, NR * 8])
        sch.dma("sp", fl[:], FL[:, :], [], ["fl"])
        modt = sb("modt", [128, 48 * 3])
        a1 = sb("a1", [128, 8 * 3])
        a2 = sb("a2", [128, 8 * 3])
        modv = modt[:].rearrange("p (f s) -> p f s", s=3)
        a1v = a1[:].rearrange("p (k s) -> p k s", s=3)
        a2v = a2[:].rearrange("p (k s) -> p k s", s=3)

        def flag(r, j):
            return fl[:, r * 8 + j: r * 8 + j + 1]

        with contextlib.ExitStack() as st:
          if W_ADA is not None:
              ct = sb("ct", [128, 24], stack=st)
              sct = sb("sct", [128, 24], stack=st)
              badat = sb("badat", [128, 48], stack=st)
              n1wt = sb("n1wt", [128, 8], stack=st)
              n2wt = sb("n2wt", [128, 8], stack=st)
              wab = [sb(f"wab{i}", [128, 8 * 768], stack=st) for i in range(2)]
              sch.dma("sp", ct[:], CT[:, :], [], ["ct"])
              sch.dma("sp", badat[:], B_ADAT[:, :], [], ["badat"])
              sch.dma("sp", n1wt[:], N1WT[:, :], [], ["n1wt"])
              sch.dma("sp", n2wt[:], N2WT[:, :], [], ["n2wt"])
              sch.op("act", ["ct"], ["sct"], lambda e: e.activation(out=sct[:], in_=ct[:], func=AF.Silu))
              sctv = sct[:].rearrange("p (k s) -> p k s", s=3)
              ps, psk = psring.next()
              W_ADAv = W_ADA.rearrange("(kc p) c -> p kc c", p=128)
              for blk in range(8):
                  wa = wab[blk % 2]
                  wav = wa[:].rearrange("p (k c) -> p k c", c=768)
                  sch.dma("sp", wav, W_ADAv[:, :, blk * 768:(blk + 1) * 768], [], [("wab", blk % 2)])
                  for f6 in range(6):
                      fc = blk * 6 + f6
                      for kc in range(8):
                          sch.op("pe", [("wab", blk % 2), "sct"], [psk],
                                 lambda e, fc=fc, kc=kc, f6=f6, wav=wav: e.matmul(
                                     ps[:, fc * 3:(fc + 1) * 3], lhsT=wav[:, kc, f6 * 128:(f6 + 1) * 128],
                                     rhs=sctv[:, kc, :], start=(kc == 0), stop=(kc == 7)))
              psv = ps[:, 0:144].rearrange("p (f s) -> p f s", s=3)
              for s in range(3):
                  sch.op("dve", [psk, "badat"], ["modt"],
                         lambda e, s=s: e.tensor_tensor(out=modv[:, :, s], in0=psv[:, :, s], in1=badat[:], op=ALU.add))
              for s in range(3):
                  sch.op("dve", ["modt", "n1wt"], ["a1"],
                         lambda e, s=s: e.scalar_tensor_tensor(out=a1v[:, :, s], in0=modv[:, 8:16, s], scalar=1.0,
                                                               in1=n1wt[:], op0=ALU.add, op1=ALU.mult))
                  sch.op("dve", ["modt", "n2wt"], ["a2"],
                         lambda e, s=s: e.scalar_tensor_tensor(out=a2v[:, :, s], in0=modv[:, 32:40, s], scalar=1.0,
                                                               in1=n2wt[:], op0=ALU.add, op1=ALU.mult))
              sch.barrier()

        def rstd_from_ps(psS, pskS, w, scale, dst, dstk, tmp, tmpk):
            sch.op("act", [pskS], [tmpk],
                   lambda e: e.activation(out=tmp[:, :w], in_=psS[:, :w], func=AF.Ln, scale=scale, bias=epsb[:, 0:1]))
            sch.op("act", [tmpk], [dstk],
                   lambda e: e.activation(out=dst[:, :w], in_=tmp[:, :w], func=AF.Exp, scale=-0.5))

        epsb = sb("epsb", [128, 2])
        sch.op("dve", [], ["epsb"], lambda e: e.memset(epsb[:, 0:1], EPS))
        sch.op("dve", [], ["epsb"], lambda e: e.memset(epsb[:, 1:2], 1.0))

        if "A" in stages:
          with contextlib.ExitStack() as st:
            win = sb("win", [128, 8 * N_IN], BF16, stack=st)
            winv = win[:].rearrange("p (k c) -> p k c", c=N_IN)
            W_INv = W_IN.rearrange("(kc p) c -> p kc c", p=128)
            for kc in range(8):
                for c0 in range(0, N_IN, 1928):
                    sch.dma("pool", winv[:, kc, c0:c0 + 1928], W_INv[:, kc, c0:c0 + 1928], [], ["win"])
            qknw = sb("qknw", [128, 2], stack=st)
            sch.dma("sp", qknw[:], QKNW[:, :], [], ["qknw"])
            alog = sb("alog", [128, 16], stack=st)
            dtb = sb("dtb", [128, 16], stack=st)
            nega = sb("nega", [128, 16], stack=st)
            sch.dma("sp", alog[:], ALOG[:, :], [], ["alog"])
            sch.dma("sp", dtb[:], DTB[:, :], [], ["dtb"])
            sch.op("act", ["alog"], ["nega0"], lambda e: e.activation(out=nega[:], in_=alog[:], func=AF.Exp))
            sch.op("dve", ["nega0"], ["nega"],
                   lambda e: e.tensor_scalar(out=nega[:], in0=nega[:], scalar1=-1.0, scalar2=None, op0=ALU.mult))
            xts = [sb(f"xt{i}", [128, 8 * 512], stack=st) for i in range(1)]
            hts = [sb(f"ht{i}", [128, 8 * 512], BF16, stack=st) for i in range(2)]
            sqs = Ring([(sb(f"sq{i}", [128, 512], stack=st), ("sq", i)) for i in range(3)])
            tmps = Ring([(sb(f"tmpa{i}", [128, 512], stack=st), ("tmpa", i)) for i in range(3)])
            rstds = Ring([(sb(f"rstd{i}", [128, 512], stack=st), ("rstd", i)) for i in range(2)])
            xns = Ring([(sb(f"xn{i}", [128, 512], stack=st), ("xn", i)) for i in range(2)])
            t1s = Ring([(sb(f"t1{i}", [128, 512], stack=st), ("t1", i)) for i in range(2)])
            obf = Ring([(sb(f"obf{i}", [128, 512], BF16, stack=st), ("obf", i)) for i in range(3)])
            of32 = Ring([(sb(f"of32{i}", [128, 512], stack=st), ("of32", i)) for i in range(4)])
            coss = [sb(f"cos{i}", [128, 512], stack=st) for i in range(2)]
            sins = [sb(f"sin{i}", [128, 512], stack=st) for i in range(2)]
            vbf = Ring([(sb(f"vbf{i}", [128, 256], BF16, stack=st), ("vbf", i)) for i in range(2)])
            bgt = Ring([(sb(f"bgt{i}", [128, 48], stack=st), ("bgt", i)) for i in range(2)])
            XTv = XT.rearrange("(kc p) t -> p kc t", p=128)
            tix = 0
            for run in runs:
                n, full, s, off = run["n"], run["full"], run["seq"], run["off"]
                ri = run["idx"]
                for (t0, w) in tiles_of(n + 4):
                    b = tix % 2
                    tix += 1
                    xt = xts[0][:].rearrange("p (k t) -> p k t", t=512)
                    ht = hts[b][:].rearrange("p (k t) -> p k t", t=512)
                    kx, kh = ("xt", 0), ("ht", b)
                    g0 = off + t0
                    sch.dma("sp", xt[:, :, :w], XTv[:, :, g0:g0 + w], [], [kx])
                    sch.dma("sp", coss[b][:, :w], COS[:, g0:g0 + w], [], [("cos", b)])
                    sch.dma("sp", sins[b][:, :w], SIN[:, g0:g0 + w], [], [("sin", b)])
                    psS, pskS = psring.next()
                    for kc in range(8):
                        sq, sqk = sqs.next()
                        sch.op("act", [kx], [sqk],
                               lambda e, sq=sq, kc=kc: e.activation(out=sq[:, :w], in_=xt[:, kc, :w], func=AF.Square))
                        sch.op("pe", [sqk, "consts"], [pskS],
                               lambda e, sq=sq, kc=kc: e.matmul(psS[:, :w], lhsT=ones, rhs=sq[:, :w],
                                                                start=(kc == 0), stop=(kc == 7)))
                    rstd, rstdk = rstds.next()
                    tmp, tmpk = tmps.next()
                    rstd_from_ps(psS, pskS, w, 1.0 / D, rstd, rstdk, tmp, tmpk)
                    for kc in range(8):
                        tmp, tmpk = tmps.next()
                        sch.op("dve", [kx, rstdk], [tmpk],
                               lambda e, tmp=tmp, kc=kc: e.tensor_tensor(out=tmp[:, :w], in0=xt[:, kc, :w],
                                                                         in1=rstd[:, :w], op=ALU.mult))
                        sch.op("act", [tmpk, "a1", "modt"], [kh],
                               lambda e, tmp=tmp, kc=kc: e.activation(out=ht[:, kc, :w], in_=tmp[:, :w],
                                                                      func=AF.Identity, scale=a1v[:, kc, s:s + 1],
                                                                      bias=modv[:, 0 + kc, s:s + 1]))

                    def proj(c0, m=128):
                        ps, psk = psring.next()
                        for kc in range(8):
                            sch.op("pe", [kh, "win"], [psk],
                                   lambda e, kc=kc: e.matmul(ps[:m, :w], lhsT=winv[:, kc, c0:c0 + m], rhs=ht[:, kc, :w],
                                                             start=(kc == 0), stop=(kc == 7)))
                        return ps, psk

                    def head_norm_rope(c0, wcol, dst_dram):
                        ps, psk = proj(c0)
                        sq, sqk = sqs.next()
                        sch.op("act", [psk], [sqk], lambda e: e.activation(out=sq[:, :w], in_=ps[:, :w], func=AF.Square))
                        ps2, psk2 = psring.next()
                        sch.op("pe", [sqk, "consts"], [psk2],
                               lambda e: e.matmul(ps2[:, :w], lhsT=ones, rhs=sq[:, :w], start=True, stop=True))
                        tmp, tmpk = tmps.next()
                        rs, rsk = rstds.next()
                        rstd_from_ps(ps2, psk2, w, 1.0 / 128, rs, rsk, tmp, tmpk)
                        xn, xnk = xns.next()
                        sch.op("dve", [psk, rsk, "qknw"], [xnk],
                               lambda e: e.scalar_tensor_tensor(out=xn[:, :w], in0=ps[:, :w], scalar=qknw[:, wcol:wcol + 1],
                                                                in1=rs[:, :w], op0=ALU.mult, op1=ALU.mult))
                        ps3, psk3 = psring.next()
                        sch.op("pe", [xnk, "consts"], [psk3],
                               lambda e: e.matmul(ps3[:, :w], lhsT=rrot, rhs=xn[:, :w], start=True, stop=True))
                        t1, t1k = t1s.next()
                        sch.op("pool", [xnk, ("cos", b)], [t1k],
                               lambda e: e.tensor_tensor(out=t1[:, :w], in0=xn[:, :w], in1=coss[b][:, :w], op=ALU.mult))
                        tmp2, tmp2k = tmps.next()
                        sch.op("dve", [psk3, ("sin", b)], [tmp2k],
                               lambda e: e.tensor_tensor(out=tmp2[:, :w], in0=ps3[:, :w], in1=sins[b][:, :w], op=ALU.mult))
                        ob, obk = obf.next()
                        sch.op("dve", [t1k, tmp2k], [obk],
                               lambda e: e.tensor_tensor(out=ob[:, :w], in0=t1[:, :w], in1=tmp2[:, :w], op=ALU.add))
                        lo, hi = max(t0, 2), min(t0 + w, n + 2)
                        if hi > lo:
                            sch.dma("pool", dst_dram(lo - 2, hi - 2), ob[:, lo - t0:hi - t0], [obk], [])

                    if full:
                        for h in range(8):
                            head_norm_rope(h * 128, 0,
                                           lambda a, bb, h=h: QT[h * 128:(h + 1) * 128, run["ooff"] + a: run["ooff"] + bb])
                    for g in range(2):
                        head_norm_rope(1024 + g * 128, 1,
                                       lambda a, bb, g=g: KT[run["ctx"]][g * 128:(g + 1) * 128,
                                                                         run["coff"] + a: run["coff"] + bb])
                    for sub in range(0, w, 128):
                        sw = min(128, w - sub)
                        ps, psk = psring.next()
                        for kc in range(8):
                            sch.op("pe", [kh, "win"], [psk],
                                   lambda e, kc=kc: e.matmul(ps[:sw, 0:256], lhsT=ht[:, kc, sub:sub + sw],
                                                             rhs=winv[:, kc, 1280:1536], start=(kc == 0), stop=(kc == 7)))
                        for kc in range(8):
                            sch.op("pe", [kh, "win"], [psk],
                                   lambda e, kc=kc: e.matmul(ps[:sw, 256:288], lhsT=ht[:, kc, sub:sub + sw],
                                                             rhs=winv[:, kc, 5632:5664], start=(kc == 0), stop=(kc == 7)))
                        vb, vbk = vbf.next()
                        sch.op("act", [psk], [vbk], lambda e: e.activation(out=vb[:sw, :], in_=ps[:sw, 0:256], func=AF.Copy))
                        lo, hi = max(t0 + sub, 2), min(t0 + sub + sw, n + 2)
                        if hi > lo:
                            c0 = run["coff"]
                            sch.dma("pool", VV[run["ctx"]][c0 + lo - 2:c0 + hi - 2, :],
                                    vb[lo - t0 - sub:hi - t0 - sub, :], [vbk], [])
                        bg, bgk = bgt.next()
                        sch.op("act", [psk], [bgk],
                               lambda e: e.activation(out=bg[:sw, 0:16], in_=ps[:sw, 256:272], func=AF.Sigmoid))
                        sch.op("dve", [psk, "dtb"], [bgk],
                               lambda e: e.tensor_tensor(out=bg[:sw, 32:48], in0=ps[:sw, 272:288], in1=dtb[:sw, 0:16],
                                                         op=ALU.add))
                        sch.op("act", [bgk], [bgk], lambda e: e.activation(out=bg[:sw, 32:48], in_=bg[:sw, 32:48], func=AF.Exp))
                        sch.op("act", [bgk], [bgk],
                               lambda e: e.activation(out=bg[:sw, 32:48], in_=bg[:sw, 32:48], func=AF.Ln, bias=epsb[:sw, 1:2]))
                        sch.op("dve", [bgk, "nega"], [bgk],
                               lambda e: e.tensor_tensor(out=bg[:sw, 16:32], in0=bg[:sw, 32:48], in1=nega[:sw, 0:16], op=ALU.mult))
                        if not full:
                            for cf, cb in ((0, 8), (16, 24)):
                                sch.op("dve", [bgk, "fl"], [bgk],
                                       lambda e, cb=cb: e.tensor_scalar(out=bg[:sw, cb:cb + 8], in0=bg[:sw, cb:cb + 8],
                                                                        scalar1=flag(ri, 6)[:sw, :], scalar2=None, op0=ALU.mult))
                                sch.op("dve", [bgk, "fl"], [bgk],
                                       lambda e, cf=cf, cb=cb: e.scalar_tensor_tensor(
                                           out=bg[:sw, cf:cf + 8], in0=bg[:sw, cf:cf + 8], scalar=flag(ri, 5)[:sw, :],
                                           in1=bg[:sw, cb:cb + 8], op0=ALU.mult, op1=ALU.add))
                        sch.dma("pool", BG[g0 + sub:g0 + sub + sw, 0:32], bg[:sw, 0:32], [bgk], [])
                    for j in range(24):
                        if (not full) and j < 8:
                            continue
                        ps, psk = proj(1536 + j * 128)
                        o, ok = of32.next()
                        if j % 2 == 0:
                            sch.op("act", [psk], [ok], lambda e, o=o, ps=ps: e.activation(out=o[:, :w], in_=ps[:, :w], func=AF.Copy))
                        else:
                            sch.op("dve", [psk], [ok], lambda e, o=o, ps=ps: e.tensor_copy(out=o[:, :w], in_=ps[:, :w]))
                        sch.dma("pool", DPRE3[j // 8][(j % 8) * 128:(j % 8 + 1) * 128, g0:g0 + w], o[:, :w], [ok], [])
                    if full:
                        lo, hi = max(t0, 2), min(t0 + w, n + 2)
                        oo = run["ooff"]
                        for j in range(8):
                            ps, psk = proj(4608 + j * 128)
                            o, ok = of32.next()
                            sch.op("act", [psk], [ok], lambda e, o=o, ps=ps: e.activation(out=o[:, :w], in_=ps[:, :w], func=AF.Silu))
                            if hi > lo:
                                sch.dma("pool", ZS[j * 128:(j + 1) * 128, oo + lo - 2:oo + hi - 2], o[:, lo - t0:hi - t0], [ok], [])
                        for j in range(16):
                            ps, psk = proj(5664 + j * 128)
                            o, ok = of32.next()
                            sch.op("act", [psk], [ok], lambda e, o=o, ps=ps: e.activation(out=o[:, :w], in_=ps[:, :w], func=AF.Sigmoid))
                            if hi > lo:
                                sch.dma("pool", GATES[j * 128:(j + 1) * 128, oo + lo - 2:oo + hi - 2], o[:, lo - t0:hi - t0], [ok], [])
            sch.barrier()

        if "CONV" in stages:
          with contextlib.ExitStack() as st:
            cw = sb("cw", [128, NR * 24 * 5], stack=st)
            sch.dma("sp", cw[:], CW[:, :], [], ["cw"])
            cwv = cw[:].rearrange("p (r j k) -> p r j k", j=24, k=5)
            wins = Ring([(sb(f"cwin{i}", [128, 516], stack=st), ("cwin", i)) for i in range(3)])
            accs = Ring([(sb(f"cacc{i}", [128, 512], stack=st), ("cacc", i)) for i in range(2)])
            sils = Ring([(sb(f"csil{i}", [128, 512], stack=st), ("csil", i)) for i in range(3)])
            sq2 = Ring([(sb(f"csq{i}", [128, 512], stack=st), ("csq", i)) for i in range(2)])
            tm2 = Ring([(sb(f"ctm{i}", [128, 512], stack=st), ("ctm", i)) for i in range(2)])
            rn2 = Ring([(sb(f"crn{i}", [128, 512], stack=st), ("crn", i)) for i in range(2)])
            ou2 = Ring([(sb(f"cou{i}", [128, 512], stack=st), ("cou", i)) for i in range(3)])
            for run in runs:
                n, full, off, ri = run["n"], run["full"], run["off"], run["idx"]
                for j in range(24):
                    if (not full) and j < 8:
                        continue
                    for (t0, w) in tiles_of(n):
                        g0 = off + t0
                        win, wk = wins.next()
                        sch.dma("sp", win[:, :w + 4], DPRE3[j // 8][(j % 8) * 128:(j % 8 + 1) * 128, g0:g0 + w + 4], [], [wk])
                        if t0 == 0:
                            sch.op("dve", [wk, "fl"], [wk],
                                   lambda e: e.tensor_scalar(out=win[:, 0:2], in0=win[:, 0:2], scalar1=flag(ri, 0),
                                                             scalar2=None, op0=ALU.mult))
                        if t0 + w == n:
                            sch.op("dve", [wk, "fl"], [wk],
                                   lambda e: e.tensor_scalar(out=win[:, w + 2:w + 4], in0=win[:, w + 2:w + 4],
                                                             scalar1=flag(ri, 1), scalar2=None, op0=ALU.mult))
                        acc, ak = accs.next()
                        sch.op("dve", [wk, "cw"], [ak],
                               lambda e: e.tensor_scalar(out=acc[:, :w], in0=win[:, 0:w], scalar1=cwv[:, ri, j, 0:1],
                                                         scalar2=None, op0=ALU.mult))
                        for k in range(1, 5):
                            sch.op("dve", [wk, "cw", ak], [ak],
                                   lambda e, k=k: e.scalar_tensor_tensor(out=acc[:, :w], in0=win[:, k:k + w],
                                                                         scalar=cwv[:, ri, j, k:k + 1], in1=acc[:, :w],
                                                                         op0=ALU.mult, op1=ALU.add))
                        sil, sk = sils.next()
                        sch.op("act", [ak], [sk], lambda e: e.activation(out=sil[:, :w], in_=acc[:, :w], func=AF.Silu))
                        dst = DQKV3[j // 8][(j % 8) * 128:(j % 8 + 1) * 128, off + 2 + t0: off + 2 + t0 + w]
                        if j >= 16:
                            sch.dma("pool", dst, sil[:, :w], [sk], [])
                            continue
                        sq, sqk = sq2.next()
                        sch.op("act", [sk], [sqk], lambda e: e.activation(out=sq[:, :w], in_=sil[:, :w], func=AF.Square))
                        ps, psk = psring.next()
                        sch.op("pe", [sqk, "consts"], [psk],
                               lambda e: e.matmul(ps[:, :w], lhsT=ones, rhs=sq[:, :w], start=True, stop=True))
                        tm, tmk = tm2.next()
                        rn, rnk = rn2.next()
                        rstd_from_ps(ps, psk, w, 1.0, rn, rnk, tm, tmk)
                        ou, ouk = ou2.next()
                        cmul = (128.0 ** -0.5) if j < 8 else 1.0
                        sch.op("dve", [sk, rnk], [ouk],
                               lambda e: e.scalar_tensor_tensor(out=ou[:, :w], in0=sil[:, :w], scalar=cmul, in1=rn[:, :w],
                                                                op0=ALU.mult, op1=ALU.mult))
                        sch.dma("pool", dst, ou[:, :w], [ouk], [])
            sch.barrier()

        if "DN" in stages:
          with contextlib.ExitStack() as st:
            MASKA = [consts[:, 384:512], consts[:, 512:640]]
            MASKQ = [consts[:, 640:768], consts[:, 768:896]]
            CUM = [consts[:, 896:1024], consts[:, 1024:1152]]
            ONESBD = consts[:, 1152:1280]
            WIN = int(os.environ.get("DN_WIN", "6"))
            rings = {}

            def RB(name, depth=3, shape=(128, 128), dt=F32):
                if name not in rings:
                    rings[name] = Ring([(sb(f"dn_{name}{i}", list(shape), dt, stack=st), (name, i)) for i in range(depth)])
                return rings[name].next()

            slot_rings = [Ring([(sb(f"dn_l{sl}_{i}", [128, 128], stack=st), ("dnl", sl, i)) for i in range(8)]) for sl in range(WIN)]
            slot_rh = [[(sb(f"dn_rh{sl}_{i}", [128, 128], stack=st), ("dnrh", sl, i)) for i in range(2)] for sl in range(WIN)]
            free_slots = list(range(WIN))
            Sst, S16 = {}, {}
            for h in range(8):
                for d in range(2):
                    Sst[(h, d)] = [sb(f"dn_S{h}_{d}_{i}", [128, 128], stack=st) for i in range(2)]
                    S16[(h, d)] = [sb(f"dn_Sh{h}_{d}_{i}", [128, 128], BF16, stack=st) for i in range(2)]
            Scur = {k: 0 for k in Sst}
            SF = [sb(f"dn_SF{h}", [128, 128], stack=st) for h in range(8)]
            SB = [sb(f"dn_SB{h}", [128, 128], stack=st) for h in range(8)]
            for h in range(8):
                sch.op("pool", [], [("SF", h)], lambda e, h=h: e.memset(SF[h][:], 0.0))
                sch.op("pool", [], [("SB", h)], lambda e, h=h: e.memset(SB[h][:], 0.0))
                sch.op("pool", [], [("Sb", h, 0, 0)], lambda e, h=h: e.memset(Sst[(h, 0)][0][:], 0.0))
            DQv3 = [DQKV3[i].rearrange("(h d) c -> d h c", h=8) for i in range(3)]
            evac_flip = [0]
            chain_pos = {}

            def evac(dst, dstk, src, srck):
                evac_flip[0] ^= 1
                if evac_flip[0]:
                    sch.op("act", [srck], [dstk], lambda e: e.activation(out=dst, in_=src, func=AF.Copy))
                else:
                    sch.op("dve", [srck], [dstk], lambda e: e.tensor_copy(out=dst, in_=src))

            def mm(ps, psk, lhsT, rhs, reads, start=True, stop=True):
                sch.op("pe", list(reads), [psk], lambda e: e.matmul(ps, lhsT=lhsT, rhs=rhs, start=start, stop=stop))

            def tr(ps, psk, in_, reads):
                sch.op("pe", list(reads) + ["consts"], [psk], lambda e: e.transpose(out=ps, in_=in_, identity=ident))

            def load_block(run, p, d, blk, nb):
                full, off = run["full"], run["off"]
                tagd = f"d{d}"
                qkvb, qk = RB("qkvb" + tagd, 2, (128, 2 * 1024))
                kq16, kq16k = RB("kq16" + tagd, 2, (128, 2 * 1024), BF16)
                c0 = off + 2 + blk * 512
                for tq in range(3):
                    if tq == 0 and not full:
                        continue
                    for hh in range(2):
                        src = DQv3[tq][:, 2 * p + hh, c0:c0 + nb * 64].rearrange("d (c t) -> d c t", t=64)
                        if tq == 0:
                            sch.dma("pool", kq16[:, 0:nb * 128].rearrange("p (c h t) -> p h c t", h=2, t=64)[:, hh], src, [], [kq16k])
                        else:
                            sch.dma("sp", qkvb[:, (tq - 1) * 1024:(tq - 1) * 1024 + nb * 128].rearrange("p (c h t) -> p h c t", h=2, t=64)[:, hh],
                                    src, [], [qk])
                sch.op("act", [qk], [kq16k], lambda e: e.activation(out=kq16[:, 1024:1024 + nb * 128], in_=qkvb[:, 0:nb * 128], func=AF.Copy))
                bgb, bk = RB("bgb" + tagd, 2, (128, 8 * 64))
                bgv = bgb[:].rearrange("p (c f) -> p c f", f=64)
                BGr = BG[c0:c0 + nb * 64, :].rearrange("(c t) f -> t c f", t=64)
                for half in range(2):
                    pr = slice(half * 64, half * 64 + 64)
                    sh = 1 if half == 1 else 0
                    sch.dma("sp", bgv[pr, :nb, 0:16], BGr[:, :, 0:16], [], [bk])
                    sch.dma("sp", bgv[pr, :nb, 16:32], BGr[:, :, sh:sh + 16], [], [bk])
                    sch.dma("sp", bgv[pr, :nb, 32:48], BGr[:, :, 16:32], [], [bk])
                    sch.dma("sp", bgv[pr, :nb, 48:64], BGr[:, :, 16 + sh:32 + sh], [], [bk])
                psc, psck = psring.next()
                pst, pstk = psring.next()
                pscv = psc[:, 0:256].rearrange("p (c f) -> p c f", f=32)
                pstv = pst[:, 0:256].rearrange("p (c f) -> p c f", f=32)
                mm(pscv[:, :nb, :], psck, CUM[d], bgv[:, :nb, 32:64], [bk, "consts"])
                mm(pstv[:, :nb, :], pstk, ONESBD, bgv[:, :nb, 32:64], [bk, "consts"])
                sm, smk = RB("small" + tagd, 2, (128, 8 * 256))
                smv = sm[:].rearrange("p (a c f) -> p a c f", a=8, f=32)
                GC, EGC, EDK, EGT, BE, NB, EDKA, EDKB = (smv[:, a, :, :] for a in range(8))
                sch.op("act", [psck], [smk], lambda e: e.activation(out=GC[:, :nb, :], in_=pscv[:, :nb, :], func=AF.Copy))
                sch.op("act", [psck], [smk], lambda e: e.activation(out=EGC[:, :nb, :], in_=pscv[:, :nb, :], func=AF.Exp))
                sch.op("dve", [pstk, smk], [smk],
                       lambda e: e.tensor_tensor(out=EDK[:, :nb, :], in0=pstv[:, :nb, :], in1=GC[:, :nb, :], op=ALU.subtract))
                sch.op("act", [smk], [smk], lambda e: e.activation(out=EDK[:, :nb, :], in_=EDK[:, :nb, :], func=AF.Exp))
                sch.op("act", [pstk], [smk], lambda e: e.activation(out=EGT[:, :nb, :], in_=pstv[:, :nb, :], func=AF.Exp))
                sch.op("dve", [smk, "consts"], [smk],
                       lambda e: e.tensor_scalar(out=EDKA[:, :nb, :], in0=EDK[:, :nb, :], scalar1=ONESBD[:, 0:1], scalar2=None, op0=ALU.mult))
                sch.op("dve", [smk, "consts"], [smk],
                       lambda e: e.tensor_scalar(out=EDKB[:, :nb, :], in0=EDK[:, :nb, :], scalar1=ONESBD[:, 64:65], scalar2=None, op0=ALU.mult))
                sch.op("dve", [bk, smk], [smk],
                       lambda e: e.tensor_tensor(out=BE[:, :nb, :], in0=bgv[:, :nb, 0:32], in1=EGC[:, :nb, :], op=ALU.mult))
                sch.op("dve", [bk], [smk],
                       lambda e: e.tensor_scalar(out=NB[:, :nb, :], in0=bgv[:, :nb, 0:32], scalar1=-1.0, scalar2=None, op0=ALU.mult))
                return dict(qkvb=qkvb, qk=qk, kq16=kq16, kq16k=kq16k, bgv=bgv, bk=bk, GC=GC, EGC=EGC, EDK=EDK, EGT=EGT, BE=BE, NB=NB,
                            EDKA=EDKA, EDKB=EDKB, smk=smk)

            def inst_gen(run, p, d, c, bc, ostctx, tseq):
                full = run["full"]
                nch = run["n"] // 64
                blk, ch = c // 8, c % 8
                nb = min(8, nch - blk * 8)
                slot = free_slots.pop(0)
                slot_rings[slot].i = 0
                TB = slot_rings[slot].next
                qkvb, qk, kq16, kq16k, bgv, bk, smk = bc["qkvb"], bc["qk"], bc["kq16"], bc["kq16k"], bc["bgv"], bc["bk"], bc["smk"]
                colp = 16 + d * 8 + 2 * p
                pcol = lambda A: A[:, ch, colp:colp + 1]
                cs = slice(ch * 64, ch * 64 + 64)
                KTp, VTp = (qkvb[:, tq * 1024 + ch * 128: tq * 1024 + ch * 128 + 128] for tq in (0, 1))
                Q16, K16 = (kq16[:, tq * 1024 + ch * 128: tq * 1024 + ch * 128 + 128] for tq in (0, 1))
                ps_k, ps_kk = psring.next()
                tr(ps_k[:, 0:128], ps_kk, KTp, [qk])
                ps_v, ps_vk = psring.next()
                tr(ps_v[:, 0:128], ps_vk, VTp, [qk])
                dg, dgk = TB()
                sch.op("pool", ["consts", smk], [dgk],
                       lambda e: e.tensor_scalar(out=dg[:], in0=ident, scalar1=pcol(bc["GC"]), scalar2=None, op0=ALU.mult))
                rhsk, rhskk = slot_rh[slot][0]
                sch.op("act", [ps_kk, smk], [rhskk],
                       lambda e: e.activation(out=rhsk[:], in_=ps_k[:, 0:128], func=AF.Identity, scale=pcol(bc["BE"])))
                kda, kdak = RB("kda", WIN + 2, dt=BF16)
                kdb, kdbk = RB("kdb", WIN + 2, dt=BF16)
                sch.op("dve", [ps_kk, smk], [kdak],
                       lambda e: e.tensor_scalar(out=kda[:], in0=ps_k[:, 0:128], scalar1=pcol(bc["EDKA"]), scalar2=None, op0=ALU.mult))
                sch.op("dve", [ps_kk, smk], [kdbk],
                       lambda e: e.tensor_scalar(out=kdb[:], in0=ps_k[:, 0:128], scalar1=pcol(bc["EDKB"]), scalar2=None, op0=ALU.mult))
                rhsv, rhsvk = slot_rh[slot][1]
                sch.op("act", [ps_vk, bk], [rhsvk],
                       lambda e: e.activation(out=rhsv[:], in_=ps_v[:, 0:128], func=AF.Identity, scale=pcol(bgv)))
                yield
                ps_g, ps_gk = psring.next()
                mm(ps_g[:, 0:128], ps_gk, K16, K16, [kq16k])
                ps_r, ps_rk = psring.next()
                mm(ps_r[:, 0:128], ps_rk, ones, dg[:], [dgk, "consts"])
                gm, gmk = TB()
                sch.op("dve", [ps_gk, "consts"], [gmk],
                       lambda e: e.tensor_tensor(out=gm[:], in0=ps_g[:, 0:128], in1=MASKA[d], op=ALU.mult))
                t1, t1k = TB()
                sch.op("dve", [ps_rk, smk], [t1k],
                       lambda e: e.tensor_scalar(out=t1[:], in0=ps_r[:, 0:128], scalar1=pcol(bc["GC"]), scalar2=0.0,
                                                 op0=ALU.subtract, op1=ALU.max))
                if full:
                    ps_q, ps_qk = psring.next()
                    mm(ps_q[:, 0:128], ps_qk, K16, Q16, [kq16k])
                    t2, t2k = TB()
                    sch.op("dve", [ps_rk, smk], [t2k],
                           lambda e: e.tensor_scalar(out=t2[:], in0=ps_r[:, 0:128], scalar1=pcol(bc["GC"]), scalar2=0.0,
                                                     op0=ALU.subtract, op1=ALU.min))
                    erow, erowk = TB()
                    sch.op("act", [ps_rk], [erowk], lambda e: e.activation(out=erow[:], in_=ps_r[:, 0:128], func=AF.Exp))
                    kqm, kqmk = TB()
                    sch.op("dve", [ps_qk, "consts"], [kqmk],
                           lambda e: e.tensor_tensor(out=kqm[:], in0=ps_q[:, 0:128], in1=MASKQ[d], op=ALU.mult))
                yield
                sch.op("act", [t1k], [t1k], lambda e: e.activation(out=t1[:], in_=t1[:], func=AF.Exp, scale=-1.0))
                b0, b0k = TB()
                sch.op("dve", [gmk, t1k, smk], [b0k],
                       lambda e: e.scalar_tensor_tensor(out=b0[:], in0=gm[:], scalar=pcol(bc["NB"]), in1=t1[:], op0=ALU.mult, op1=ALU.mult))
                if full:
                    sch.op("act", [t2k], [t2k], lambda e: e.activation(out=t2[:], in_=t2[:], func=AF.Exp))
                    aqt, aqtk = RB("aqt", WIN + 2, dt=BF16)
                    sch.op("pool", [kqmk, t2k], [aqtk], lambda e: e.tensor_tensor(out=aqt[:], in0=kqm[:], in1=t2[:], op=ALU.mult))
                    qet, qetk = RB("qet", WIN + 2, dt=BF16)
                    sch.op("pool", [kq16k, erowk], [qetk],
                           lambda e: e.tensor_tensor(out=qet[:], in0=Q16, in1=erow[:], op=ALU.mult))
                yield
                ps_t, ps_tk = psring.next()
                tr(ps_t[:, 0:128], ps_tk, b0[:], [b0k])
                bt, btk = TB()
                sch.op("act", [ps_tk], [btk], lambda e: e.activation(out=bt[:], in_=ps_t[:, 0:128], func=AF.Copy))
                pp, ppk = TB()
                sch.op("dve", [ps_tk, "consts"], [ppk],
                       lambda e: e.tensor_tensor(out=pp[:], in0=ps_t[:, 0:128], in1=ident, op=ALU.add))
                bprev, bprevk, btprev, btprevk = b0, b0k, bt, btk
                for lev in range(1, 6):
                    yield
                    ps_b, ps_bk = psring.next()
                    mm(ps_b[:, 0:128], ps_bk, btprev[:], bprev[:], [btprevk, bprevk])
                    ib, ibk = TB()
                    sch.op("dve", [ps_bk, "consts"], [ibk],
                           lambda e: e.tensor_tensor(out=ib[:], in0=ps_b[:, 0:128], in1=ident, op=ALU.add))
                    if lev < 5:
                        bn, bnk = TB()
                        sch.op("act", [ps_bk], [bnk], lambda e: e.activation(out=bn[:], in_=ps_b[:, 0:128], func=AF.Copy))
                    yield
                    ps_p, ps_pk = psring.next()
                    mm(ps_p[:, 0:128], ps_pk, ib[:], pp[:], [ibk, ppk])
                    pn, pnk = TB()
                    evac(pn[:], pnk, ps_p[:, 0:128], ps_pk)
                    pp, ppk = pn, pnk
                    if lev < 5:
                        ps_b2, ps_b2k = psring.next()
                        tr(ps_b2[:, 0:128], ps_b2k, bn[:], [bnk])
                        btn, btnk = TB()
                        evac(btn[:], btnk, ps_b2[:, 0:128], ps_b2k)
                        bprev, bprevk, btprev, btprevk = bn, bnk, btn, btnk
                TT, TTk = pp, ppk
                yield
                ps_u, ps_uk = psring.next()
                mm(ps_u[:, 0:128], ps_uk, TT[:], rhsv[:], [TTk, rhsvk])
                ps_w, ps_wk = psring.next()
                mm(ps_w[:, 0:128], ps_wk, rhsk[:], TT[:], [TTk, rhskk])
                uu, uuk = RB("uu", WIN + 2)
                evac(uu[:], uuk, ps_u[:, 0:128], ps_uk)
                wta, wtak = RB("wta", WIN + 2, dt=BF16)
                wtb, wtbk = RB("wtb", WIN + 2, dt=BF16)
                if ("wtz", wtak) not in rings:
                    rings[("wtz", wtak)] = True
                    sch.op("pool", [], [wtak], lambda e: e.memset(wta[:], 0.0))
                    sch.op("pool", [], [wtbk], lambda e: e.memset(wtb[:], 0.0))
                sch.op("act", [ps_wk], [wtak], lambda e: e.activation(out=wta[:, 0:64], in_=ps_w[:, 0:64], func=AF.Copy))
                sch.op("dve", [ps_wk], [wtbk], lambda e: e.tensor_copy(out=wtb[:, 64:128], in_=ps_w[:, 64:128]))
                free_slots.append(slot)
                yield
                h0, h1 = 2 * p, 2 * p + 1
                assert chain_pos.get((run["idx"], p, d), 0) == tseq, "chain order violated"
                c0_, c1_ = Scur[(h0, d)], Scur[(h1, d)]
                S0, S1 = Sst[(h0, d)][c0_], Sst[(h1, d)][c1_]
                S0h, S1h = S16[(h0, d)][c0_], S16[(h1, d)][c1_]
                S0n, S1n = Sst[(h0, d)][1 - c0_], Sst[(h1, d)][1 - c1_]
                S0hn, S1hn = S16[(h0, d)][1 - c0_], S16[(h1, d)][1 - c1_]
                s0ck, s1ck = ("Sb", h0, d, c0_), ("Sb", h1, d, c1_)
                s0nk, s1nk = ("Sb", h0, d, 1 - c0_), ("Sb", h1, d, 1 - c1_)
                s0hk, s1hk = ("Sh", h0, d, c0_), ("Sh", h1, d, c1_)
                s0hnk, s1hnk = ("Sh", h0, d, 1 - c0_), ("Sh", h1, d, 1 - c1_)
                Scur[(h0, d)], Scur[(h1, d)] = 1 - c0_, 1 - c1_
                ps_ws, ps_wsk = psring.next()
                mm(ps_ws[:, 0:128], ps_wsk, wta[:], S0h[:], [wtak, s0hk], start=True, stop=False)
                mm(ps_ws[:, 0:128], ps_wsk, wtb[:], S1h[:], [wtbk, s1hk], start=False, stop=True)
                vn, vnk = RB("vn", WIN + 2, dt=BF16)
                sch.op("dve", [uuk, ps_wsk], [vnk],
                       lambda e: e.tensor_tensor(out=vn[:], in0=uu[:], in1=ps_ws[:, 0:128], op=ALU.subtract))
                yield
                ps_s0, ps_s0k = psring.next()
                mm(ps_s0[:, 0:128], ps_s0k, kda[:], vn[:], [kdak, vnk])
                ps_s1, ps_s1k = psring.next()
                mm(ps_s1[:, 0:128], ps_s1k, kdb[:], vn[:], [kdbk, vnk])
                if full:
                    ps_o, ps_ok = psring.next()
                    mm(ps_o[:, 0:128], ps_ok, vn[:], aqt[:], [vnk, aqtk], start=True, stop=False)
                    mm(ps_o[:, 0:64], ps_ok, S0h[:], qet[:, 0:64], [s0hk, qetk], start=False, stop=False)
                    mm(ps_o[:, 64:128], ps_ok, S1h[:], qet[:, 64:128], [s1hk, qetk], start=False, stop=True)
                for (ps_s, ps_sk, Sx, Sn, Shn, sck, snk, shnk, hh) in ((ps_s0, ps_s0k, S0, S0n, S0hn, s0ck, s0nk, s0hnk, h0),
                                                                       (ps_s1, ps_s1k, S1, S1n, S1hn, s1ck, s1nk, s1hnk, h1)):
                    ecol = bc["EGT"][:, ch, d * 8 + hh: d * 8 + hh + 1]
                    sch.op("dve", [ps_sk, sck, smk], [snk],
                           lambda e, Sx=Sx, Sn=Sn, ps_s=ps_s, ecol=ecol: e.scalar_tensor_tensor(
                               out=Sn[:], in0=Sx[:], scalar=ecol, in1=ps_s[:, 0:128], op0=ALU.mult, op1=ALU.add))
                    sch.op("pool", [snk], [shnk], lambda e, Sn=Sn, Shn=Shn: e.tensor_copy(out=Shn[:], in_=Sn[:]))
                chain_pos[(run["idx"], p, d)] = tseq + 1
                if full:
                    first_of_block = (ch == 0) if d == 0 else (ch == nb - 1)
                    last_of_block = (ch == nb - 1) if d == 0 else (ch == 0)
                    if first_of_block:
                        ostctx[d] = RB(f"ostd{d}", 2, (128, 2 * 512))
                    ost, ostk = ostctx[d]
                    ostv = ost[:].rearrange("p (h c) -> p h c", h=2)
                    sch.op("act", [ps_ok], [ostk],
                           lambda e: e.activation(out=ostv[:, :, cs], in_=ps_o[:, 0:128].rearrange("p (h c) -> p h c", h=2), func=AF.Copy))
                    if last_of_block:
                        OD = OF if d == 0 else OB
                        oo = run["ooff"] + blk * 512
                        ODv = OD.rearrange("(h d) c -> d h c", h=8)
                        sch.dma("pool", ODv[:, 2 * p:2 * p + 2, oo:oo + nb * 64], ostv[:, :, :nb * 64], [ostk], [])

            order = [r for r in runs if not r["full"]] + [runs[2], runs[0], runs[1]]
            for run in order:
                n, full, off, ri = run["n"], run["full"], run["off"], run["idx"]
                nch = n // 64
                dirs = [0, 1] if full else [0]
                for h in range(8):
                    for d in dirs:
                        cur = Scur[(h, d)]
                        Sb, Sh = Sst[(h, d)][cur], S16[(h, d)][cur]
                        sbk, shk = ("Sb", h, d, cur), ("Sh", h, d, cur)
                        if not full:
                            sch.op("dve", [sbk, "fl"], [sbk],
                                   lambda e, Sb=Sb: e.tensor_scalar(out=Sb[:], in0=Sb[:], scalar1=flag(ri, 2), scalar2=None, op0=ALU.mult))
                        elif run["name"] == "own":
                            src = SF[h] if d == 0 else SB[h]
                            sk = ("SF", h) if d == 0 else ("SB", h)
                            sch.op("pool", [sk], [sbk], lambda e, Sb=Sb, src=src: e.tensor_copy(out=Sb[:], in_=src[:]))
                        else:
                            sch.op("pool", [], [sbk], lambda e, Sb=Sb: e.memset(Sb[:], 0.0))
                        sch.op("pool", [sbk], [shk], lambda e, Sb=Sb, Sh=Sh: e.tensor_copy(out=Sh[:], in_=Sb[:]))
                for p in range(4):
                    ostctx = {}
                    active = []
                    bctx = {}
                    NST = 18
                    STAG = max(3, -(-NST * len(dirs) // WIN))
                    tnext = 0
                    since = STAG
                    while tnext < nch or active:
                        if tnext < nch and since >= STAG and len(active) + len(dirs) <= WIN:
                            since = 0
                            for d in dirs:
                                c = tnext if d == 0 else nch - 1 - tnext
                                blk = c // 8
                                nb = min(8, nch - blk * 8)
                                if bctx.get(d, (None,))[0] != blk:
                                    bctx[d] = (blk, load_block(run, p, d, blk, nb))
                                active.append(inst_gen(run, p, d, c, bctx[d][1], ostctx, tnext))
                            tnext += 1
                        since += 1
                        nxt = []
                        for g in active:
                            try:
                                next(g)
                                nxt.append(g)
                            except StopIteration:
                                pass
                        active = nxt
                if not full:
                    for h in range(8):
                        Sb = Sst[(h, 0)][Scur[(h, 0)]]
                        sck = ("Sb", h, 0, Scur[(h, 0)])
                        sch.op("dve", [sck, "fl", ("SF", h)], [("SF", h)],
                               lambda e, Sb=Sb, h=h: e.scalar_tensor_tensor(out=SF[h][:], in0=Sb[:], scalar=flag(ri, 3), in1=SF[h][:],
                                                                            op0=ALU.mult, op1=ALU.add))
                        sch.op("dve", [sck, "fl", ("SB", h)], [("SB", h)],
                               lambda e, Sb=Sb, h=h: e.scalar_tensor_tensor(out=SB[h][:], in0=Sb[:], scalar=flag(ri, 4), in1=SB[h][:],
                                                                            op0=ALU.mult, op1=ALU.add))
            sch.barrier()

        if "ATT" in stages:
          with contextlib.ExitStack() as st:
            NKV = max(cfg["ctx_n"])
            kt = sb("att_kt", [128, NKV], BF16, stack=st)
            vt = sb("att_vt", [128, NKV], BF16, stack=st)
            onesb = sb("att_ones", [128, 128], BF16, stack=st)
            sch.op("dve", [], ["onesb"], lambda e: e.memset(onesb[:], 1.0))
            qts = Ring([(sb(f"att_q{i}", [128, 512], BF16, stack=st), ("attq", i)) for i in range(2)])
            pts = Ring([(sb(f"att_p{i}", [128, 512], BF16, stack=st), ("attp", i)) for i in range(4)])
            recs = Ring([(sb(f"att_r{i}", [128, 512], stack=st), ("attr", i)) for i in range(2)])
            daccs = Ring([(sb(f"att_d{i}", [128, 512], stack=st), ("attd", i)) for i in range(2)])
            aos = Ring([(sb(f"att_o{i}", [128, 512], BF16, stack=st), ("atto", i)) for i in range(2)])
            po, pok = psb[0], ("ps", 0)
            pd, pdk = psb[1], ("ps", 1)
            ring2 = Ring([(psb[i], ("ps", i)) for i in range(2, 8)])
            scale = 128.0 ** -0.5
            for run in runs[:3]:
                n, c, oo = run["n"], run["ctx"], run["ooff"]
                nkv = cfg["ctx_n"][c]
                nkb = nkv // 128
                vtv = vt[:, :nkv].rearrange("p (kb d) -> p kb d", d=128)
                for g in range(2):
                    for k0 in range(0, nkv, 4096):
                        k1 = min(nkv, k0 + 4096)
                        sch.dma("sp", kt[:, k0:k1], KT[c][g * 128:(g + 1) * 128, k0:k1], [], ["kt"])
                    VVr = VV[c].rearrange("(kb t) f -> t kb f", t=128)
                    for b0 in range(0, nkb, 8):
                        b1 = min(nkb, b0 + 8)
                        sch.dma("sp", vtv[:, b0:b1, :], VVr[:, b0:b1, g * 128:(g + 1) * 128], [], ["vt"])
                    for hq in range(4 * g, 4 * g + 4):
                        for (t0, w) in tiles_of(n):
                            qt, qk_ = qts.next()
                            sch.dma("sp", qt[:, :w], QT[hq * 128:(hq + 1) * 128, oo + t0:oo + t0 + w], [], [qk_])
                            for kb in range(nkb):
                                ps_s, ps_sk = ring2.next()
                                sch.op("pe", ["kt", qk_], [ps_sk],
                                       lambda e: e.matmul(ps_s[:, :w], lhsT=kt[:, kb * 128:(kb + 1) * 128], rhs=qt[:, :w],
                                                          start=True, stop=True))
                                pt, ptk = pts.next()
                                sch.op("act", [ps_sk], [ptk],
                                       lambda e: e.activation(out=pt[:, :w], in_=ps_s[:, :w], func=AF.Exp, scale=scale))
                                sch.op("pe", ["vt", ptk], [pok],
                                       lambda e: e.matmul(po[:, :w], lhsT=vtv[:, kb, :], rhs=pt[:, :w],
                                                          start=(kb == 0), stop=(kb == nkb - 1)))
                                if kb == 0:
                                    dacc, dacck = daccs.next()
                                    sch.op("dve", [ptk], [dacck], lambda e: e.tensor_copy(out=dacc[:, :w], in_=pt[:, :w]))
                                else:
                                    sch.op("dve", [ptk, dacck], [dacck],
                                           lambda e: e.tensor_tensor(out=dacc[:, :w], in0=dacc[:, :w], in1=pt[:, :w], op=ALU.add))
                            sch.op("pe", ["consts", dacck], [pdk],
                                   lambda e: e.matmul(pd[:, :w], lhsT=ones, rhs=dacc[:, :w], start=True, stop=True))
                            rec, reck = recs.next()
                            sch.op("dve", [pdk], [reck], lambda e: e.reciprocal(out=rec[:, :w], in_=pd[:, :w]))
                            ao, aok = aos.next()
                            sch.op("dve", [pok, reck], [aok],
                                   lambda e: e.tensor_tensor(out=ao[:, :w], in0=po[:, :w], in1=rec[:, :w], op=ALU.mult))
                            sch.dma("pool", ATT[hq * 128:(hq + 1) * 128, oo + t0:oo + t0 + w], ao[:, :w], [aok], [])
            sch.barrier()

        def load_w_bf16(dst, W, nk, ncols, key):
            Wv = W.rearrange("(kc p) c -> p kc c", p=128)
            dv = dst[:].rearrange("p (k c) -> p k c", c=ncols)
            for kc in range(nk):
                for c0 in range(0, ncols, 2048):
                    c1 = min(ncols, c0 + 2048)
                    sch.dma("pool", dv[:, kc, c0:c1], Wv[:, kc, c0:c1], [], [key])
            return dv

        own_tiles = []
        for run in runs[:3]:
            for (t0, w) in tiles_of(run["n"]):
                own_tiles.append((run, t0, w))

        if "C1" in stages:
          with contextlib.ExitStack() as st:
            wpa = load_w_bf16(sb("wpa", [128, 8 * D], BF16, stack=st), W_PA, 8, D, "wpa")
            wpb = load_w_bf16(sb("wpb", [128, 8 * D], BF16, stack=st), W_PB, 8, D, "wpb")
            wo = load_w_bf16(sb("wo", [128, 8 * D], BF16, stack=st), W_OUT, 8, D, "wo")
            dnw = sb("dnw", [128, 1], stack=st)
            sch.dma("sp", dnw[:], DNW[:, :], [], ["dnw"])
            f32r = Ring([(sb(f"c1f{i}", [128, 512], stack=st), ("c1f", i)) for i in range(10)])
            attb = sb("c1att", [128, 8 * 512], BF16, stack=st)
            attv = attb[:].rearrange("p (k t) -> p k t", t=512)
            ogb = sb("c1og", [128, 8 * 512], BF16, stack=st)
            ogv = ogb[:].rearrange("p (k t) -> p k t", t=512)
            mb = sb("c1m", [128, 8 * 512], BF16, stack=st)
            mv = mb[:].rearrange("p (k t) -> p k t", t=512)
            XTv2 = XT.rearrange("(kc p) t -> p kc t", p=128)
            for (run, t0, w) in own_tiles:
                s_, oo, off = run["seq"], run["ooff"] + t0, run["off"] + 2 + t0
                ATTv = ATT.rearrange("(kc p) t -> p kc t", p=128)
                sch.dma("sp", attv[:, :, :w], ATTv[:, :, oo:oo + w], [], ["c1att"])
                for h in range(8):
                    a_, ak_ = f32r.next()
                    b_, bk_ = f32r.next()
                    z_, zk_ = f32r.next()
                    sch.dma("sp", a_[:, :w], OF[h * 128:(h + 1) * 128, oo:oo + w], [], [ak_])
                    sch.dma("sp", b_[:, :w], OB[h * 128:(h + 1) * 128, oo:oo + w], [], [bk_])
                    sch.dma("sp", z_[:, :w], ZS[h * 128:(h + 1) * 128, oo:oo + w], [], [zk_])
                    sch.op("pool", [ak_, bk_], [ak_], lambda e: e.tensor_tensor(out=a_[:, :w], in0=a_[:, :w], in1=b_[:, :w], op=ALU.add))
                    sch.op("act", [ak_], [bk_], lambda e: e.activation(out=b_[:, :w], in_=a_[:, :w], func=AF.Square))
                    ps, psk = psring.next()
                    sch.op("pe", [bk_, "consts"], [psk], lambda e: e.matmul(ps[:, :w], lhsT=ones, rhs=b_[:, :w], start=True, stop=True))
                    t_, tk_ = f32r.next()
                    rstd_from_ps(ps, psk, w, 1.0 / 128, b_, bk_, t_, tk_)
                    sch.op("dve", [ak_, bk_, "dnw"], [ak_],
                           lambda e: e.scalar_tensor_tensor(out=a_[:, :w], in0=a_[:, :w], scalar=dnw[:, 0:1], in1=b_[:, :w],
                                                            op0=ALU.mult, op1=ALU.mult))
                    sch.op("dve", [ak_, zk_], [("c1og", h)],
                           lambda e: e.tensor_tensor(out=ogv[:, h, :w], in0=a_[:, :w], in1=z_[:, :w], op=ALU.mult))
                for fc in range(8):
                    ga_, gak = f32r.next()
                    gb_, gbk = f32r.next()
                    sch.dma("sp", ga_[:, :w], GATES[fc * 128:(fc + 1) * 128, oo:oo + w], [], [gak])
                    sch.dma("sp", gb_[:, :w], GATES[D + fc * 128:D + (fc + 1) * 128, oo:oo + w], [], [gbk])
                    psa, psak = psring.next()
                    for kc in range(8):
                        sch.op("pe", ["wpa", "c1att"], [psak],
                               lambda e, kc=kc: e.matmul(psa[:, :w], lhsT=wpa[:, kc, fc * 128:(fc + 1) * 128], rhs=attv[:, kc, :w],
                                                         start=(kc == 0), stop=(kc == 7)))
                    psb_, psbk = psring.next()
                    for kc in range(8):
                        sch.op("pe", ["wpb", ("c1og", kc)], [psbk],
                               lambda e, kc=kc: e.matmul(psb_[:, :w], lhsT=wpb[:, kc, fc * 128:(fc + 1) * 128], rhs=ogv[:, kc, :w],
                                                         start=(kc == 0), stop=(kc == 7)))
                    sch.op("dve", [psak, gak], [gak],
                           lambda e: e.tensor_tensor(out=ga_[:, :w], in0=ga_[:, :w], in1=psa[:, :w], op=ALU.mult))
                    sch.op("dve", [psbk, gbk], [gbk],
                           lambda e: e.tensor_tensor(out=gb_[:, :w], in0=gb_[:, :w], in1=psb_[:, :w], op=ALU.mult))
                    sch.op("pool", [gak, gbk], [("c1m", fc)],
                           lambda e: e.tensor_tensor(out=mv[:, fc, :w], in0=ga_[:, :w], in1=gb_[:, :w], op=ALU.add))
                for fc in range(8):
                    x_, xk_ = f32r.next()
                    sch.dma("sp", x_[:, :w], XTv2[:, fc, off:off + w], [], [xk_])
                    ps, psk = psring.next()
                    for kc in range(8):
                        sch.op("pe", ["wo", ("c1m", kc)], [psk],
                               lambda e, kc=kc: e.matmul(ps[:, :w], lhsT=wo[:, kc, fc * 128:(fc + 1) * 128], rhs=mv[:, kc, :w],
                                                         start=(kc == 0), stop=(kc == 7)))
                    sch.op("dve", [psk, xk_, "modt"], [xk_],
                           lambda e: e.scalar_tensor_tensor(out=x_[:, :w], in0=ps[:, :w], scalar=modv[:, 16 + fc, s_:s_ + 1],
                                                            in1=x_[:, :w], op0=ALU.mult, op1=ALU.add))
                    sch.dma("pool", X1[fc * 128:(fc + 1) * 128, oo:oo + w], x_[:, :w], [xk_], [])
            sch.barrier()

        if "C2" in stages:
          with contextlib.ExitStack() as st:
            wf1 = load_w_bf16(sb("wf1", [128, 8 * 2 * DFF], BF16, stack=st), W_F1, 8, 2 * DFF, "wf1")
            wf2 = load_w_bf16(sb("wf2", [128, 22 * D], BF16, stack=st), W_F2, 22, D, "wf2")
            fnw = sb("fnw", [128, 8], stack=st)
            sch.dma("sp", fnw[:], FNWT[:, :], [], ["fnw"])
            x1b = sb("c2x1", [128, 8 * 512], stack=st)
            x1v = x1b[:].rearrange("p (k t) -> p k t", t=512)
            h2b = sb("c2h2", [128, 8 * 512], BF16, stack=st)
            h2v = h2b[:].rearrange("p (k t) -> p k t", t=512)
            acb = sb("c2ac", [128, 22 * 512], BF16, stack=st)
            acv = acb[:].rearrange("p (k t) -> p k t", t=512)
            f32r = Ring([(sb(f"c2f{i}", [128, 512], stack=st), ("c2f", i)) for i in range(6)])
            rsA = sb("c2rsA", [128, 512], stack=st)
            rsB = sb("c2rsB", [128, 512], stack=st)
            X1v = X1.rearrange("(kc p) t -> p kc t", p=128)
            for (run, t0, w) in own_tiles:
                s_, oo = run["seq"], run["ooff"] + t0
                sch.dma("sp", x1v[:, :, :w], X1v[:, :, oo:oo + w], [], ["c2x1"])
                psS, pskS = psring.next()
                for kc in range(8):
                    q_, qk2 = f32r.next()
                    sch.op("act", ["c2x1"], [qk2], lambda e, kc=kc, q_=q_: e.activation(out=q_[:, :w], in_=x1v[:, kc, :w], func=AF.Square))
                    sch.op("pe", [qk2, "consts"], [pskS],
                           lambda e, kc=kc, q_=q_: e.matmul(psS[:, :w], lhsT=ones, rhs=q_[:, :w], start=(kc == 0), stop=(kc == 7)))
                rs_, rsk = rsA, "c2rsA"
                t_, tk_ = f32r.next()
                rstd_from_ps(psS, pskS, w, 1.0 / D, rs_, rsk, t_, tk_)
                for kc in range(8):
                    t_, tk_ = f32r.next()
                    sch.op("dve", ["c2x1", rsk], [tk_],
                           lambda e, kc=kc, t_=t_: e.tensor_tensor(out=t_[:, :w], in0=x1v[:, kc, :w], in1=rs_[:, :w], op=ALU.mult))
                    sch.op("act", [tk_, "a2", "modt"], [("c2h2", kc)],
                           lambda e, kc=kc, t_=t_: e.activation(out=h2v[:, kc, :w], in_=t_[:, :w], func=AF.Identity,
                                                                scale=a2v[:, kc, s_:s_ + 1], bias=modv[:, 24 + kc, s_:s_ + 1]))
                for j in range(22):
                    psu, psuk = psring.next()
                    for kc in range(8):
                        sch.op("pe", ["wf1", ("c2h2", kc)], [psuk],
                               lambda e, kc=kc: e.matmul(psu[:, :w], lhsT=wf1[:, kc, j * 128:(j + 1) * 128], rhs=h2v[:, kc, :w],
                                                         start=(kc == 0), stop=(kc == 7)))
                    psv_, psvk = psring.next()
                    for kc in range(8):
                        sch.op("pe", ["wf1", ("c2h2", kc)], [psvk],
                               lambda e, kc=kc: e.matmul(psv_[:, :w], lhsT=wf1[:, kc, DFF + j * 128:DFF + (j + 1) * 128],
                                                         rhs=h2v[:, kc, :w], start=(kc == 0), stop=(kc == 7)))
                    su, suk = f32r.next()
                    sch.op("act", [psuk], [suk], lambda e: e.activation(out=su[:, :w], in_=psu[:, :w], func=AF.Silu))
                    sch.op("dve", [suk, psvk], [("c2ac", j)],
                           lambda e: e.tensor_tensor(out=acv[:, j, :w], in0=su[:, :w], in1=psv_[:, :w], op=ALU.mult))
                for fc in range(8):
                    ps, psk = psring.next()
                    for j in range(22):
                        sch.op("pe", ["wf2", ("c2ac", j)], [psk],
                               lambda e, j=j: e.matmul(ps[:, :w], lhsT=wf2[:, j, fc * 128:(fc + 1) * 128], rhs=acv[:, j, :w],
                                                       start=(j == 0), stop=(j == 21)))
                    sch.op("dve", [psk, "c2x1", "modt"], ["c2x1"],
                           lambda e: e.scalar_tensor_tensor(out=x1v[:, fc, :w], in0=ps[:, :w], scalar=modv[:, 40 + fc, s_:s_ + 1],
                                                            in1=x1v[:, fc, :w], op0=ALU.mult, op1=ALU.add))
                psS2, pskS2 = psring.next()
                for fc in range(8):
                    q_, qk2 = f32r.next()
                    sch.op("act", ["c2x1"], [qk2], lambda e, q_=q_: e.activation(out=q_[:, :w], in_=x1v[:, fc, :w], func=AF.Square))
                    sch.op("pe", [qk2, "consts"], [pskS2],
                           lambda e, q_=q_: e.matmul(psS2[:, :w], lhsT=ones, rhs=q_[:, :w], start=(fc == 0), stop=(fc == 7)))
                rs_, rsk = rsB, "c2rsB"
                t_, tk_ = f32r.next()
                rstd_from_ps(psS2, pskS2, w, 1.0 / D, rs_, rsk, t_, tk_)
                for fc in range(8):
                    y_, yk_ = f32r.next()
                    sch.op("dve", ["c2x1", rsk, "fnw"], [yk_],
                           lambda e: e.scalar_tensor_tensor(out=y_[:, :w], in0=x1v[:, fc, :w], scalar=fnw[:, fc:fc + 1], in1=rs_[:, :w],
                                                            op0=ALU.mult, op1=ALU.mult))
                    sch.dma("pool", YT[fc * 128:(fc + 1) * 128, oo:oo + w], y_[:, :w], [yk_], [])
            sch.barrier()

        sch.barrier()
    nc._ninstr = sch.ninstr
    nc._needed = sch.needed
    nc._nincs = sum(len(v) for v in sch.map.values())
    nc._nops = dict(sch.idx)
    nc._declared = declared
    return nc


def rope_tables_T(pos):
    half = 64
    inv = 1.0 / (10000.0 ** (np.arange(0, half, 2, dtype=np.float32) / half))
    row = (pos // 64).astype(np.float32)
    col = (pos % 64).astype(np.float32)
    ar = row[:, None] * inv[None, :]
    ac = col[:, None] * inv[None, :]
    ang = np.concatenate([ar, ar, ac, ac], axis=-1).astype(np.float32)
    return np.cos(ang).T.astype(np.float32), np.sin(ang).T.astype(np.float32)


def make_consts():
    c = np.zeros((128, 10 * 128), np.float32)
    c[:, 0:128] = np.eye(128, dtype=np.float32)
    c[:, 128:256] = 1.0
    R = np.zeros((128, 128), np.float32)
    for a in range(2):
        for cc in range(32):
            R[a * 64 + 32 + cc, a * 64 + cc] = -1.0
            R[a * 64 + cc, a * 64 + 32 + cc] = 1.0
    c[:, 256:384] = R
    i = np.arange(64)
    low_strict = (i[:, None] > i[None, :]).astype(np.float32)
    low_inc = (i[:, None] >= i[None, :]).astype(np.float32)

    def bd(m):
        z = np.zeros((128, 128), np.float32)
        z[:64, :64] = m
        z[64:, 64:] = m
        return z
    c[:, 384:512] = bd(low_strict)
    c[:, 512:640] = bd(low_strict.T)
    c[:, 640:768] = bd(low_inc.T)
    c[:, 768:896] = bd(low_inc)
    c[:, 896:1024] = bd(low_inc.T)
    c[:, 1024:1152] = bd(low_inc)
    c[:, 1152:1280] = bd(np.ones((64, 64), np.float32))
    return c


def host_inputs(cfg, core, x_prompt, x_sample, c_prompt, c_sample, norm1_w, norm2_w, w_ada, b_ada, w_in,
                q_norm_w, k_norm_w, conv_w, a_log, dt_bias, dn_norm_w, w_proj_a, w_proj_b, w_out,
                w_ffn_in, w_ffn_out, final_norm_w, shared):
    SP, SEG = cfg["SP"], cfg["SEG"]
    runs = cfg["runs"]
    NTOT = cfg["NTOT"]
    NR = len(runs)
    XTm = np.zeros((D, NTOT), np.float32)
    pos = np.zeros((NTOT,), np.int64)
    FLm = np.zeros((NR, 8), np.float32)
    cwT = np.ascontiguousarray(conv_w[0].T.reshape(24, 128, 5).transpose(1, 0, 2))
    CWm = np.zeros((128, NR, 24, 5), np.float32)
    xs = x_sample[0]
    S_S = xs.shape[0]
    i = core
    for run in runs:
        n, off, r = run["n"], run["off"], run["idx"]
        if run["name"] in ("pA", "pB"):
            xx = x_prompt[2 * core + (0 if run["name"] == "pA" else 1)]
            XTm[:, off + 2: off + 2 + n] = xx.T
            pos[off + 2: off + 2 + n] = np.arange(n)
            CWm[:, r] = cwT
        elif run["name"] == "own":
            a, b = i * SEG, (i + 1) * SEG
            lo, hi = max(a - 2, 0), min(b + 2, S_S)
            XTm[:, off + 2 - (a - lo): off + 2 + n + (hi - b)] = xs[lo:hi].T
            pos[off + 2: off + 2 + n] = np.arange(a, b)
            FLm[r, 0] = 1.0 if a > 0 else 0.0
            FLm[r, 1] = 1.0 if b < S_S else 0.0
            CWm[:, r] = cwT
        else:
            k = r - 3
            if k < i:
                seg = k
                a, b = seg * SEG, (seg + 1) * SEG
                idx = np.arange(a - 2, b + 2)
                fwd = True
            else:
                seg = 7 - (k - i)
                a, b = seg * SEG, (seg + 1) * SEG
                idx = np.arange(b + 1, a - 3, -1)
                fwd = False
            valid = (idx >= 0) & (idx < S_S)
            cols = off + np.arange(n + 4)
            XTm[:, cols[valid]] = xs[idx[valid]].T
            pos[cols[valid]] = idx[valid]
            FLm[r, 0] = 1.0 if valid[0] else 0.0
            FLm[r, 1] = 1.0 if valid[-1] else 0.0
            FLm[r, 2] = 0.0 if (k == 0 or k == i) else 1.0
            FLm[r, 3] = 1.0 if (k == i - 1) else 0.0
            FLm[r, 4] = 1.0 if (k == 6 and i <= 6) else 0.0
            FLm[r, 5] = 1.0 if fwd else 0.0
            FLm[r, 6] = 0.0 if fwd else 1.0
            CWm[:, r] = cwT if fwd else cwT[:, :, ::-1]
    cosT, sinT = rope_tables_T(pos)
    cseq = np.stack([c_prompt[2 * core], c_prompt[2 * core + 1], c_sample[0]], axis=0)
    CTm = np.ascontiguousarray(cseq.reshape(3, 8, 128).transpose(2, 1, 0)).reshape(128, 24)
    m = dict(shared)
    m.update({
        "XT": XTm, "COS": np.ascontiguousarray(cosT), "SIN": np.ascontiguousarray(sinT),
        "FL": np.ascontiguousarray(np.broadcast_to(FLm.reshape(1, NR * 8), (128, NR * 8))),
        "CW": np.ascontiguousarray(CWm.reshape(128, NR * 24 * 5)),
        "CT": CTm,
    })
    return m


def host_shared(norm1_w, norm2_w, w_ada, b_ada, w_in, q_norm_w, k_norm_w, a_log, dt_bias, dn_norm_w,
                w_proj_a, w_proj_b, w_out, w_ffn_in, w_ffn_out, final_norm_w):
    def vT(v, nch):
        return np.ascontiguousarray(np.asarray(v, np.float32).reshape(nch, 128).T)
    rep = lambda v: np.ascontiguousarray(np.broadcast_to(np.asarray(v, np.float32).reshape(1, -1), (128, v.size)))
    return {
        "w_ada": np.ascontiguousarray(w_ada[0]), "b_adaT": vT(b_ada[0], 48),
        "n1wT": vT(norm1_w[0], 8), "n2wT": vT(norm2_w[0], 8), "fnwT": vT(final_norm_w, 8),
        "w_in": np.ascontiguousarray(w_in[0]),
        "qknw": np.ascontiguousarray(np.stack([q_norm_w[0], k_norm_w[0]], axis=1).astype(np.float32)),
        "dnw": np.ascontiguousarray(dn_norm_w[0].reshape(128, 1).astype(np.float32)),
        "alog_rep": rep(a_log[0]), "dtb_rep": rep(dt_bias[0]),
        "w_proj_a": np.ascontiguousarray(w_proj_a[0]), "w_proj_b": np.ascontiguousarray(w_proj_b[0]),
        "w_out": np.ascontiguousarray(w_out[0]), "w_ffn_in": np.ascontiguousarray(w_ffn_in[0]),
        "w_ffn_out": np.ascontiguousarray(w_ffn_out[0]),
        "consts": make_consts(),
    }


def run_kernel(cfg, inputs, debug=False, stages=("A", "CONV", "DN", "ATT", "C1", "C2"), trace=False):
    inputs = {k: np.asarray(v) for k, v in inputs.items()}
    nc = build(cfg, debug=debug, stages=stages)
    wkeys = ["norm1_w", "norm2_w", "w_ada", "b_ada", "w_in", "q_norm_w", "k_norm_w", "a_log", "dt_bias",
             "dn_norm_w", "w_proj_a", "w_proj_b", "w_out", "w_ffn_in", "w_ffn_out", "final_norm_w"]
    shared = host_shared(**{k: inputs[k] for k in wkeys})
    in_maps = [host_inputs(cfg, c, shared=shared, **inputs) for c in range(NCORES)]
    in_maps = [{k: v for k, v in m.items() if k in nc._declared} for m in in_maps]
    res = run_bass_kernel_spmd(nc, in_maps, core_ids=list(range(NCORES)), trace=trace)
    return res, nc


def kernel(**inputs):
    cfg = make_cfg()
    res, _ = run_kernel(cfg, inputs)
    SP, SEG = cfg["SP"], cfg["SEG"]
    B = inputs["x_prompt"].shape[0]
    y_prompt = np.zeros((B, SP, D), np.float32)
    y_sample = np.zeros((1, 8 * SEG, D), np.float32)
    for c in range(NCORES):
        yt = np.asarray(res.results[c]["YT"])
        y_prompt[2 * c] = yt[:, 0:SP].T
        y_prompt[2 * c + 1] = yt[:, SP:2 * SP].T
        y_sample[0, c * SEG:(c + 1) * SEG] = yt[:, 2 * SP:2 * SP + SEG].T
    return (y_prompt, y_sample)
```

```python
import contextlib
import os
import numpy as np
import concourse.bass as bass
import concourse.mybir as mybir
from concourse.bass_utils import run_bass_kernel_spmd

F32 = mybir.dt.float32
BF16 = mybir.dt.bfloat16
AF = mybir.ActivationFunctionType
ALU = mybir.AluOpType

D = 1024
NKC = 8
N_IN = 7712
DFF = 2816
EPS = 1e-6
NCORES = 8
EPOCH = 30000
ATTACH_WAITS = os.environ.get('ATTACH_WAITS', '0') == '1'


class Sched:
    def __init__(self, nc, es, n_dma_sems=32, needed=None):
        self.nc = nc
        self.es = es
        self.eng = {"pe": nc.tensor, "act": nc.scalar, "dve": nc.vector, "pool": nc.gpsimd, "sp": nc.sync}
        self.sems = {}
        self.idx = {e: 0 for e in ["pe", "act", "dve", "pool"]}
        self.cnt = {e: 0 for e in self.idx}
        self.epoch = {e: 0 for e in self.idx}
        self.map = {e: {} for e in self.idx}
        self.needed_in = needed
        self.needed = {e: set() for e in self.idx}
        self.dsem = [es.enter_context(nc.semaphore(f"dq{i}")) for i in range(n_dma_sems)]
        self.dcnt = [0] * n_dma_sems
        self.dnext = 0
        self.waited = {}
        self.lastw = {}
        self.rd = {}
        self.ninstr = 0
        self.pending = None

    def _sem(self, key):
        if key[0] == "d":
            return self.dsem[key[1]]
        if key not in self.sems:
            self.sems[key] = self.es.enter_context(self.nc.semaphore(f"s_{key[0]}_{key[1]}"))
        return self.sems[key]

    def _resolve(self, key, val):
        if key[0] == "d":
            return self.dsem[key[1]], val
        f = key[1]
        self.needed[f].add(val)
        if self.needed_in is not None:
            assert val in self.needed_in[f], "two-pass mismatch"
        ep, c = self.map[f][val]
        return self._sem((f, ep)), c

    def _emit_wait(self, e, key, val):
        if self.pending is not None:
            self.pending.append((key, val))
            return
        sem, v = self._resolve(key, val)
        self.eng[e].wait_ge(sem, v)
        self.ninstr += 1

    def _wait(self, e, ev):
        if ev is None:
            return
        key, val = ev
        if e == "pe" and key == ("c", "pe"):
            return
        k = (e, key)
        if self.waited.get(k, 0) >= val:
            return
        self.waited[k] = val
        self._emit_wait(e, key, val)

    def _deps(self, e, reads, writes):
        for r in reads:
            self._wait(e, self.lastw.get(r))
            if isinstance(r, tuple) and r[0] == "ps":
                for key, val in self.rd.get(r, {}).items():
                    if key != ("c", e):
                        self._wait(e, (key, val))
        for w in writes:
            self._wait(e, self.lastw.get(w))
            for key, val in self.rd.get(w, {}).items():
                self._wait(e, (key, val))

    def _record(self, ev, reads, writes):
        key, val = ev
        for r in reads:
            d = self.rd.setdefault(r, {})
            if d.get(key, 0) < val:
                d[key] = val
        for w in writes:
            self.lastw[w] = ev
            self.rd[w] = {}

    def _with_waits(self, e, reads, writes, fn, pre=None):
        self.pending = [] if ATTACH_WAITS else None
        if pre is not None:
            self._wait(e, pre)
        self._deps(e, reads, writes)
        pend, self.pending = self.pending, None
        if pend:
            for key, val in pend[:-1]:
                self._emit_wait(e, key, val)
        ins = fn(self.eng[e])
        if pend:
            sem, v = self._resolve(*pend[-1])
            ins._wait_ge(sem, v)
        return ins

    def op(self, e, reads, writes, fn):
        ins = self._with_waits(e, reads, writes, fn)
        self.idx[e] += 1
        i = self.idx[e]
        if self.needed_in is None or i in self.needed_in[e]:
            if self.cnt[e] >= EPOCH:
                self.epoch[e] += 1
                self.cnt[e] = 0
            self.cnt[e] += 1
            ins.then_inc(self._sem((e, self.epoch[e])), 1)
            self.map[e][i] = (self.epoch[e], self.cnt[e])
        self.ninstr += 1
        self._record((("c", e), i), reads, writes)

    def dma(self, q, out, in_, reads, writes):
        i = self.dnext
        self.dnext = (self.dnext + 1) % len(self.dsem)
        pre = (("d", i), self.dcnt[i]) if self.dcnt[i] > 0 else None
        ins = self._with_waits(q, reads, writes, lambda eng: eng.dma_start(out=out, in_=in_), pre=pre)
        self.dcnt[i] += 16
        assert self.dcnt[i] < 60000
        ins.then_inc(self.dsem[i], 16)
        self.ninstr += 1
        self._record((("d", i), self.dcnt[i]), reads, writes)

    def barrier(self):
        evs = []
        for f in self.idx:
            if self.idx[f] > 0:
                evs.append((("c", f), self.idx[f]))
        for i, v in enumerate(self.dcnt):
            if v > 0:
                evs.append((("d", i), v))
        for e in ["pe", "act", "dve", "pool", "sp"]:
            for key, val in evs:
                if key == ("c", e):
                    continue
                k = (e, key)
                if self.waited.get(k, 0) >= val:
                    continue
                self.waited[k] = val
                self._emit_wait(e, key, val)
        self.lastw = {}
        self.rd = {}


class Ring:
    def __init__(self, items):
        self.items = items
        self.i = 0

    def next(self):
        it = self.items[self.i]
        self.i = (self.i + 1) % len(self.items)
        return it


def make_cfg(SP=4096, SEG=2048):
    cfg = dict(SP=SP, SEG=SEG)
    runs = []
    off = 0
    for name, n, full, seq in [("pA", SP, True, 0), ("pB", SP, True, 1), ("own", SEG, True, 2)] + [
        (f"o{r}", SEG, False, 2) for r in range(7)
    ]:
        runs.append(dict(name=name, n=n, full=full, seq=seq, off=off, idx=len(runs)))
        off += n + 4
    cfg["runs"] = runs
    cfg["NTOT"] = off
    cfg["NOWN"] = 2 * SP + SEG
    o = 0
    for r in runs[:3]:
        r["ooff"] = o
        o += r["n"]
    cfg["ctx_n"] = [SP, SP, 8 * SEG]
    runs[0]["ctx"], runs[0]["coff"] = 0, 0
    runs[1]["ctx"], runs[1]["coff"] = 1, 0
    runs[2]["ctx"], runs[2]["coff"] = 2, 0
    for r in range(7):
        runs[3 + r]["ctx"], runs[3 + r]["coff"] = 2, (1 + r) * SEG
    return cfg


def tiles_of(n, w=512):
    return [(t0, min(w, n - t0)) for t0 in range(0, n, w)]


def build(cfg, debug=False, stages=("A", "CONV", "DN", "ATT", "C1", "C2"), needed=None, two_pass=True):
    if two_pass and needed is None:
        dry = build(cfg, debug=debug, stages=stages, two_pass=False)
        needed = dry._needed
    nc = bass.Bass("TRN2", target_bir_lowering=False)
    runs = cfg["runs"]
    NTOT = cfg["NTOT"]
    NOWN = cfg["NOWN"]
    NR = len(runs)
    okind = "ExternalOutput" if debug else "Internal"

    declared = set()

    def din(name, shape, dt=F32, need=True):
        if not need:
            return None
        declared.add(name)
        return nc.dram_tensor(name, list(shape), dt, kind="ExternalInput").ap()

    dbg_names = os.environ.get("DBG_OUT", "").split(",") if debug else []

    def dscr(name, shape, dt=F32):
        kind = "ExternalOutput" if (debug and (name in dbg_names or dbg_names == ["all"])) else "Internal"
        return nc.dram_tensor(name, list(shape), dt, kind=kind).ap()

    XT = din("XT", [D, NTOT], need=("A" in stages or "C1" in stages))
    COS = din("COS", [128, NTOT])
    SIN = din("SIN", [128, NTOT])
    FL = din("FL", [128, NR * 8])
    CW = din("CW", [128, NR * 24 * 5])
    CT = din("CT", [128, 8 * 3])
    W_ADA = din("w_ada", [D, 6 * D], need="S0" in stages or "A" in stages or "C1" in stages or "C2" in stages)
    B_ADAT = din("b_adaT", [128, 48])
    N1WT = din("n1wT", [128, 8])
    N2WT = din("n2wT", [128, 8])
    FNWT = din("fnwT", [128, 8])
    W_IN = din("w_in", [D, N_IN], need="A" in stages)
    QKNW = din("qknw", [128, 2])
    DNW = din("dnw", [128, 1])
    ALOG = din("alog_rep", [128, 16])
    DTB = din("dtb_rep", [128, 16])
    W_PA = din("w_proj_a", [D, D], need="C1" in stages)
    W_PB = din("w_proj_b", [D, D], need="C1" in stages)
    W_OUT = din("w_out", [D, D], need="C1" in stages)
    W_F1 = din("w_ffn_in", [D, 2 * DFF], need="C2" in stages)
    W_F2 = din("w_ffn_out", [DFF, D], need="C2" in stages)
    CONSTS = din("consts", [128, 10 * 128])
    YT = nc.dram_tensor("YT", [D, NOWN], F32, kind="ExternalOutput").ap()
    QT = dscr("QT", [8 * 128, NOWN], BF16)
    KT = [dscr(f"KT{c}", [2 * 128, cfg["ctx_n"][c]], BF16) for c in range(3)]
    VV = [dscr(f"VV{c}", [cfg["ctx_n"][c], 256], BF16) for c in range(3)]
    DPRE3 = [dscr(f"DPRE{i}", [1024, NTOT]) for i in range(3)]
    ZS = dscr("ZS", [D, NOWN])
    GATES = dscr("GATES", [2 * D, NOWN])
    BG = dscr("BG", [NTOT, 33])
    DQKV3 = [dscr(f"DQKV{i}", [1024, NTOT]) for i in range(3)]
    OF = dscr("OF", [D, NOWN])
    OB = dscr("OB", [D, NOWN])
    ATT = dscr("ATT", [D, NOWN], BF16)
    X1 = dscr("X1", [D, NOWN])

    with contextlib.ExitStack() as es:
        sch = Sched(nc, es, needed=needed)

        def sb(name, shape, dt=F32, stack=es):
            return stack.enter_context(nc.sbuf_tensor("sb_" + name, list(shape), dt))

        psb = [es.enter_context(nc.psum_tensor(f"ps{i}", [128, 512], F32)) for i in range(8)]
        psring = Ring([(psb[i], ("ps", i)) for i in range(8)])

        consts = sb("consts", [128, 10 * 128])
        sch.dma("sp", consts[:], CONSTS[:, :], [], ["consts"])
        ident = consts[:, 0:128]
        ones = consts[:, 128:256]
        rrot = consts[:, 256:384]
        fl = sb("fl", [128, NR * 8])
        sch.dma("sp", fl[:], FL[:, :], [], ["fl"])
        modt = sb("modt", [128, 48 * 3])
        a1 = sb("a1", [128, 8 * 3])
        a2 = sb("a2", [128, 8 * 3])
        modv = modt[:].rearrange("p (f s) -> p f s", s=3)
        a1v = a1[:].rearrange("p (k s) -> p k s", s=3)
        a2v = a2[:].rearrange("p (k s) -> p k s", s=3)

        def flag(r, j):
            return fl[:, r * 8 + j: r * 8 + j + 1]

        with contextlib.ExitStack() as st:
          if W_ADA is not None:
              ct = sb("ct", [128, 24], stack=st)
              sct = sb("sct", [128, 24], stack=st)
              badat = sb("badat", [128, 48], stack=st)
              n1wt = sb("n1wt", [128, 8], stack=st)
              n2wt = sb("n2wt", [128, 8], stack=st)
              wab = [sb(f"wab{i}", [128, 8 * 768], stack=st) for i in range(2)]
              sch.dma("sp", ct[:], CT[:, :], [], ["ct"])
              sch.dma("sp", badat[:], B_ADAT[:, :], [], ["badat"])
              sch.dma("sp", n1wt[:], N1WT[:, :], [], ["n1wt"])
              sch.dma("sp", n2wt[:], N2WT[:, :], [], ["n2wt"])
              sch.op("act", ["ct"], ["sct"], lambda e: e.activation(out=sct[:], in_=ct[:], func=AF.Silu))
              sctv = sct[:].rearrange("p (k s) -> p k s", s=3)
              ps, psk = psring.next()
              W_ADAv = W_ADA.rearrange("(kc p) c -> p kc c", p=128)
              for blk in range(8):
                  wa = wab[blk % 2]
                  wav = wa[:].rearrange("p (k c) -> p k c", c=768)
                  sch.dma("sp", wav, W_ADAv[:, :, blk * 768:(blk + 1) * 768], [], [("wab", blk % 2)])
                  for f6 in range(6):
                      fc = blk * 6 + f6
                      for kc in range(8):
                          sch.op("pe", [("wab", blk % 2), "sct"], [psk],
                                 lambda e, fc=fc, kc=kc, f6=f6, wav=wav: e.matmul(
                                     ps[:, fc * 3:(fc + 1) * 3], lhsT=wav[:, kc, f6 * 128:(f6 + 1) * 128],
                                     rhs=sctv[:, kc, :], start=(kc == 0), stop=(kc == 7)))
              psv = ps[:, 0:144].rearrange("p (f s) -> p f s", s=3)
              for s in range(3):
                  sch.op("dve", [psk, "badat"], ["modt"],
                         lambda e, s=s: e.tensor_tensor(out=modv[:, :, s], in0=psv[:, :, s], in1=badat[:], op=ALU.add))
              for s in range(3):
                  sch.op("dve", ["modt", "n1wt"], ["a1"],
                         lambda e, s=s: e.scalar_tensor_tensor(out=a1v[:, :, s], in0=modv[:, 8:16, s], scalar=1.0,
                                                               in1=n1wt[:], op0=ALU.add, op1=ALU.mult))
                  sch.op("dve", ["modt", "n2wt"], ["a2"],
                         lambda e, s=s: e.scalar_tensor_tensor(out=a2v[:, :, s], in0=modv[:, 32:40, s], scalar=1.0,
                                                               in1=n2wt[:], op0=ALU.add, op1=ALU.mult))
              sch.barrier()

        def rstd_from_ps(psS, pskS, w, scale, dst, dstk, tmp, tmpk):
            sch.op("act", [pskS], [tmpk],
                   lambda e: e.activation(out=tmp[:, :w], in_=psS[:, :w], func=AF.Ln, scale=scale, bias=epsb[:, 0:1]))
            sch.op("act", [tmpk], [dstk],
                   lambda e: e.activation(out=dst[:, :w], in_=tmp[:, :w], func=AF.Exp, scale=-0.5))

        epsb = sb("epsb", [128, 2])
        sch.op("dve", [], ["epsb"], lambda e: e.memset(epsb[:, 0:1], EPS))
        sch.op("dve", [], ["epsb"], lambda e: e.memset(epsb[:, 1:2], 1.0))

        if "A" in stages:
          with contextlib.ExitStack() as st:
            win = sb("win", [128, 8 * N_IN], BF16, stack=st)
            winv = win[:].rearrange("p (k c) -> p k c", c=N_IN)
            W_INv = W_IN.rearrange("(kc p) c -> p kc c", p=128)
            for kc in range(8):
                for c0 in range(0, N_IN, 1928):
                    sch.dma("pool", winv[:, kc, c0:c0 + 1928], W_INv[:, kc, c0:c0 + 1928], [], ["win"])
            qknw = sb("qknw", [128, 2], stack=st)
            sch.dma("sp", qknw[:], QKNW[:, :], [], ["qknw"])
            alog = sb("alog", [128, 16], stack=st)
            dtb = sb("dtb", [128, 16], stack=st)
            nega = sb("nega", [128, 16], stack=st)
            sch.dma("sp", alog[:], ALOG[:, :], [], ["alog"])
            sch.dma("sp", dtb[:], DTB[:, :], [], ["dtb"])
            sch.op("act", ["alog"], ["nega0"], lambda e: e.activation(out=nega[:], in_=alog[:], func=AF.Exp))
            sch.op("dve", ["nega0"], ["nega"],
                   lambda e: e.tensor_scalar(out=nega[:], in0=nega[:], scalar1=-1.0, scalar2=None, op0=ALU.mult))
            xts = [sb(f"xt{i}", [128, 8 * 512], stack=st) for i in range(1)]
            hts = [sb(f"ht{i}", [128, 8 * 512], BF16, stack=st) for i in range(2)]
            sqs = Ring([(sb(f"sq{i}", [128, 512], stack=st), ("sq", i)) for i in range(3)])
            tmps = Ring([(sb(f"tmpa{i}", [128, 512], stack=st), ("tmpa", i)) for i in range(3)])
            rstds = Ring([(sb(f"rstd{i}", [128, 512], stack=st), ("rstd", i)) for i in range(2)])
            xns = Ring([(sb(f"xn{i}", [128, 512], stack=st), ("xn", i)) for i in range(2)])
            t1s = Ring([(sb(f"t1{i}", [128, 512], stack=st), ("t1", i)) for i in range(2)])
            obf = Ring([(sb(f"obf{i}", [128, 512], BF16, stack=st), ("obf", i)) for i in range(3)])
            of32 = Ring([(sb(f"of32{i}", [128, 512], stack=st), ("of32", i)) for i in range(4)])
            coss = [sb(f"cos{i}", [128, 512], stack=st) for i in range(2)]
            sins = [sb(f"sin{i}", [128, 512], stack=st) for i in range(2)]
            vbf = Ring([(sb(f"vbf{i}", [128, 256], BF16, stack=st), ("vbf", i)) for i in range(2)])
            bgt = Ring([(sb(f"bgt{i}", [128, 48], stack=st), ("bgt", i)) for i in range(2)])
            XTv = XT.rearrange("(kc p) t -> p kc t", p=128)
            tix = 0
            for run in runs:
                n, full, s, off = run["n"], run["full"], run["seq"], run["off"]
                ri = run["idx"]
                for (t0, w) in tiles_of(n + 4):
                    b = tix % 2
                    tix += 1
                    xt = xts[0][:].rearrange("p (k t) -> p k t", t=512)
                    ht = hts[b][:].rearrange("p (k t) -> p k t", t=512)
                    kx, kh = ("xt", 0), ("ht", b)
                    g0 = off + t0
                    sch.dma("sp", xt[:, :, :w], XTv[:, :, g0:g0 + w], [], [kx])
                    sch.dma("sp", coss[b][:, :w], COS[:, g0:g0 + w], [], [("cos", b)])
                    sch.dma("sp", sins[b][:, :w], SIN[:, g0:g0 + w], [], [("sin", b)])
                    psS, pskS = psring.next()
                    for kc in range(8):
                        sq, sqk = sqs.next()
                        sch.op("act", [kx], [sqk],
                               lambda e, sq=sq, kc=kc: e.activation(out=sq[:, :w], in_=xt[:, kc, :w], func=AF.Square))
                        sch.op("pe", [sqk, "consts"], [pskS],
                               lambda e, sq=sq, kc=kc: e.matmul(psS[:, :w], lhsT=ones, rhs=sq[:, :w],
                                                                start=(kc == 0), stop=(kc == 7)))
                    rstd, rstdk = rstds.next()
                    tmp, tmpk = tmps.next()
                    rstd_from_ps(psS, pskS, w, 1.0 / D, rstd, rstdk, tmp, tmpk)
                    for kc in range(8):
                        tmp, tmpk = tmps.next()
                        sch.op("dve", [kx, rstdk], [tmpk],
                               lambda e, tmp=tmp, kc=kc: e.tensor_tensor(out=tmp[:, :w], in0=xt[:, kc, :w],
                                                                         in1=rstd[:, :w], op=ALU.mult))
                        sch.op("act", [tmpk, "a1", "modt"], [kh],
                               lambda e, tmp=tmp, kc=kc: e.activation(out=ht[:, kc, :w], in_=tmp[:, :w],
                                                                      func=AF.Identity, scale=a1v[:, kc, s:s + 1],
                                                                      bias=modv[:, 0 + kc, s:s + 1]))

                    def proj(c0, m=128):
                        ps, psk = psring.next()
                        for kc in range(8):
                            sch.op("pe", [kh, "win"], [psk],
                                   lambda e, kc=kc: e.matmul(ps[:m, :w], lhsT=winv[:, kc, c0:c0 + m], rhs=ht[:, kc, :w],
                                                             start=(kc == 0), stop=(kc == 7)))
                        return ps, psk

                    def head_norm_rope(c0, wcol, dst_dram):
                        ps, psk = proj(c0)
                        sq, sqk = sqs.next()
                        sch.op("act", [psk], [sqk], lambda e: e.activation(out=sq[:, :w], in_=ps[:, :w], func=AF.Square))
                        ps2, psk2 = psring.next()
                        sch.op("pe", [sqk, "consts"], [psk2],
                               lambda e: e.matmul(ps2[:, :w], lhsT=ones, rhs=sq[:, :w], start=True, stop=True))
                        tmp, tmpk = tmps.next()
                        rs, rsk = rstds.next()
                        rstd_from_ps(ps2, psk2, w, 1.0 / 128, rs, rsk, tmp, tmpk)
                        xn, xnk = xns.next()
                        sch.op("dve", [psk, rsk, "qknw"], [xnk],
                               lambda e: e.scalar_tensor_tensor(out=xn[:, :w], in0=ps[:, :w], scalar=qknw[:, wcol:wcol + 1],
                                                                in1=rs[:, :w], op0=ALU.mult, op1=ALU.mult))
                        ps3, psk3 = psring.next()
                        sch.op("pe", [xnk, "consts"], [psk3],
                               lambda e: e.matmul(ps3[:, :w], lhsT=rrot, rhs=xn[:, :w], start=True, stop=True))
                        t1, t1k = t1s.next()
                        sch.op("pool", [xnk, ("cos", b)], [t1k],
                               lambda e: e.tensor_tensor(out=t1[:, :w], in0=xn[:, :w], in1=coss[b][:, :w], op=ALU.mult))
                        tmp2, tmp2k = tmps.next()
                        sch.op("dve", [psk3, ("sin", b)], [tmp2k],
                               lambda e: e.tensor_tensor(out=tmp2[:, :w], in0=ps3[:, :w], in1=sins[b][:, :w], op=ALU.mult))
                        ob, obk = obf.next()
                        sch.op("dve", [t1k, tmp2k], [obk],
                               lambda e: e.tensor_tensor(out=ob[:, :w], in0=t1[:, :w], in1=tmp2[:, :w], op=ALU.add))
                        lo, hi = max(t0, 2), min(t0 + w, n + 2)
                        if hi > lo:
                            sch.dma("pool", dst_dram(lo - 2, hi - 2), ob[:, lo - t0:hi - t0], [obk], [])

                    if full:
                        for h in range(8):
                            head_norm_rope(h * 128, 0,
                                           lambda a, bb, h=h: QT[h * 128:(h + 1) * 128, run["ooff"] + a: run["ooff"] + bb])
                    for g in range(2):
                        head_norm_rope(1024 + g * 128, 1,
                                       lambda a, bb, g=g: KT[run["ctx"]][g * 128:(g + 1) * 128,
                                                                         run["coff"] + a: run["coff"] + bb])
                    for sub in range(0, w, 128):
                        sw = min(128, w - sub)
                        ps, psk = psring.next()
                        for kc in range(8):
                            sch.op("pe", [kh, "win"], [psk],
                                   lambda e, kc=kc: e.matmul(ps[:sw, 0:256], lhsT=ht[:, kc, sub:sub + sw],
                                                             rhs=winv[:, kc, 1280:1536], start=(kc == 0), stop=(kc == 7)))
                        for kc in range(8):
                            sch.op("pe", [kh, "win"], [psk],
                                   lambda e, kc=kc: e.matmul(ps[:sw, 256:288], lhsT=ht[:, kc, sub:sub + sw],
                                                             rhs=winv[:, kc, 5632:5664], start=(kc == 0), stop=(kc == 7)))
                        vb, vbk = vbf.next()
                        sch.op("act", [psk], [vbk], lambda e: e.activation(out=vb[:sw, :], in_=ps[:sw, 0:256], func=AF.Copy))
                        lo, hi = max(t0 + sub, 2), min(t0 + sub + sw, n + 2)
                        if hi > lo:
                            c0 = run["coff"]
                            sch.dma("pool", VV[run["ctx"]][c0 + lo - 2:c0 + hi - 2, :],
                                    vb[lo - t0 - sub:hi - t0 - sub, :], [vbk], [])
                        bg, bgk = bgt.next()
                        sch.op("act", [psk], [bgk],
                               lambda e: e.activation(out=bg[:sw, 0:16], in_=ps[:sw, 256:272], func=AF.Sigmoid))
                        sch.op("dve", [psk, "dtb"], [bgk],
                               lambda e: e.tensor_tensor(out=bg[:sw, 32:48], in0=ps[:sw, 272:288], in1=dtb[:sw, 0:16],
                                                         op=ALU.add))
                        sch.op("act", [bgk], [bgk], lambda e: e.activation(out=bg[:sw, 32:48], in_=bg[:sw, 32:48], func=AF.Exp))
                        sch.op("act", [bgk], [bgk],
                               lambda e: e.activation(out=bg[:sw, 32:48], in_=bg[:sw, 32:48], func=AF.Ln, bias=epsb[:sw, 1:2]))
                        sch.op("dve", [bgk, "nega"], [bgk],
                               lambda e: e.tensor_tensor(out=bg[:sw, 16:32], in0=bg[:sw, 32:48], in1=nega[:sw, 0:16], op=ALU.mult))
                        if not full:
                            for cf, cb in ((0, 8), (16, 24)):
                                sch.op("dve", [bgk, "fl"], [bgk],
                                       lambda e, cb=cb: e.tensor_scalar(out=bg[:sw, cb:cb + 8], in0=bg[:sw, cb:cb + 8],
                                                                        scalar1=flag(ri, 6)[:sw, :], scalar2=None, op0=ALU.mult))
                                sch.op("dve", [bgk, "fl"], [bgk],
                                       lambda e, cf=cf, cb=cb: e.scalar_tensor_tensor(
                                           out=bg[:sw, cf:cf + 8], in0=bg[:sw, cf:cf + 8], scalar=flag(ri, 5)[:sw, :],
                                           in1=bg[:sw, cb:cb + 8], op0=ALU.mult, op1=ALU.add))
                        sch.dma("pool", BG[g0 + sub:g0 + sub + sw, 0:32], bg[:sw, 0:32], [bgk], [])
                    for j in range(24):
                        if (not full) and j < 8:
                            continue
                        ps, psk = proj(1536 + j * 128)
                        o, ok = of32.next()
                        if j % 2 == 0:
                            sch.op("act", [psk], [ok], lambda e, o=o, ps=ps: e.activation(out=o[:, :w], in_=ps[:, :w], func=AF.Copy))
                        else:
                            sch.op("dve", [psk], [ok], lambda e, o=o, ps=ps: e.tensor_copy(out=o[:, :w], in_=ps[:, :w]))
                        sch.dma("pool", DPRE3[j // 8][(j % 8) * 128:(j % 8 + 1) * 128, g0:g0 + w], o[:, :w], [ok], [])
                    if full:
                        lo, hi = max(t0, 2), min(t0 + w, n + 2)
                        oo = run["ooff"]
                        for j in range(8):
                            ps, psk = proj(4608 + j * 128)
                            o, ok = of32.next()
                            sch.op("act", [psk], [ok], lambda e, o=o, ps=ps: e.activation(out=o[:, :w], in_=ps[:, :w], func=AF.Silu))
                            if hi > lo:
                                sch.dma("pool", ZS[j * 128:(j + 1) * 128, oo + lo - 2:oo + hi - 2], o[:, lo - t0:hi - t0], [ok], [])
                        for j in range(16):
                            ps, psk = proj(5664 + j * 128)
                            o, ok = of32.next()
                            sch.op("act", [psk], [ok], lambda e, o=o, ps=ps: e.activation(out=o[:, :w], in_=ps[:, :w], func=AF.Sigmoid))
                            if hi > lo:
                                sch.dma("pool", GATES[j * 128:(j + 1) * 128, oo + lo - 2:oo + hi - 2], o[:, lo - t0:hi - t0], [ok], [])
            sch.barrier()

        if "CONV" in stages:
          with contextlib.ExitStack() as st:
            cw = sb("cw", [128, NR * 24 * 5], stack=st)
            sch.dma("sp", cw[:], CW[:, :], [], ["cw"])
            cwv = cw[:].rearrange("p (r j k) -> p r j k", j=24, k=5)
            wins = Ring([(sb(f"cwin{i}", [128, 516], stack=st), ("cwin", i)) for i in range(3)])
            accs = Ring([(sb(f"cacc{i}", [128, 512], stack=st), ("cacc", i)) for i in range(2)])
            sils = Ring([(sb(f"csil{i}", [128, 512], stack=st), ("csil", i)) for i in range(3)])
            sq2 = Ring([(sb(f"csq{i}", [128, 512], stack=st), ("csq", i)) for i in range(2)])
            tm2 = Ring([(sb(f"ctm{i}", [128, 512], stack=st), ("ctm", i)) for i in range(2)])
            rn2 = Ring([(sb(f"crn{i}", [128, 512], stack=st), ("crn", i)) for i in range(2)])
            ou2 = Ring([(sb(f"cou{i}", [128, 512], stack=st), ("cou", i)) for i in range(3)])
            for run in runs:
                n, full, off, ri = run["n"], run["full"], run["off"], run["idx"]
                for j in range(24):
                    if (not full) and j < 8:
                        continue
                    for (t0, w) in tiles_of(n):
                        g0 = off + t0
                        win, wk = wins.next()
                        sch.dma("sp", win[:, :w + 4], DPRE3[j // 8][(j % 8) * 128:(j % 8 + 1) * 128, g0:g0 + w + 4], [], [wk])
                        if t0 == 0:
                            sch.op("dve", [wk, "fl"], [wk],
                                   lambda e: e.tensor_scalar(out=win[:, 0:2], in0=win[:, 0:2], scalar1=flag(ri, 0),
                                                             scalar2=None, op0=ALU.mult))
                        if t0 + w == n:
                            sch.op("dve", [wk, "fl"], [wk],
                                   lambda e: e.tensor_scalar(out=win[:, w + 2:w + 4], in0=win[:, w + 2:w + 4],
                                                             scalar1=flag(ri, 1), scalar2=None, op0=ALU.mult))
                        acc, ak = accs.next()
                        sch.op("dve", [wk, "cw"], [ak],
                               lambda e: e.tensor_scalar(out=acc[:, :w], in0=win[:, 0:w], scalar1=cwv[:, ri, j, 0:1],
                                                         scalar2=None, op0=ALU.mult))
                        for k in range(1, 5):
                            sch.op("dve", [wk, "cw", ak], [ak],
                                   lambda e, k=k: e.scalar_tensor_tensor(out=acc[:, :w], in0=win[:, k:k + w],
                                                                         scalar=cwv[:, ri, j, k:k + 1], in1=acc[:, :w],
                                                                         op0=ALU.mult, op1=ALU.add))
                        sil, sk = sils.next()
                        sch.op("act", [ak], [sk], lambda e: e.activation(out=sil[:, :w], in_=acc[:, :w], func=AF.Silu))
                        dst = DQKV3[j // 8][(j % 8) * 128:(j % 8 + 1) * 128, off + 2 + t0: off + 2 + t0 + w]
                        if j >= 16:
                            sch.dma("pool", dst, sil[:, :w], [sk], [])
                            continue
                        sq, sqk = sq2.next()
                        sch.op("act", [sk], [sqk], lambda e: e.activation(out=sq[:, :w], in_=sil[:, :w], func=AF.Square))
                        ps, psk = psring.next()
                        sch.op("pe", [sqk, "consts"], [psk],
                               lambda e: e.matmul(ps[:, :w], lhsT=ones, rhs=sq[:, :w], start=True, stop=True))
                        tm, tmk = tm2.next()
                        rn, rnk = rn2.next()
                        rstd_from_ps(ps, psk, w, 1.0, rn, rnk, tm, tmk)
                        ou, ouk = ou2.next()
                        cmul = (128.0 ** -0.5) if j < 8 else 1.0
                        sch.op("dve", [sk, rnk], [ouk],
                               lambda e: e.scalar_tensor_tensor(out=ou[:, :w], in0=sil[:, :w], scalar=cmul, in1=rn[:, :w],
                                                                op0=ALU.mult, op1=ALU.mult))
                        sch.dma("pool", dst, ou[:, :w], [ouk], [])
            sch.barrier()

        if "DN" in stages:
          with contextlib.ExitStack() as st:
            MASKA = [consts[:, 384:512], consts[:, 512:640]]
            MASKQ = [consts[:, 640:768], consts[:, 768:896]]
            CUM = [consts[:, 896:1024], consts[:, 1024:1152]]
            ONESBD = consts[:, 1152:1280]
            WIN = int(os.environ.get("DN_WIN", "8"))
            rings = {}

            def RB(name, depth=3, shape=(128, 128), dt=F32):
                if name not in rings:
                    rings[name] = Ring([(sb(f"dn_{name}{i}", list(shape), dt, stack=st), (name, i)) for i in range(depth)])
                return rings[name].next()

            slot_rings = [Ring([(sb(f"dn_l{sl}_{i}", [128, 128], stack=st), ("dnl", sl, i)) for i in range(8)]) for sl in range(WIN)]
            slot_rh = [[(sb(f"dn_rh{sl}_{i}", [128, 128], stack=st), ("dnrh", sl, i)) for i in range(2)] for sl in range(WIN)]
            free_slots = list(range(WIN))
            Sst, S16 = {}, {}
            for h in range(8):
                for d in range(2):
                    Sst[(h, d)] = [sb(f"dn_S{h}_{d}_{i}", [128, 128], stack=st) for i in range(2)]
                    S16[(h, d)] = [sb(f"dn_Sh{h}_{d}_{i}", [128, 128], BF16, stack=st) for i in range(2)]
            Scur = {k: 0 for k in Sst}
            SF = [sb(f"dn_SF{h}", [128, 128], stack=st) for h in range(8)]
            SB = [sb(f"dn_SB{h}", [128, 128], stack=st) for h in range(8)]
            for h in range(8):
                sch.op("pool", [], [("SF", h)], lambda e, h=h: e.memset(SF[h][:], 0.0))
                sch.op("pool", [], [("SB", h)], lambda e, h=h: e.memset(SB[h][:], 0.0))
                sch.op("pool", [], [("Sb", h, 0, 0)], lambda e, h=h: e.memset(Sst[(h, 0)][0][:], 0.0))
            DQv3 = [DQKV3[i].rearrange("(h d) c -> d h c", h=8) for i in range(3)]
            evac_flip = [0]
            chain_pos = {}

            def evac(dst, dstk, src, srck):
                evac_flip[0] ^= 1
                if evac_flip[0]:
                    sch.op("act", [srck], [dstk], lambda e: e.activation(out=dst, in_=src, func=AF.Copy))
                else:
                    sch.op("dve", [srck], [dstk], lambda e: e.tensor_copy(out=dst, in_=src))

            def mm(ps, psk, lhsT, rhs, reads, start=True, stop=True):
                sch.op("pe", list(reads), [psk], lambda e: e.matmul(ps, lhsT=lhsT, rhs=rhs, start=start, stop=stop))

            def tr(ps, psk, in_, reads):
                sch.op("pe", list(reads) + ["consts"], [psk], lambda e: e.transpose(out=ps, in_=in_, identity=ident))

            def load_block(run, p, d, blk, nb):
                full, off = run["full"], run["off"]
                tagd = f"d{d}"
                qkvb, qk = RB("qkvb" + tagd, 2, (128, 2 * 1024))
                kq16, kq16k = RB("kq16" + tagd, 2, (128, 2 * 1024), BF16)
                c0 = off + 2 + blk * 512
                for tq in range(3):
                    if tq == 0 and not full:
                        continue
                    for hh in range(2):
                        src = DQv3[tq][:, 2 * p + hh, c0:c0 + nb * 64].rearrange("d (c t) -> d c t", t=64)
                        if tq == 0:
                            sch.dma("pool", kq16[:, 0:nb * 128].rearrange("p (c h t) -> p h c t", h=2, t=64)[:, hh], src, [], [kq16k])
                        else:
                            sch.dma("sp", qkvb[:, (tq - 1) * 1024:(tq - 1) * 1024 + nb * 128].rearrange("p (c h t) -> p h c t", h=2, t=64)[:, hh],
                                    src, [], [qk])
                sch.op("act", [qk], [kq16k], lambda e: e.activation(out=kq16[:, 1024:1024 + nb * 128], in_=qkvb[:, 0:nb * 128], func=AF.Copy))
                bgb, bk = RB("bgb" + tagd, 2, (128, 8 * 64))
                bgv = bgb[:].rearrange("p (c f) -> p c f", f=64)
                BGr = BG[c0:c0 + nb * 64, :].rearrange("(c t) f -> t c f", t=64)
                for half in range(2):
                    pr = slice(half * 64, half * 64 + 64)
                    sh = 1 if half == 1 else 0
                    sch.dma("sp", bgv[pr, :nb, 0:16], BGr[:, :, 0:16], [], [bk])
                    sch.dma("sp", bgv[pr, :nb, 16:32], BGr[:, :, sh:sh + 16], [], [bk])
                    sch.dma("sp", bgv[pr, :nb, 32:48], BGr[:, :, 16:32], [], [bk])
                    sch.dma("sp", bgv[pr, :nb, 48:64], BGr[:, :, 16 + sh:32 + sh], [], [bk])
                psc, psck = psring.next()
                pst, pstk = psring.next()
                pscv = psc[:, 0:256].rearrange("p (c f) -> p c f", f=32)
                pstv = pst[:, 0:256].rearrange("p (c f) -> p c f", f=32)
                mm(pscv[:, :nb, :], psck, CUM[d], bgv[:, :nb, 32:64], [bk, "consts"])
                mm(pstv[:, :nb, :], pstk, ONESBD, bgv[:, :nb, 32:64], [bk, "consts"])
                sm, smk = RB("small" + tagd, 2, (128, 8 * 256))
                smv = sm[:].rearrange("p (a c f) -> p a c f", a=8, f=32)
                GC, EGC, EDK, EGT, BE, NB, EDKA, EDKB = (smv[:, a, :, :] for a in range(8))
                sch.op("act", [psck], [smk], lambda e: e.activation(out=GC[:, :nb, :], in_=pscv[:, :nb, :], func=AF.Copy))
                sch.op("act", [psck], [smk], lambda e: e.activation(out=EGC[:, :nb, :], in_=pscv[:, :nb, :], func=AF.Exp))
                sch.op("dve", [pstk, smk], [smk],
                       lambda e: e.tensor_tensor(out=EDK[:, :nb, :], in0=pstv[:, :nb, :], in1=GC[:, :nb, :], op=ALU.subtract))
                sch.op("act", [smk], [smk], lambda e: e.activation(out=EDK[:, :nb, :], in_=EDK[:, :nb, :], func=AF.Exp))
                sch.op("act", [pstk], [smk], lambda e: e.activation(out=EGT[:, :nb, :], in_=pstv[:, :nb, :], func=AF.Exp))
                sch.op("dve", [smk, "consts"], [smk],
                       lambda e: e.tensor_scalar(out=EDKA[:, :nb, :], in0=EDK[:, :nb, :], scalar1=ONESBD[:, 0:1], scalar2=None, op0=ALU.mult))
                sch.op("dve", [smk, "consts"], [smk],
                       lambda e: e.tensor_scalar(out=EDKB[:, :nb, :], in0=EDK[:, :nb, :], scalar1=ONESBD[:, 64:65], scalar2=None, op0=ALU.mult))
                sch.op("dve", [bk, smk], [smk],
                       lambda e: e.tensor_tensor(out=BE[:, :nb, :], in0=bgv[:, :nb, 0:32], in1=EGC[:, :nb, :], op=ALU.mult))
                sch.op("dve", [bk], [smk],
                       lambda e: e.tensor_scalar(out=NB[:, :nb, :], in0=bgv[:, :nb, 0:32], scalar1=-1.0, scalar2=None, op0=ALU.mult))
                return dict(qkvb=qkvb, qk=qk, kq16=kq16, kq16k=kq16k, bgv=bgv, bk=bk, GC=GC, EGC=EGC, EDK=EDK, EGT=EGT, BE=BE, NB=NB,
                            EDKA=EDKA, EDKB=EDKB, smk=smk)

            def inst_gen(run, p, d, c, bc, ostctx, tseq):
                full = run["full"]
                nch = run["n"] // 64
                blk, ch = c // 8, c % 8
                nb = min(8, nch - blk * 8)
                slot = free_slots.pop(0)
                slot_rings[slot].i = 0
                TB = slot_rings[slot].next
                qkvb, qk, kq16, kq16k, bgv, bk, smk = bc["qkvb"], bc["qk"], bc["kq16"], bc["kq16k"], bc["bgv"], bc["bk"], bc["smk"]
                colp = 16 + d * 8 + 2 * p
                pcol = lambda A: A[:, ch, colp:colp + 1]
                cs = slice(ch * 64, ch * 64 + 64)
                KTp, VTp = (qkvb[:, tq * 1024 + ch * 128: tq * 1024 + ch * 128 + 128] for tq in (0, 1))
                Q16, K16 = (kq16[:, tq * 1024 + ch * 128: tq * 1024 + ch * 128 + 128] for tq in (0, 1))
                ps_k, ps_kk = psring.next()
                tr(ps_k[:, 0:128], ps_kk, KTp, [qk])
                ps_v, ps_vk = psring.next()
                tr(ps_v[:, 0:128], ps_vk, VTp, [qk])
                dg, dgk = TB()
                sch.op("pool", ["consts", smk], [dgk],
                       lambda e: e.tensor_scalar(out=dg[:], in0=ident, scalar1=pcol(bc["GC"]), scalar2=None, op0=ALU.mult))
                rhsk, rhskk = slot_rh[slot][0]
                sch.op("act", [ps_kk, smk], [rhskk],
                       lambda e: e.activation(out=rhsk[:], in_=ps_k[:, 0:128], func=AF.Identity, scale=pcol(bc["BE"])))
                kda, kdak = RB("kda", WIN + 2, dt=BF16)
                kdb, kdbk = RB("kdb", WIN + 2, dt=BF16)
                sch.op("dve", [ps_kk, smk], [kdak],
                       lambda e: e.tensor_scalar(out=kda[:], in0=ps_k[:, 0:128], scalar1=pcol(bc["EDKA"]), scalar2=None, op0=ALU.mult))
                sch.op("dve", [ps_kk, smk], [kdbk],
                       lambda e: e.tensor_scalar(out=kdb[:], in0=ps_k[:, 0:128], scalar1=pcol(bc["EDKB"]), scalar2=None, op0=ALU.mult))
                rhsv, rhsvk = slot_rh[slot][1]
                sch.op("act", [ps_vk, bk], [rhsvk],
                       lambda e: e.activation(out=rhsv[:], in_=ps_v[:, 0:128], func=AF.Identity, scale=pcol(bgv)))
                yield
                ps_g, ps_gk = psring.next()
                mm(ps_g[:, 0:128], ps_gk, K16, K16, [kq16k])
                ps_r, ps_rk = psring.next()
                mm(ps_r[:, 0:128], ps_rk, ones, dg[:], [dgk, "consts"])
                gm, gmk = TB()
                sch.op("dve", [ps_gk, "consts"], [gmk],
                       lambda e: e.tensor_tensor(out=gm[:], in0=ps_g[:, 0:128], in1=MASKA[d], op=ALU.mult))
                t1, t1k = TB()
                sch.op("dve", [ps_rk, smk], [t1k],
                       lambda e: e.tensor_scalar(out=t1[:], in0=ps_r[:, 0:128], scalar1=pcol(bc["GC"]), scalar2=0.0,
                                                 op0=ALU.subtract, op1=ALU.max))
                if full:
                    ps_q, ps_qk = psring.next()
                    mm(ps_q[:, 0:128], ps_qk, K16, Q16, [kq16k])
                    t2, t2k = TB()
                    sch.op("dve", [ps_rk, smk], [t2k],
                           lambda e: e.tensor_scalar(out=t2[:], in0=ps_r[:, 0:128], scalar1=pcol(bc["GC"]), scalar2=0.0,
                                                     op0=ALU.subtract, op1=ALU.min))
                    erow, erowk = TB()
                    sch.op("act", [ps_rk], [erowk], lambda e: e.activation(out=erow[:], in_=ps_r[:, 0:128], func=AF.Exp))
                    kqm, kqmk = TB()
                    sch.op("dve", [ps_qk, "consts"], [kqmk],
                           lambda e: e.tensor_tensor(out=kqm[:], in0=ps_q[:, 0:128], in1=MASKQ[d], op=ALU.mult))
                yield
                sch.op("act", [t1k], [t1k], lambda e: e.activation(out=t1[:], in_=t1[:], func=AF.Exp, scale=-1.0))
                b0, b0k = TB()
                sch.op("dve", [gmk, t1k, smk], [b0k],
                       lambda e: e.scalar_tensor_tensor(out=b0[:], in0=gm[:], scalar=pcol(bc["NB"]), in1=t1[:], op0=ALU.mult, op1=ALU.mult))
                if full:
                    sch.op("act", [t2k], [t2k], lambda e: e.activation(out=t2[:], in_=t2[:], func=AF.Exp))
                    aqt, aqtk = RB("aqt", WIN + 2, dt=BF16)
                    sch.op("pool", [kqmk, t2k], [aqtk], lambda e: e.tensor_tensor(out=aqt[:], in0=kqm[:], in1=t2[:], op=ALU.mult))
                    qet, qetk = RB("qet", WIN + 2, dt=BF16)
                    sch.op("pool", [kq16k, erowk], [qetk],
                           lambda e: e.tensor_tensor(out=qet[:], in0=Q16, in1=erow[:], op=ALU.mult))
                yield
                ps_t, ps_tk = psring.next()
                tr(ps_t[:, 0:128], ps_tk, b0[:], [b0k])
                bt, btk = TB()
                sch.op("act", [ps_tk], [btk], lambda e: e.activation(out=bt[:], in_=ps_t[:, 0:128], func=AF.Copy))
                pp, ppk = TB()
                sch.op("dve", [ps_tk, "consts"], [ppk],
                       lambda e: e.tensor_tensor(out=pp[:], in0=ps_t[:, 0:128], in1=ident, op=ALU.add))
                bprev, bprevk, btprev, btprevk = b0, b0k, bt, btk
                for lev in range(1, 6):
                    yield
                    ps_b, ps_bk = psring.next()
                    mm(ps_b[:, 0:128], ps_bk, btprev[:], bprev[:], [btprevk, bprevk])
                    ib, ibk = TB()
                    sch.op("dve", [ps_bk, "consts"], [ibk],
                           lambda e: e.tensor_tensor(out=ib[:], in0=ps_b[:, 0:128], in1=ident, op=ALU.add))
                    if lev < 5:
                        bn, bnk = TB()
                        sch.op("act", [ps_bk], [bnk], lambda e: e.activation(out=bn[:], in_=ps_b[:, 0:128], func=AF.Copy))
                    yield
                    ps_p, ps_pk = psring.next()
                    mm(ps_p[:, 0:128], ps_pk, ib[:], pp[:], [ibk, ppk])
                    pn, pnk = TB()
                    evac(pn[:], pnk, ps_p[:, 0:128], ps_pk)
                    pp, ppk = pn, pnk
                    if lev < 5:
                        ps_b2, ps_b2k = psring.next()
                        tr(ps_b2[:, 0:128], ps_b2k, bn[:], [bnk])
                        btn, btnk = TB()
                        evac(btn[:], btnk, ps_b2[:, 0:128], ps_b2k)
                        bprev, bprevk, btprev, btprevk = bn, bnk, btn, btnk
                TT, TTk = pp, ppk
                yield
                ps_u, ps_uk = psring.next()
                mm(ps_u[:, 0:128], ps_uk, TT[:], rhsv[:], [TTk, rhsvk])
                ps_w, ps_wk = psring.next()
                mm(ps_w[:, 0:128], ps_wk, rhsk[:], TT[:], [TTk, rhskk])
                uu, uuk = RB("uu", WIN + 2)
                evac(uu[:], uuk, ps_u[:, 0:128], ps_uk)
                wta, wtak = RB("wta", WIN + 2, dt=BF16)
                wtb, wtbk = RB("wtb", WIN + 2, dt=BF16)
                if ("wtz", wtak) not in rings:
                    rings[("wtz", wtak)] = True
                    sch.op("pool", [], [wtak], lambda e: e.memset(wta[:], 0.0))
                    sch.op("pool", [], [wtbk], lambda e: e.memset(wtb[:], 0.0))
                sch.op("act", [ps_wk], [wtak], lambda e: e.activation(out=wta[:, 0:64], in_=ps_w[:, 0:64], func=AF.Copy))
                sch.op("dve", [ps_wk], [wtbk], lambda e: e.tensor_copy(out=wtb[:, 64:128], in_=ps_w[:, 64:128]))
                free_slots.append(slot)
                yield
                h0, h1 = 2 * p, 2 * p + 1
                assert chain_pos.get((run["idx"], p, d), 0) == tseq, "chain order violated"
                c0_, c1_ = Scur[(h0, d)], Scur[(h1, d)]
                S0, S1 = Sst[(h0, d)][c0_], Sst[(h1, d)][c1_]
                S0h, S1h = S16[(h0, d)][c0_], S16[(h1, d)][c1_]
                S0n, S1n = Sst[(h0, d)][1 - c0_], Sst[(h1, d)][1 - c1_]
                S0hn, S1hn = S16[(h0, d)][1 - c0_], S16[(h1, d)][1 - c1_]
                s0ck, s1ck = ("Sb", h0, d, c0_), ("Sb", h1, d, c1_)
                s0nk, s1nk = ("Sb", h0, d, 1 - c0_), ("Sb", h1, d, 1 - c1_)
                s0hk, s1hk = ("Sh", h0, d, c0_), ("Sh", h1, d, c1_)
                s0hnk, s1hnk = ("Sh", h0, d, 1 - c0_), ("Sh", h1, d, 1 - c1_)
                Scur[(h0, d)], Scur[(h1, d)] = 1 - c0_, 1 - c1_
                ps_ws, ps_wsk = psring.next()
                mm(ps_ws[:, 0:128], ps_wsk, wta[:], S0h[:], [wtak, s0hk], start=True, stop=False)
                mm(ps_ws[:, 0:128], ps_wsk, wtb[:], S1h[:], [wtbk, s1hk], start=False, stop=True)
                vn, vnk = RB("vn", WIN + 2, dt=BF16)
                sch.op("dve", [uuk, ps_wsk], [vnk],
                       lambda e: e.tensor_tensor(out=vn[:], in0=uu[:], in1=ps_ws[:, 0:128], op=ALU.subtract))
                yield
                ps_s0, ps_s0k = psring.next()
                mm(ps_s0[:, 0:128], ps_s0k, kda[:], vn[:], [kdak, vnk])
                ps_s1, ps_s1k = psring.next()
                mm(ps_s1[:, 0:128], ps_s1k, kdb[:], vn[:], [kdbk, vnk])
                if full:
                    ps_o, ps_ok = psring.next()
                    mm(ps_o[:, 0:128], ps_ok, vn[:], aqt[:], [vnk, aqtk], start=True, stop=False)
                    mm(ps_o[:, 0:64], ps_ok, S0h[:], qet[:, 0:64], [s0hk, qetk], start=False, stop=False)
                    mm(ps_o[:, 64:128], ps_ok, S1h[:], qet[:, 64:128], [s1hk, qetk], start=False, stop=True)
                for (ps_s, ps_sk, Sx, Sn, Shn, sck, snk, shnk, hh) in ((ps_s0, ps_s0k, S0, S0n, S0hn, s0ck, s0nk, s0hnk, h0),
                                                                       (ps_s1, ps_s1k, S1, S1n, S1hn, s1ck, s1nk, s1hnk, h1)):
                    ecol = bc["EGT"][:, ch, d * 8 + hh: d * 8 + hh + 1]
                    sch.op("dve", [ps_sk, sck, smk], [snk],
                           lambda e, Sx=Sx, Sn=Sn, ps_s=ps_s, ecol=ecol: e.scalar_tensor_tensor(
                               out=Sn[:], in0=Sx[:], scalar=ecol, in1=ps_s[:, 0:128], op0=ALU.mult, op1=ALU.add))
                    sch.op("pool", [snk], [shnk], lambda e, Sn=Sn, Shn=Shn: e.tensor_copy(out=Shn[:], in_=Sn[:]))
                chain_pos[(run["idx"], p, d)] = tseq + 1
                if full:
                    first_of_block = (ch == 0) if d == 0 else (ch == nb - 1)
                    last_of_block = (ch == nb - 1) if d == 0 else (ch == 0)
                    if first_of_block:
                        ostctx[d] = RB(f"ostd{d}", 2, (128, 2 * 512))
                    ost, ostk = ostctx[d]
                    ostv = ost[:].rearrange("p (h c) -> p h c", h=2)
                    sch.op("act", [ps_ok], [ostk],
                           lambda e: e.activation(out=ostv[:, :, cs], in_=ps_o[:, 0:128].rearrange("p (h c) -> p h c", h=2), func=AF.Copy))
                    if last_of_block:
                        OD = OF if d == 0 else OB
                        oo = run["ooff"] + blk * 512
                        ODv = OD.rearrange("(h d) c -> d h c", h=8)
                        sch.dma("pool", ODv[:, 2 * p:2 * p + 2, oo:oo + nb * 64], ostv[:, :, :nb * 64], [ostk], [])

            order = [r for r in runs if not r["full"]] + [runs[2], runs[0], runs[1]]
            for run in order:
                n, full, off, ri = run["n"], run["full"], run["off"], run["idx"]
                nch = n // 64
                dirs = [0, 1] if full else [0]
                for h in range(8):
                    for d in dirs:
                        cur = Scur[(h, d)]
                        Sb, Sh = Sst[(h, d)][cur], S16[(h, d)][cur]
                        sbk, shk = ("Sb", h, d, cur), ("Sh", h, d, cur)
                        if not full:
                            sch.op("dve", [sbk, "fl"], [sbk],
                                   lambda e, Sb=Sb: e.tensor_scalar(out=Sb[:], in0=Sb[:], scalar1=flag(ri, 2), scalar2=None, op0=ALU.mult))
                        elif run["name"] == "own":
                            src = SF[h] if d == 0 else SB[h]
                            sk = ("SF", h) if d == 0 else ("SB", h)
                            sch.op("pool", [sk], [sbk], lambda e, Sb=Sb, src=src: e.tensor_copy(out=Sb[:], in_=src[:]))
                        else:
                            sch.op("pool", [], [sbk], lambda e, Sb=Sb: e.memset(Sb[:], 0.0))
                        sch.op("pool", [sbk], [shk], lambda e, Sb=Sb, Sh=Sh: e.tensor_copy(out=Sh[:], in_=Sb[:]))
                for p in range(4):
                    ostctx = {}
                    active = []
                    bctx = {}
                    NST = 18
                    STAG = max(3, -(-NST * len(dirs) // WIN))
                    tnext = 0
                    since = STAG
                    while tnext < nch or active:
                        if tnext < nch and since >= STAG and len(active) + len(dirs) <= WIN:
                            since = 0
                            for d in dirs:
                                c = tnext if d == 0 else nch - 1 - tnext
                                blk = c // 8
                                nb = min(8, nch - blk * 8)
                                if bctx.get(d, (None,))[0] != blk:
                                    bctx[d] = (blk, load_block(run, p, d, blk, nb))
                                active.append(inst_gen(run, p, d, c, bctx[d][1], ostctx, tnext))
                            tnext += 1
                        since += 1
                        nxt = []
                        for g in active:
                            try:
                                next(g)
                                nxt.append(g)
                            except StopIteration:
                                pass
                        active = nxt
                if not full:
                    for h in range(8):
                        Sb = Sst[(h, 0)][Scur[(h, 0)]]
                        sck = ("Sb", h, 0, Scur[(h, 0)])
                        sch.op("dve", [sck, "fl", ("SF", h)], [("SF", h)],
                               lambda e, Sb=Sb, h=h: e.scalar_tensor_tensor(out=SF[h][:], in0=Sb[:], scalar=flag(ri, 3), in1=SF[h][:],
                                                                            op0=ALU.mult, op1=ALU.add))
                        sch.op("dve", [sck, "fl", ("SB", h)], [("SB", h)],
                               lambda e, Sb=Sb, h=h: e.scalar_tensor_tensor(out=SB[h][:], in0=Sb[:], scalar=flag(ri, 4), in1=SB[h][:],
                                                                            op0=ALU.mult, op1=ALU.add))
            sch.barrier()

        if "ATT" in stages:
          with contextlib.ExitStack() as st:
            NKV = max(cfg["ctx_n"])
            kt = sb("att_kt", [128, NKV], BF16, stack=st)
            vt = sb("att_vt", [128, NKV], BF16, stack=st)
            onesb = sb("att_ones", [128, 128], BF16, stack=st)
            sch.op("dve", [], ["onesb"], lambda e: e.memset(onesb[:], 1.0))
            qts = Ring([(sb(f"att_q{i}", [128, 512], BF16, stack=st), ("attq", i)) for i in range(2)])
            pts = Ring([(sb(f"att_p{i}", [128, 512], BF16, stack=st), ("attp", i)) for i in range(4)])
            recs = Ring([(sb(f"att_r{i}", [128, 512], stack=st), ("attr", i)) for i in range(2)])
            daccs = Ring([(sb(f"att_d{i}", [128, 512], stack=st), ("attd", i)) for i in range(2)])
            aos = Ring([(sb(f"att_o{i}", [128, 512], BF16, stack=st), ("atto", i)) for i in range(2)])
            po, pok = psb[0], ("ps", 0)
            pd, pdk = psb[1], ("ps", 1)
            ring2 = Ring([(psb[i], ("ps", i)) for i in range(2, 8)])
            scale = 128.0 ** -0.5
            for run in runs[:3]:
                n, c, oo = run["n"], run["ctx"], run["ooff"]
                nkv = cfg["ctx_n"][c]
                nkb = nkv // 128
                vtv = vt[:, :nkv].rearrange("p (kb d) -> p kb d", d=128)
                for g in range(2):
                    for k0 in range(0, nkv, 4096):
                        k1 = min(nkv, k0 + 4096)
                        sch.dma("sp", kt[:, k0:k1], KT[c][g * 128:(g + 1) * 128, k0:k1], [], ["kt"])
                    VVr = VV[c].rearrange("(kb t) f -> t kb f", t=128)
                    for b0 in range(0, nkb, 8):
                        b1 = min(nkb, b0 + 8)
                        sch.dma("sp", vtv[:, b0:b1, :], VVr[:, b0:b1, g * 128:(g + 1) * 128], [], ["vt"])
                    for hq in range(4 * g, 4 * g + 4):
                        for (t0, w) in tiles_of(n):
                            qt, qk_ = qts.next()
                            sch.dma("sp", qt[:, :w], QT[hq * 128:(hq + 1) * 128, oo + t0:oo + t0 + w], [], [qk_])
                            for kb in range(nkb):
                                ps_s, ps_sk = ring2.next()
                                sch.op("pe", ["kt", qk_], [ps_sk],
                                       lambda e: e.matmul(ps_s[:, :w], lhsT=kt[:, kb * 128:(kb + 1) * 128], rhs=qt[:, :w],
                                                          start=True, stop=True))
                                pt, ptk = pts.next()
                                sch.op("act", [ps_sk], [ptk],
                                       lambda e: e.activation(out=pt[:, :w], in_=ps_s[:, :w], func=AF.Exp, scale=scale))
                                sch.op("pe", ["vt", ptk], [pok],
                                       lambda e: e.matmul(po[:, :w], lhsT=vtv[:, kb, :], rhs=pt[:, :w],
                                                          start=(kb == 0), stop=(kb == nkb - 1)))
                                if kb == 0:
                                    dacc, dacck = daccs.next()
                                    sch.op("dve", [ptk], [dacck], lambda e: e.tensor_copy(out=dacc[:, :w], in_=pt[:, :w]))
                                else:
                                    sch.op("dve", [ptk, dacck], [dacck],
                                           lambda e: e.tensor_tensor(out=dacc[:, :w], in0=dacc[:, :w], in1=pt[:, :w], op=ALU.add))
                            sch.op("pe", ["consts", dacck], [pdk],
                                   lambda e: e.matmul(pd[:, :w], lhsT=ones, rhs=dacc[:, :w], start=True, stop=True))
                            rec, reck = recs.next()
                            sch.op("dve", [pdk], [reck], lambda e: e.reciprocal(out=rec[:, :w], in_=pd[:, :w]))
                            ao, aok = aos.next()
                            sch.op("dve", [pok, reck], [aok],
                                   lambda e: e.tensor_tensor(out=ao[:, :w], in0=po[:, :w], in1=rec[:, :w], op=ALU.mult))
                            sch.dma("pool", ATT[hq * 128:(hq + 1) * 128, oo + t0:oo + t0 + w], ao[:, :w], [aok], [])
            sch.barrier()

        def load_w_bf16(dst, W, nk, ncols, key):
            Wv = W.rearrange("(kc p) c -> p kc c", p=128)
            dv = dst[:].rearrange("p (k c) -> p k c", c=ncols)
            for kc in range(nk):
                for c0 in range(0, ncols, 2048):
                    c1 = min(ncols, c0 + 2048)
                    sch.dma("pool", dv[:, kc, c0:c1], Wv[:, kc, c0:c1], [], [key])
            return dv

        own_tiles = []
        for run in runs[:3]:
            for (t0, w) in tiles_of(run["n"]):
                own_tiles.append((run, t0, w))

        if "C1" in stages:
          with contextlib.ExitStack() as st:
            wpa = load_w_bf16(sb("wpa", [128, 8 * D], BF16, stack=st), W_PA, 8, D, "wpa")
            wpb = load_w_bf16(sb("wpb", [128, 8 * D], BF16, stack=st), W_PB, 8, D, "wpb")
            wo = load_w_bf16(sb("wo", [128, 8 * D], BF16, stack=st), W_OUT, 8, D, "wo")
            dnw = sb("dnw", [128, 1], stack=st)
            sch.dma("sp", dnw[:], DNW[:, :], [], ["dnw"])
            f32r = Ring([(sb(f"c1f{i}", [128, 512], stack=st), ("c1f", i)) for i in range(10)])
            attb = sb("c1att", [128, 8 * 512], BF16, stack=st)
            attv = attb[:].rearrange("p (k t) -> p k t", t=512)
            ogb = sb("c1og", [128, 8 * 512], BF16, stack=st)
            ogv = ogb[:].rearrange("p (k t) -> p k t", t=512)
            mb = sb("c1m", [128, 8 * 512], BF16, stack=st)
            mv = mb[:].rearrange("p (k t) -> p k t", t=512)
            XTv2 = XT.rearrange("(kc p) t -> p kc t", p=128)
            for (run, t0, w) in own_tiles:
                s_, oo, off = run["seq"], run["ooff"] + t0, run["off"] + 2 + t0
                ATTv = ATT.rearrange("(kc p) t -> p kc t", p=128)
                sch.dma("sp", attv[:, :, :w], ATTv[:, :, oo:oo + w], [], ["c1att"])
                for h in range(8):
                    a_, ak_ = f32r.next()
                    b_, bk_ = f32r.next()
                    z_, zk_ = f32r.next()
                    sch.dma("sp", a_[:, :w], OF[h * 128:(h + 1) * 128, oo:oo + w], [], [ak_])
                    sch.dma("sp", b_[:, :w], OB[h * 128:(h + 1) * 128, oo:oo + w], [], [bk_])
                    sch.dma("sp", z_[:, :w], ZS[h * 128:(h + 1) * 128, oo:oo + w], [], [zk_])
                    sch.op("pool", [ak_, bk_], [ak_], lambda e: e.tensor_tensor(out=a_[:, :w], in0=a_[:, :w], in1=b_[:, :w], op=ALU.add))
                    sch.op("act", [ak_], [bk_], lambda e: e.activation(out=b_[:, :w], in_=a_[:, :w], func=AF.Square))
                    ps, psk = psring.next()
                    sch.op("pe", [bk_, "consts"], [psk], lambda e: e.matmul(ps[:, :w], lhsT=ones, rhs=b_[:, :w], start=True, stop=True))
                    t_, tk_ = f32r.next()
                    rstd_from_ps(ps, psk, w, 1.0 / 128, b_, bk_, t_, tk_)
                    sch.op("dve", [ak_, bk_, "dnw"], [ak_],
                           lambda e: e.scalar_tensor_tensor(out=a_[:, :w], in0=a_[:, :w], scalar=dnw[:, 0:1], in1=b_[:, :w],
                                                            op0=ALU.mult, op1=ALU.mult))
                    sch.op("dve", [ak_, zk_], [("c1og", h)],
                           lambda e: e.tensor_tensor(out=ogv[:, h, :w], in0=a_[:, :w], in1=z_[:, :w], op=ALU.mult))
                for fc in range(8):
                    ga_, gak = f32r.next()
                    gb_, gbk = f32r.next()
                    sch.dma("sp", ga_[:, :w], GATES[fc * 128:(fc + 1) * 128, oo:oo + w], [], [gak])
                    sch.dma("sp", gb_[:, :w], GATES[D + fc * 128:D + (fc + 1) * 128, oo:oo + w], [], [gbk])
                    psa, psak = psring.next()
                    for kc in range(8):
                        sch.op("pe", ["wpa", "c1att"], [psak],
                               lambda e, kc=kc: e.matmul(psa[:, :w], lhsT=wpa[:, kc, fc * 128:(fc + 1) * 128], rhs=attv[:, kc, :w],
                                                         start=(kc == 0), stop=(kc == 7)))
                    psb_, psbk = psring.next()
                    for kc in range(8):
                        sch.op("pe", ["wpb", ("c1og", kc)], [psbk],
                               lambda e, kc=kc: e.matmul(psb_[:, :w], lhsT=wpb[:, kc, fc * 128:(fc + 1) * 128], rhs=ogv[:, kc, :w],
                                                         start=(kc == 0), stop=(kc == 7)))
                    sch.op("dve", [psak, gak], [gak],
                           lambda e: e.tensor_tensor(out=ga_[:, :w], in0=ga_[:, :w], in1=psa[:, :w], op=ALU.mult))
                    sch.op("dve", [psbk, gbk], [gbk],
                           lambda e: e.tensor_tensor(out=gb_[:, :w], in0=gb_[:, :w], in1=psb_[:, :w], op=ALU.mult))
                    sch.op("pool", [gak, gbk], [("c1m", fc)],
                           lambda e: e.tensor_tensor(out=mv[:, fc, :w], in0=ga_[:, :w], in1=gb_[:, :w], op=ALU.add))
                for fc in range(8):
                    x_, xk_ = f32r.next()
                    sch.dma("sp", x_[:, :w], XTv2[:, fc, off:off + w], [], [xk_])
                    ps, psk = psring.next()
                    for kc in range(8):
                        sch.op("pe", ["wo", ("c1m", kc)], [psk],
                               lambda e, kc=kc: e.matmul(ps[:, :w], lhsT=wo[:, kc, fc * 128:(fc + 1) * 128], rhs=mv[:, kc, :w],
                                                         start=(kc == 0), stop=(kc == 7)))
                    sch.op("dve", [psk, xk_, "modt"], [xk_],
                           lambda e: e.scalar_tensor_tensor(out=x_[:, :w], in0=ps[:, :w], scalar=modv[:, 16 + fc, s_:s_ + 1],
                                                            in1=x_[:, :w], op0=ALU.mult, op1=ALU.add))
                    sch.dma("pool", X1[fc * 128:(fc + 1) * 128, oo:oo + w], x_[:, :w], [xk_], [])
            sch.barrier()

        if "C2" in stages:
          with contextlib.ExitStack() as st:
            wf1 = load_w_bf16(sb("wf1", [128, 8 * 2 * DFF], BF16, stack=st), W_F1, 8, 2 * DFF, "wf1")
            wf2 = load_w_bf16(sb("wf2", [128, 22 * D], BF16, stack=st), W_F2, 22, D, "wf2")
            fnw = sb("fnw", [128, 8], stack=st)
            sch.dma("sp", fnw[:], FNWT[:, :], [], ["fnw"])
            x1b = sb("c2x1", [128, 8 * 512], stack=st)
            x1v = x1b[:].rearrange("p (k t) -> p k t", t=512)
            h2b = sb("c2h2", [128, 8 * 512], BF16, stack=st)
            h2v = h2b[:].rearrange("p (k t) -> p k t", t=512)
            acb = sb("c2ac", [128, 22 * 512], BF16, stack=st)
            acv = acb[:].rearrange("p (k t) -> p k t", t=512)
            f32r = Ring([(sb(f"c2f{i}", [128, 512], stack=st), ("c2f", i)) for i in range(6)])
            rsA = sb("c2rsA", [128, 512], stack=st)
            rsB = sb("c2rsB", [128, 512], stack=st)
            X1v = X1.rearrange("(kc p) t -> p kc t", p=128)
            for (run, t0, w) in own_tiles:
                s_, oo = run["seq"], run["ooff"] + t0
                sch.dma("sp", x1v[:, :, :w], X1v[:, :, oo:oo + w], [], ["c2x1"])
                psS, pskS = psring.next()
                for kc in range(8):
                    q_, qk2 = f32r.next()
                    sch.op("act", ["c2x1"], [qk2], lambda e, kc=kc, q_=q_: e.activation(out=q_[:, :w], in_=x1v[:, kc, :w], func=AF.Square))
                    sch.op("pe", [qk2, "consts"], [pskS],
                           lambda e, kc=kc, q_=q_: e.matmul(psS[:, :w], lhsT=ones, rhs=q_[:, :w], start=(kc == 0), stop=(kc == 7)))
                rs_, rsk = rsA, "c2rsA"
                t_, tk_ = f32r.next()
                rstd_from_ps(psS, pskS, w, 1.0 / D, rs_, rsk, t_, tk_)
                for kc in range(8):
                    t_, tk_ = f32r.next()
                    sch.op("dve", ["c2x1", rsk], [tk_],
                           lambda e, kc=kc, t_=t_: e.tensor_tensor(out=t_[:, :w], in0=x1v[:, kc, :w], in1=rs_[:, :w], op=ALU.mult))
                    sch.op("act", [tk_, "a2", "modt"], [("c2h2", kc)],
                           lambda e, kc=kc, t_=t_: e.activation(out=h2v[:, kc, :w], in_=t_[:, :w], func=AF.Identity,
                                                                scale=a2v[:, kc, s_:s_ + 1], bias=modv[:, 24 + kc, s_:s_ + 1]))
                for j in range(22):
                    psu, psuk = psring.next()
                    for kc in range(8):
                        sch.op("pe", ["wf1", ("c2h2", kc)], [psuk],
                               lambda e, kc=kc: e.matmul(psu[:, :w], lhsT=wf1[:, kc, j * 128:(j + 1) * 128], rhs=h2v[:, kc, :w],
                                                         start=(kc == 0), stop=(kc == 7)))
                    psv_, psvk = psring.next()
                    for kc in range(8):
                        sch.op("pe", ["wf1", ("c2h2", kc)], [psvk],
                               lambda e, kc=kc: e.matmul(psv_[:, :w], lhsT=wf1[:, kc, DFF + j * 128:DFF + (j + 1) * 128],
                                                         rhs=h2v[:, kc, :w], start=(kc == 0), stop=(kc == 7)))
                    su, suk = f32r.next()
                    sch.op("act", [psuk], [suk], lambda e: e.activation(out=su[:, :w], in_=psu[:, :w], func=AF.Silu))
                    sch.op("dve", [suk, psvk], [("c2ac", j)],
                           lambda e: e.tensor_tensor(out=acv[:, j, :w], in0=su[:, :w], in1=psv_[:, :w], op=ALU.mult))
                for fc in range(8):
                    ps, psk = psring.next()
                    for j in range(22):
                        sch.op("pe", ["wf2", ("c2ac", j)], [psk],
                               lambda e, j=j: e.matmul(ps[:, :w], lhsT=wf2[:, j, fc * 128:(fc + 1) * 128], rhs=acv[:, j, :w],
                                                       start=(j == 0), stop=(j == 21)))
                    sch.op("dve", [psk, "c2x1", "modt"], ["c2x1"],
                           lambda e: e.scalar_tensor_tensor(out=x1v[:, fc, :w], in0=ps[:, :w], scalar=modv[:, 40 + fc, s_:s_ + 1],
                                                            in1=x1v[:, fc, :w], op0=ALU.mult, op1=ALU.add))
                psS2, pskS2 = psring.next()
                for fc in range(8):
                    q_, qk2 = f32r.next()
                    sch.op("act", ["c2x1"], [qk2], lambda e, q_=q_: e.activation(out=q_[:, :w], in_=x1v[:, fc, :w], func=AF.Square))
                    sch.op("pe", [qk2, "consts"], [pskS2],
                           lambda e, q_=q_: e.matmul(psS2[:, :w], lhsT=ones, rhs=q_[:, :w], start=(fc == 0), stop=(fc == 7)))
                rs_, rsk = rsB, "c2rsB"
                t_, tk_ = f32r.next()
                rstd_from_ps(psS2, pskS2, w, 1.0 / D, rs_, rsk, t_, tk_)
                for fc in range(8):
                    y_, yk_ = f32r.next()
                    sch.op("dve", ["c2x1", rsk, "fnw"], [yk_],
                           lambda e: e.scalar_tensor_tensor(out=y_[:, :w], in0=x1v[:, fc, :w], scalar=fnw[:, fc:fc + 1], in1=rs_[:, :w],
                                                            op0=ALU.mult, op1=ALU.mult))
                    sch.dma("pool", YT[fc * 128:(fc + 1) * 128, oo:oo + w], y_[:, :w], [yk_], [])
            sch.barrier()

        sch.barrier()
    nc._ninstr = sch.ninstr
    nc._needed = sch.needed
    nc._nincs = sum(len(v) for v in sch.map.values())
    nc._nops = dict(sch.idx)
    nc._declared = declared
    return nc


def rope_tables_T(pos):
    half = 64
    inv = 1.0 / (10000.0 ** (np.arange(0, half, 2, dtype=np.float32) / half))
    row = (pos // 64).astype(np.float32)
    col = (pos % 64).astype(np.float32)
    ar = row[:, None] * inv[None, :]
    ac = col[:, None] * inv[None, :]
    ang = np.concatenate([ar, ar, ac, ac], axis=-1).astype(np.float32)
    return np.cos(ang).T.astype(np.float32), np.sin(ang).T.astype(np.float32)


def make_consts():
    c = np.zeros((128, 10 * 128), np.float32)
    c[:, 0:128] = np.eye(128, dtype=np.float32)
    c[:, 128:256] = 1.0
    R = np.zeros((128, 128), np.float32)
    for a in range(2):
        for cc in range(32):
            R[a * 64 + 32 + cc, a * 64 + cc] = -1.0
            R[a * 64 + cc, a * 64 + 32 + cc] = 1.0
    c[:, 256:384] = R
    i = np.arange(64)
    low_strict = (i[:, None] > i[None, :]).astype(np.float32)
    low_inc = (i[:, None] >= i[None, :]).astype(np.float32)

    def bd(m):
        z = np.zeros((128, 128), np.float32)
        z[:64, :64] = m
        z[64:, 64:] = m
        return z
    c[:, 384:512] = bd(low_strict)
    c[:, 512:640] = bd(low_strict.T)
    c[:, 640:768] = bd(low_inc.T)
    c[:, 768:896] = bd(low_inc)
    c[:, 896:1024] = bd(low_inc.T)
    c[:, 1024:1152] = bd(low_inc)
    c[:, 1152:1280] = bd(np.ones((64, 64), np.float32))
    return c


def host_inputs(cfg, core, x_prompt, x_sample, c_prompt, c_sample, norm1_w, norm2_w, w_ada, b_ada, w_in,
                q_norm_w, k_norm_w, conv_w, a_log, dt_bias, dn_norm_w, w_proj_a, w_proj_b, w_out,
                w_ffn_in, w_ffn_out, final_norm_w, shared):
    SP, SEG = cfg["SP"], cfg["SEG"]
    runs = cfg["runs"]
    NTOT = cfg["NTOT"]
    NR = len(runs)
    XTm = np.zeros((D, NTOT), np.float32)
    pos = np.zeros((NTOT,), np.int64)
    FLm = np.zeros((NR, 8), np.float32)
    cwT = np.ascontiguousarray(conv_w[0].T.reshape(24, 128, 5).transpose(1, 0, 2))
    CWm = np.zeros((128, NR, 24, 5), np.float32)
    xs = x_sample[0]
    S_S = xs.shape[0]
    i = core
    for run in runs:
        n, off, r = run["n"], run["off"], run["idx"]
        if run["name"] in ("pA", "pB"):
            xx = x_prompt[2 * core + (0 if run["name"] == "pA" else 1)]
            XTm[:, off + 2: off + 2 + n] = xx.T
            pos[off + 2: off + 2 + n] = np.arange(n)
            CWm[:, r] = cwT
        elif run["name"] == "own":
            a, b = i * SEG, (i + 1) * SEG
            lo, hi = max(a - 2, 0), min(b + 2, S_S)
            XTm[:, off + 2 - (a - lo): off + 2 + n + (hi - b)] = xs[lo:hi].T
            pos[off + 2: off + 2 + n] = np.arange(a, b)
            FLm[r, 0] = 1.0 if a > 0 else 0.0
            FLm[r, 1] = 1.0 if b < S_S else 0.0
            CWm[:, r] = cwT
        else:
            k = r - 3
            if k < i:
                seg = k
                a, b = seg * SEG, (seg + 1) * SEG
                idx = np.arange(a - 2, b + 2)
                fwd = True
            else:
                seg = 7 - (k - i)
                a, b = seg * SEG, (seg + 1) * SEG
                idx = np.arange(b + 1, a - 3, -1)
                fwd = False
            valid = (idx >= 0) & (idx < S_S)
            cols = off + np.arange(n + 4)
            XTm[:, cols[valid]] = xs[idx[valid]].T
            pos[cols[valid]] = idx[valid]
            FLm[r, 0] = 1.0 if valid[0] else 0.0
            FLm[r, 1] = 1.0 if valid[-1] else 0.0
            FLm[r, 2] = 0.0 if (k == 0 or k == i) else 1.0
            FLm[r, 3] = 1.0 if (k == i - 1) else 0.0
            FLm[r, 4] = 1.0 if (k == 6 and i <= 6) else 0.0
            FLm[r, 5] = 1.0 if fwd else 0.0
            FLm[r, 6] = 0.0 if fwd else 1.0
            CWm[:, r] = cwT if fwd else cwT[:, :, ::-1]
    cosT, sinT = rope_tables_T(pos)
    cseq = np.stack([c_prompt[2 * core], c_prompt[2 * core + 1], c_sample[0]], axis=0)
    CTm = np.ascontiguousarray(cseq.reshape(3, 8, 128).transpose(2, 1, 0)).reshape(128, 24)
    m = dict(shared)
    m.update({
        "XT": XTm, "COS": np.ascontiguousarray(cosT), "SIN": np.ascontiguousarray(sinT),
        "FL": np.ascontiguousarray(np.broadcast_to(FLm.reshape(1, NR * 8), (128, NR * 8))),
        "CW": np.ascontiguousarray(CWm.reshape(128, NR * 24 * 5)),
        "CT": CTm,
    })
    return m


def host_shared(norm1_w, norm2_w, w_ada, b_ada, w_in, q_norm_w, k_norm_w, a_log, dt_bias, dn_norm_w,
                w_proj_a, w_proj_b, w_out, w_ffn_in, w_ffn_out, final_norm_w):
    def vT(v, nch):
        return np.ascontiguousarray(np.asarray(v, np.float32).reshape(nch, 128).T)
    rep = lambda v: np.ascontiguousarray(np.broadcast_to(np.asarray(v, np.float32).reshape(1, -1), (128, v.size)))
    return {
        "w_ada": np.ascontiguousarray(w_ada[0]), "b_adaT": vT(b_ada[0], 48),
        "n1wT": vT(norm1_w[0], 8), "n2wT": vT(norm2_w[0], 8), "fnwT": vT(final_norm_w, 8),
        "w_in": np.ascontiguousarray(w_in[0]),
        "qknw": np.ascontiguousarray(np.stack([q_norm_w[0], k_norm_w[0]], axis=1).astype(np.float32)),
        "dnw": np.ascontiguousarray(dn_norm_w[0].reshape(128, 1).astype(np.float32)),
        "alog_rep": rep(a_log[0]), "dtb_rep": rep(dt_bias[0]),
        "w_proj_a": np.ascontiguousarray(w_proj_a[0]), "w_proj_b": np.ascontiguousarray(w_proj_b[0]),
        "w_out": np.ascontiguousarray(w_out[0]), "w_ffn_in": np.ascontiguousarray(w_ffn_in[0]),
        "w_ffn_out": np.ascontiguousarray(w_ffn_out[0]),
        "consts": make_consts(),
    }


def run_kernel(cfg, inputs, debug=False, stages=("A", "CONV", "DN", "ATT", "C1", "C2"), trace=False):
    inputs = {k: np.asarray(v) for k, v in inputs.items()}
    nc = build(cfg, debug=debug, stages=stages)
    wkeys = ["norm1_w", "norm2_w", "w_ada", "b_ada", "w_in", "q_norm_w", "k_norm_w", "a_log", "dt_bias",
             "dn_norm_w", "w_proj_a", "w_proj_b", "w_out", "w_ffn_in", "w_ffn_out", "final_norm_w"]
    shared = host_shared(**{k: inputs[k] for k in wkeys})
    in_maps = [host_inputs(cfg, c, shared=shared, **inputs) for c in range(NCORES)]
    in_maps = [{k: v for k, v in m.items() if k in nc._declared} for m in in_maps]
    res = run_bass_kernel_spmd(nc, in_maps, core_ids=list(range(NCORES)), trace=trace)
    return res, nc


def kernel(**inputs):
    cfg = make_cfg()
    res, _ = run_kernel(cfg, inputs)
    SP, SEG = cfg["SP"], cfg["SEG"]
    B = inputs["x_prompt"].shape[0]
    y_prompt = np.zeros((B, SP, D), np.float32)
    y_sample = np.zeros((1, 8 * SEG, D), np.float32)
    for c in range(NCORES):
        yt = np.asarray(res.results[c]["YT"])
        y_prompt[2 * c] = yt[:, 0:SP].T
        y_prompt[2 * c + 1] = yt[:, SP:2 * SP].T
        y_sample[0, c * SEG:(c + 1) * SEG] = yt[:, 2 * SP:2 * SP + SEG].T
    return (y_prompt, y_sample)
```
